# Optimizing a Trainium2 kernel written in Bass

```python
import math
import jax
import jax.numpy as jnp
from jax import lax
import numpy as np

D_MODEL = 2048
BATCH = 4
SEQ = 2048
DEPTH = 1
DEC_BATCH = 128
DEC_SEQ = 8
PAST_LEN = 16384
PAGE_SIZE = 128

MIX_WIDTH = D_MODEL
GDN_WIDTH = MIX_WIDTH // 2
GDN_HEADS = 8
GDN_DK = GDN_WIDTH // GDN_HEADS
GDN_DV = GDN_DK
GDN_CONV = 4
GDN_CHUNK = 64
RWKV_WIDTH = MIX_WIDTH - GDN_WIDTH
RWKV_HEAD = 64
RWKV_HEADS = RWKV_WIDTH // RWKV_HEAD
RWKV_LORA_W = 64
RWKV_LORA_A = 64
RWKV_LORA_G = 128
RWKV_PROJ = 3 * RWKV_WIDTH + RWKV_LORA_W + RWKV_LORA_A + RWKV_LORA_G
D_FF = 5632
FFN_CONV = 3
RMS_EPS = 1e-6
L2_EPS = 1e-12
GN_EPS = 64e-5
OFF_GDN_Z = 3 * GDN_WIDTH
OFF_GDN_B = 4 * GDN_WIDTH
OFF_GDN_A = OFF_GDN_B + GDN_HEADS
OFF_RWKV = OFF_GDN_A + GDN_HEADS
IN_WIDTH = OFF_RWKV + RWKV_PROJ

kernel_name = 'hymba_gdn_rwkv7_convffn_step'


def rms_norm(x, g):
    xf = x.astype(jnp.float32)
    y = xf * lax.rsqrt(jnp.mean(xf * xf, axis=-1, keepdims=True) + RMS_EPS)
    return (y * g.astype(jnp.float32)).astype(x.dtype)


def l2_normalize(x):
    return x * lax.rsqrt(jnp.sum(x * x, axis=-1, keepdims=True) + L2_EPS)


def causal_dwconv(x, buf, w):
    width = w.shape[0]
    t = x.shape[1]
    xp = jnp.concatenate([buf.astype(x.dtype), x], axis=1)
    w = w.astype(x.dtype)
    y = xp[:, width - 1:] * w[width - 1]
    for i in range(width - 1):
        y = y + xp[:, i:i + t] * w[i]
    return y, xp[:, t:]


def chunk_gated_delta(q, k, v, beta, g, s0):
    b, t, h, dk = q.shape
    dv = v.shape[-1]
    c = min(GDN_CHUNK, t)
    n = -(-t // c)
    pad = n * c - t

    def blocks(z):
        z = jnp.pad(z, [(0, 0), (0, pad)] + [(0, 0)] * (z.ndim - 2))
        z = z.reshape((b, n, c) + z.shape[2:])
        return jnp.moveaxis(z, 3, 2)

    qc, kc, vc, bc, gc = (blocks(z) for z in (q, k, v, beta, g))
    G = jnp.cumsum(gc, axis=-1)
    idx = jnp.arange(c)
    causal = idx[:, None] >= idx[None, :]
    strict = idx[:, None] > idx[None, :]
    diff = G[..., :, None] - G[..., None, :]
    decay = jnp.where(causal, jnp.exp(jnp.where(causal, diff, 0.0)), 0.0)
    kkt = jnp.einsum('bnhid,bnhjd->bnhij', kc, kc)
    lower = jnp.where(strict, bc[..., :, None] * kkt * decay, 0.0) + jnp.eye(c, dtype=q.dtype)
    gamma = jnp.exp(G)
    rhs = jnp.concatenate([(bc * gamma)[..., None] * kc, bc[..., None] * vc], axis=-1)
    sol = lax.linalg.triangular_solve(lower, rhs, left_side=True, lower=True, unit_diagonal=True)
    w_c, u0_c = sol[..., :dk], sol[..., dk:]
    qk = jnp.einsum('bnhid,bnhjd->bnhij', qc, kc) * decay
    qg = qc * gamma[..., None]
    kt = kc * jnp.exp(G[..., -1:] - G)[..., None]
    gl = jnp.exp(G[..., -1])

    def step(s, inp):
        w_, u0_, qk_, qg_, kt_, gl_ = inp
        u = u0_ - jnp.einsum('bhcd,bhdv->bhcv', w_, s)
        o = jnp.einsum('bhcd,bhdv->bhcv', qg_, s) + jnp.einsum('bhij,bhjv->bhiv', qk_, u)
        s = gl_[..., None, None] * s + jnp.einsum('bhcd,bhcv->bhdv', kt_, u)
        return s, o

    xs = tuple(jnp.moveaxis(z, 1, 0) for z in (w_c, u0_c, qk, qg, kt, gl))
    s, o = lax.scan(step, s0, xs)
    o = jnp.transpose(o, (1, 0, 3, 2, 4)).reshape(b, n * c, h, dv)[:, :t]
    return o, s


def gdn_mixer(p_qkv, p_z, p_beta, p_a, conv_buf, s0, conv_w, a_log, dt_bias, norm_g):
    f32 = jnp.float32
    b, t, _ = p_qkv.shape
    qkv, conv_new = causal_dwconv(p_qkv, conv_buf, conv_w)
    qkv = jax.nn.silu(qkv.astype(f32)).reshape(b, t, 3, GDN_HEADS, GDN_DK)
    q = l2_normalize(qkv[:, :, 0]) * (GDN_DK ** -0.5)
    k = l2_normalize(qkv[:, :, 1])
    v = qkv[:, :, 2]
    beta = jax.nn.sigmoid(p_beta.astype(f32))
    g = -jnp.exp(a_log.astype(f32)) * jax.nn.softplus(p_a.astype(f32) + dt_bias.astype(f32))
    o, s_new = chunk_gated_delta(q, k, v, beta, g, s0.astype(f32))
    o = o * lax.rsqrt(jnp.mean(o * o, axis=-1, keepdims=True) + RMS_EPS) * norm_g.astype(f32)
    o = o * jax.nn.silu(p_z.astype(f32).reshape(b, t, GDN_HEADS, GDN_DV))
    return o.reshape(b, t, GDN_WIDTH), conv_new, s_new


def rwkv7_scan(r, decay, k, v, kk, kka, s0):
    def step(s, inp):
        r_t, w_t, k_t, v_t, kk_t, kka_t = inp
        sk = jnp.einsum('bhvk,bhk->bhv', s, kk_t)
        s = s * w_t[:, :, None, :] - sk[..., None] * kka_t[:, :, None, :] + v_t[..., None] * k_t[:, :, None, :]
        return s, jnp.einsum('bhvk,bhk->bhv', s, r_t)

    xs = tuple(jnp.moveaxis(z, 1, 0) for z in (r, decay, k, v, kk, kka))
    s, y = lax.scan(step, s0, xs)
    return jnp.moveaxis(y, 0, 1), s


def rwkv_mixer(p, shift_buf, s0, mu, w0, w_b, a0, a_b, g_b, k_k, k_a, r_k, gn_w, gn_b):
    f32 = jnp.float32
    b, t, _ = p.shape
    prev = jnp.concatenate([shift_buf[:, None].astype(p.dtype), p[:, :-1]], axis=1)
    xs = (p + (prev - p) * mu.astype(p.dtype)).astype(f32)
    splits = [RWKV_WIDTH, 2 * RWKV_WIDTH, 3 * RWKV_WIDTH, 3 * RWKV_WIDTH + RWKV_LORA_W,
              3 * RWKV_WIDTH + RWKV_LORA_W + RWKV_LORA_A]
    r, k, v, wd, ad, gd = jnp.split(xs, splits, axis=-1)
    w = -jax.nn.softplus(-(w0.astype(f32) + jnp.tanh(wd) @ w_b.astype(f32))) - 0.5
    decay = jnp.exp(-jnp.exp(w))
    a = jax.nn.sigmoid(a0.astype(f32) + ad @ a_b.astype(f32))
    gate = jax.nn.sigmoid(gd) @ g_b.astype(f32)

    def heads(z):
        return z.reshape(b, t, RWKV_HEADS, RWKV_HEAD)

    kk = l2_normalize(heads(k * k_k.astype(f32)))
    k = k * (1.0 + (a - 1.0) * k_a.astype(f32))
    r, k, v, decay, a = heads(r), heads(k), heads(v), heads(decay), heads(a)
    y, s_new = rwkv7_scan(r, decay, k, v, kk, kk * a, s0.astype(f32))
    mean = jnp.mean(y, axis=-1, keepdims=True)
    var = jnp.mean(jnp.square(y - mean), axis=-1, keepdims=True)
    y = (y - mean) * lax.rsqrt(var + GN_EPS) * gn_w.astype(f32).reshape(RWKV_HEADS, RWKV_HEAD) \
        + gn_b.astype(f32).reshape(RWKV_HEADS, RWKV_HEAD)
    y = y + jnp.sum(r * k * r_k.astype(f32), axis=-1, keepdims=True) * v
    return y.reshape(b, t, RWKV_WIDTH) * gate, p[:, -1], s_new


def conv_ffn(x, buf, w_up, conv_w, w_down):
    h, buf_new = causal_dwconv(x @ w_up, buf, conv_w)
    gate, up = jnp.split(h, 2, axis=-1)
    return (jax.nn.silu(gate) * up) @ w_down, buf_new


def layer(x, s_gdn, s_gconv, s_rwkv, s_shift, s_ffn,
          ln1_g, w_in, gdn_conv_w, gdn_a_log, gdn_dt_bias, gdn_norm_g,
          rwkv_mu, rwkv_w0, rwkv_w_b, rwkv_a0, rwkv_a_b, rwkv_g_b, rwkv_k_k, rwkv_k_a, rwkv_r_k,
          rwkv_gn_w, rwkv_gn_b, w_o, ln2_g, ffn_w_up, ffn_conv_w, ffn_w_down):
    proj = rms_norm(x, ln1_g) @ w_in
    o_a, gconv_new, gdn_new = gdn_mixer(
        proj[..., :OFF_GDN_Z], proj[..., OFF_GDN_Z:OFF_GDN_B], proj[..., OFF_GDN_B:OFF_GDN_A],
        proj[..., OFF_GDN_A:OFF_RWKV], s_gconv, s_gdn, gdn_conv_w, gdn_a_log, gdn_dt_bias, gdn_norm_g)
    o_b, shift_new, rwkv_new = rwkv_mixer(
        proj[..., OFF_RWKV:], s_shift, s_rwkv, rwkv_mu, rwkv_w0, rwkv_w_b, rwkv_a0, rwkv_a_b,
        rwkv_g_b, rwkv_k_k, rwkv_k_a, rwkv_r_k, rwkv_gn_w, rwkv_gn_b)
    mixed = jnp.concatenate([o_a, o_b], axis=-1).astype(x.dtype)
    x = x + mixed @ w_o
    f, ffn_new = conv_ffn(rms_norm(x, ln2_g), s_ffn, ffn_w_up, ffn_conv_w, ffn_w_down)
    return x + f, gdn_new, gconv_new, rwkv_new, shift_new, ffn_new


def setup_inputs(seed: int = 0) -> dict:
    key = jax.random.key(seed)
    keys = iter(jax.random.split(key, 40))

    def nrm(shape, scale):
        return scale * jax.random.normal(next(keys), shape, jnp.float32)

    def uni(shape, lo, hi):
        return jax.random.uniform(next(keys), shape, jnp.float32, lo, hi)

    L = DEPTH
    dt = jnp.exp(uni((L, GDN_HEADS), math.log(1e-3), math.log(1e-1)))
    return {
        'x_prompt': nrm((BATCH, SEQ, D_MODEL), 1.0),
        'x_sample': nrm((DEC_BATCH, DEC_SEQ, D_MODEL), 1.0),
        'state_gdn': nrm((L, DEC_BATCH, GDN_HEADS, GDN_DK, GDN_DV), 0.1),
        'state_gdn_conv': nrm((L, DEC_BATCH, GDN_CONV - 1, 3 * GDN_WIDTH), 1.0),
        'state_rwkv': nrm((L, DEC_BATCH, RWKV_HEADS, RWKV_HEAD, RWKV_HEAD), 0.1),
        'state_rwkv_shift': nrm((L, DEC_BATCH, RWKV_PROJ), 1.0),
        'state_ffn_conv': nrm((L, DEC_BATCH, FFN_CONV - 1, 2 * D_FF), 1.0),
        'ln1_g': 1.0 + nrm((L, D_MODEL), 0.02),
        'w_in': nrm((L, D_MODEL, IN_WIDTH), D_MODEL ** -0.5),
        'gdn_conv_w': nrm((L, GDN_CONV, 3 * GDN_WIDTH), GDN_CONV ** -0.5),
        'gdn_a_log': jnp.log(uni((L, GDN_HEADS), 1.0, 16.0)),
        'gdn_dt_bias': dt + jnp.log(-jnp.expm1(-dt)),
        'gdn_norm_g': 1.0 + nrm((L, GDN_DV), 0.02),
        'rwkv_mu': uni((L, RWKV_PROJ), 0.0, 1.0),
        'rwkv_w0': uni((L, RWKV_WIDTH), -6.0, 1.0),
        'rwkv_w_b': nrm((L, RWKV_LORA_W, RWKV_WIDTH), 0.5 * RWKV_LORA_W ** -0.5),
        'rwkv_a0': nrm((L, RWKV_WIDTH), 0.5),
        'rwkv_a_b': nrm((L, RWKV_LORA_A, RWKV_WIDTH), RWKV_LORA_A ** -0.5),
        'rwkv_g_b': nrm((L, RWKV_LORA_G, RWKV_WIDTH), RWKV_LORA_G ** -0.5),
        'rwkv_k_k': 0.85 + nrm((L, RWKV_WIDTH), 0.02),
        'rwkv_k_a': 1.0 + nrm((L, RWKV_WIDTH), 0.02),
        'rwkv_r_k': nrm((L, RWKV_HEADS, RWKV_HEAD), 0.1),
        'rwkv_gn_w': 1.0 + nrm((L, RWKV_WIDTH), 0.02),
        'rwkv_gn_b': nrm((L, RWKV_WIDTH), 0.02),
        'w_o': nrm((L, MIX_WIDTH, D_MODEL), MIX_WIDTH ** -0.5),
        'ln2_g': 1.0 + nrm((L, D_MODEL), 0.02),
        'ffn_w_up': nrm((L, D_MODEL, 2 * D_FF), D_MODEL ** -0.5),
        'ffn_conv_w': nrm((L, FFN_CONV, 2 * D_FF), FFN_CONV ** -0.5),
        'ffn_w_down': nrm((L, D_FF, D_MODEL), D_FF ** -0.5),
        'final_g': 1.0 + nrm((D_MODEL,), 0.02),
    }


def reference(x_prompt, x_sample, state_gdn, state_gdn_conv, state_rwkv, state_rwkv_shift, state_ffn_conv,
              ln1_g, w_in, gdn_conv_w, gdn_a_log, gdn_dt_bias, gdn_norm_g,
              rwkv_mu, rwkv_w0, rwkv_w_b, rwkv_a0, rwkv_a_b, rwkv_g_b, rwkv_k_k, rwkv_k_a, rwkv_r_k,
              rwkv_gn_w, rwkv_gn_b, w_o, ln2_g, ffn_w_up, ffn_conv_w, ffn_w_down, final_g):
    params = (ln1_g, w_in, gdn_conv_w, gdn_a_log, gdn_dt_bias, gdn_norm_g,
              rwkv_mu, rwkv_w0, rwkv_w_b, rwkv_a0, rwkv_a_b, rwkv_g_b, rwkv_k_k, rwkv_k_a, rwkv_r_k,
              rwkv_gn_w, rwkv_gn_b, w_o, ln2_g, ffn_w_up, ffn_conv_w, ffn_w_down)

    def trunk(x, states):
        new = [[] for _ in states]
        for l in range(DEPTH):
            x, *st = layer(x, *[s[l] for s in states], *[p[l] for p in params])
            for acc, s in zip(new, st):
                acc.append(s)
        return rms_norm(x, final_g), [jnp.stack(acc) for acc in new]

    sample_states = (state_gdn, state_gdn_conv, state_rwkv, state_rwkv_shift, state_ffn_conv)
    prompt_states = tuple(jnp.zeros((DEPTH, BATCH) + s.shape[2:], jnp.float32) for s in sample_states)
    y_prompt, (gdn_p, gconv_p, rwkv_p, shift_p, ffn_p) = trunk(x_prompt, prompt_states)
    y_sample, (gdn_s, gconv_s, rwkv_s, shift_s, ffn_s) = trunk(x_sample, sample_states)
    return (y_prompt, y_sample, gdn_p, gconv_p, rwkv_p, shift_p, ffn_p, gdn_s, gconv_s, rwkv_s, shift_s, ffn_s)
```

```python
import numpy as np
import concourse.bass as bass
import concourse.mybir as mybir
from concourse.bass_utils import run_bass_kernel_spmd

F32 = mybir.dt.float32
BF16 = mybir.dt.bfloat16
AF = mybir.ActivationFunctionType
ALU = mybir.AluOpType
AX = mybir.AxisListType

D = 2048
DFF = 5632
INW = 7440
NCORES = 8
SELF_SYNC = True
RELAX_SELF_WAR = True
FP32R = False
PAIR_FFN = True
FAST_RSQRT = True


class Em:
    ENG = ['pe', 'act', 'dve', 'pool', 'sp']

    def __init__(self, nc):
        self.nc = nc
        self.stream = {e: [] for e in self.ENG}
        self.sems = {}
        for e in self.ENG:
            self.sems["cnt_" + e] = nc.alloc_semaphore("cnt_" + e)
        self.ecnt = {e: 0 for e in self.ENG}
        self.epoch = {e: 0 for e in self.ENG}
        self.cur = {e: "cnt_" + e for e in self.ENG}
        self.waited = {}
        self.state = {}
        self.dcount = {}
        self.ninst = 0

    @staticmethod
    def _keys(aps):
        ks = []
        for a in aps:
            if a is None or isinstance(a, (int, float)):
                continue
            ks.append(a if isinstance(a, str) else a.tensor.name)
        return ks

    def _need(self, eng, tok):
        semname, val = tok
        if semname.startswith("cnt_" + eng) and (eng == 'pe' or not SELF_SYNC):
            return
        k = (eng, semname)
        if self.waited.get(k, 0) >= val:
            return
        self.waited[k] = val
        self.stream[eng].append(('wait', semname, val))

    def _deps(self, eng, reads, writes):
        for k in reads:
            st = self.state.get(k)
            if st and st[0]:
                self._need(eng, st[0])
        skip_self = eng in ('act', 'dve') and RELAX_SELF_WAR
        for k in writes:
            st = self.state.get(k)
            if st:
                if st[0] and not (skip_self and st[0][0].startswith("cnt_" + eng)):
                    self._need(eng, st[0])
                for s, v in st[1].items():
                    if skip_self and s.startswith("cnt_" + eng):
                        continue
                    self._need(eng, (s, v))

    def _commit(self, tok, reads, writes):
        for k in reads:
            st = self.state.setdefault(k, [None, {}])
            st[1][tok[0]] = max(st[1].get(tok[0], 0), tok[1])
        for k in writes:
            self.state[k] = [tok, {}]

    def op(self, eng, fn, ins, outs):
        reads, writes = self._keys(ins), self._keys(outs)
        self._deps(eng, reads, writes)
        if self.ecnt[eng] >= 30000:
            self.epoch[eng] += 1
            self.ecnt[eng] = 0
            nm = "cnt_%s_%d" % (eng, self.epoch[eng])
            self.sems[nm] = self.nc.alloc_semaphore(nm)
            self.cur[eng] = nm
        self.ecnt[eng] += 1
        tok = (self.cur[eng], self.ecnt[eng])
        self.stream[eng].append(('inst', fn, self.cur[eng], 1))
        self._commit(tok, reads, writes)
        self.ninst += 1

    def dma(self, eng, out, in_, chain=False, **kw):
        sb = out if out.space == 'SB' else in_
        semname = "dma_" + sb.tensor.name
        if semname not in self.sems:
            self.sems[semname] = self.nc.alloc_semaphore(semname)
            self.dcount[semname] = 0
        reads, writes = self._keys([in_]), self._keys([out])
        self._deps(eng, reads, [] if chain else writes)
        if self.dcount[semname] > 0 and not chain:
            self._need(eng, (semname, 16 * self.dcount[semname]))
        self.dcount[semname] += 1
        tok = (semname, 16 * self.dcount[semname])
        self.stream[eng].append(('inst', lambda e: e.dma_start(out=out, in_=in_, **kw), semname, 16))
        self._commit(tok, reads, writes)
        self.ninst += 1

    def finish(self, eng='sp'):
        for semname, c in self.dcount.items():
            self._need(eng, (semname, 16 * c))
        for e in self.ENG:
            if self.ecnt[e] > 0 and e != eng:
                self._need(eng, (self.cur[e], self.ecnt[e]))

    def replay(self):
        nc = self.nc
        emap = {'pe': 'tensor', 'act': 'scalar', 'dve': 'vector', 'pool': 'gpsimd', 'sp': 'sync'}
        with nc.Block() as block:
            for en in self.ENG:
                items = self.stream[en]

                def body(engine, items=items):
                    for it in items:
                        if it[0] == 'wait':
                            engine.wait_ge(self.sems[it[1]], it[2])
                        else:
                            it[1](engine).then_inc(self.sems[it[2]], it[3])
                getattr(block, emap[en])(body)


PARAM_SPEC = [('ln1g', 16), ('ln2g', 16), ('convw', 96), ('gng', 1), ('alog', 8), ('dtb', 8), ('mu', 26),
              ('w0', 8), ('a0', 8), ('kk', 8), ('ka', 8), ('rk', 8), ('gnw', 8), ('gnb', 8), ('fconvw', 264), ('hflag', 1)]
CONST_SPEC = [('ident', 128), ('ones', 128), ('blk', 128),
              ('caus_p', 128), ('strict_p', 128), ('causT_p', 128), ('strictT_p', 128), ('seq_p', 128),
              ('caus_s', 128), ('strict_s', 128), ('causT_s', 128), ('strictT_s', 128), ('seq_s', 128),
              ('seg_p', 128), ('seg_s', 128), ('rowm_p', 16), ('rowm_s', 16)]


def _offsets(spec):
    o, d = 0, {}
    for n, w in spec:
        d[n] = (o, w)
        o += w
    return d, o


POFF, PW = _offsets(PARAM_SPEC)
COFF, CW = _offsets(CONST_SPEC)


def make_consts():
    c = np.zeros((128, CW), np.float32)

    def put(n, a):
        o, w = COFF[n]
        c[:, o:o + w] = a
    i = np.arange(128)
    put('ident', np.eye(128))
    put('ones', np.ones((128, 128)))
    blk = (i[:, None] // 64) == (i[None, :] // 64)
    put('blk', blk)
    for kind, L in (('p', 128), ('s', 8)):
        same = (i[:, None] // L) == (i[None, :] // L)
        caus = same & (i[:, None] >= i[None, :])
        strict = same & (i[:, None] > i[None, :])
        put('caus_' + kind, caus)
        put('strict_' + kind, strict)
        put('causT_' + kind, caus.T)
        put('strictT_' + kind, strict.T)
        put('seq_' + kind, same)
        put('seg_' + kind, np.broadcast_to((i % L != 0)[None, :], (128, 128)))
        rm = np.zeros((128, 16))
        nseq = 128 // L
        rm[i, (i // L)] = 1.0
        put('rowm_' + kind, rm[:, :16])
    return c


def make_params(inp):
    p = np.zeros((128, PW), np.float32)

    def put(n, a):
        o, w = POFF[n]
        p[:, o:o + w] = np.asarray(a, np.float32).reshape(128, w)

    def fm(v):
        v = np.asarray(v, np.float32).reshape(-1, 128)
        return v.T
    put('ln1g', fm(inp['ln1_g'][0]))
    put('ln2g', fm(inp['ln2_g'][0]))
    cw = np.asarray(inp['gdn_conv_w'][0], np.float32)
    put('convw', cw.reshape(4, 24, 128).transpose(2, 1, 0).reshape(128, 96))
    put('gng', np.asarray(inp['gdn_norm_g'][0]).reshape(128, 1))
    put('alog', np.broadcast_to(np.asarray(inp['gdn_a_log'][0])[None, :], (128, 8)))
    put('dtb', np.broadcast_to(np.asarray(inp['gdn_dt_bias'][0])[None, :], (128, 8)))
    put('mu', fm(inp['rwkv_mu'][0]))
    put('w0', fm(inp['rwkv_w0'][0]))
    put('a0', fm(inp['rwkv_a0'][0]))
    put('kk', fm(inp['rwkv_k_k'][0]))
    put('ka', fm(inp['rwkv_k_a'][0]))
    put('rk', fm(np.asarray(inp['rwkv_r_k'][0]).reshape(-1)))
    put('gnw', fm(inp['rwkv_gn_w'][0]))
    put('gnb', fm(inp['rwkv_gn_b'][0]))
    fw = np.asarray(inp['ffn_conv_w'][0], np.float32)
    put('fconvw', fw.reshape(3, 88, 128).transpose(2, 1, 0).reshape(128, 264))
    return p


class Builder:
    def __init__(self, NPRE, NMAIN):
        self.NPRE, self.NMAIN = NPRE, NMAIN
        import os
        self.STOP = os.environ.get('KSTOP', '')
        self.DBG = bool(os.environ.get('KDBG', ''))
        nc = self.nc = bass.Bass("TRN2", target_bir_lowering=False)
        self.em = Em(nc)
        din = lambda n, s: nc.dram_tensor(n, list(s), F32, kind="ExternalInput").ap()
        dout = lambda n, s: nc.dram_tensor(n, list(s), F32, kind="ExternalOutput").ap()
        self.xpre = din("xpre", (max(NPRE, 1) * 128, D))
        self.xmain = din("xmain", (NMAIN * 128, D))
        self.xs = din("xs", (128, D))
        self.sg = din("sg", (16, 8, 128, 128))
        self.sgc = din("sgc", (48, 3072))
        self.sr = din("sr", (16, 128, 8, 128))
        self.ssh = din("ssh", (16, 3328))
        self.sfc = din("sfc", (32, 11264))
        self.w_in = din("w_in", (D, INW))
        self.w_o = din("w_o", (D, D))
        self.w_up = din("w_up", (D, 2 * DFF))
        self.w_down = din("w_down", (DFF, D))
        self.params_d = din("params", (128, PW))
        self.consts_d = din("consts", (128, CW))
        self.fg_d = din("fg", (128, D))
        self.wab_d = din("wab", (128, 1024))
        self.gb_d = din("gb", (128, 1024))
        if self.DBG:
            self.dbg = nc.dram_tensor("dbg", [2, 128, D], F32, kind="ExternalOutput").ap()
        self.y_main = dout("y_main", (NMAIN * 128, D))
        self.y_s = dout("y_s", (128, D))
        self.o_gdn_p = dout("gdn_p", (8, 128, 128))
        self.o_gconv_p = dout("gconv_p", (3, 3072))
        self.o_rwkv_p = dout("rwkv_p", (128, 8, 128))
        self.o_shift_p = dout("shift_p", (1, 3328))
        self.o_ffn_p = dout("ffn_p", (2, 11264))
        self.o_gdn_s = dout("gdn_s", (16, 8, 128, 128))
        self.o_gconv_s = dout("gconv_s", (48, 3072))
        self.o_rwkv_s = dout("rwkv_s", (16, 128, 8, 128))
        self.o_shift_s = dout("shift_s", (16, 3328))
        self.o_ffn_s = dout("ffn_s", (32, 11264))

        sb = lambda n, s, dt=F32: nc.alloc_sbuf_tensor(n, list(s), dt)
        self.params = sb("params_sb", (128, PW))
        self.consts = sb("consts_sb", (128, CW))
        self.fg = sb("fg_sb", (128, D))
        self.wab = sb("wab_sb", (128, 1024))
        self.gb = sb("gb_sb", (128, 1024))
        self.xts = [sb("xt", (128, D)), sb("xtB", (128, D))]
        self.xt = self.xts[0]
        self.xnT2 = sb("xnT2", (128, 16, 256), BF16)
        self.xn = sb("xn", (128, D))
        self.xnT = sb("xnT", (128, 16, 128), BF16)
        self.NSLOT = 4
        self.ws = [sb("ws%d" % i, (128, 2048), BF16) for i in range(self.NSLOT)]
        dsc = lambda n, s: nc.dram_tensor(n, list(s), BF16, kind="Internal").ap()
        self.c_in = dsc("wsc_in", (D, INW))
        self.c_o = dsc("wsc_o", (D, D))
        self.c_up = dsc("wsc_up", (D, 2 * DFF))
        self.c_down = dsc("wsc_down", (DFF, D))
        self.wba = sb("wba", (128, 16, 16), BF16)
        self.xp = sb("xp", (128, 24 * 176))
        self.pp = sb("pp", (128, 26 * 144))
        self.qkv = sb("qkv", (128, 24, 128))
        self.scrA = sb("scrA", (128, 3328))
        self.zT = sb("zT", (128, 8, 128))
        self.mixedT = sb("mixedT", (128, 16, 128), BF16)
        self.xhist = sb("xhist", (128, 24, 3))
        self.phist = sb("phist", (128, 26, 1))
        self.fhist = sb("fhist", (128, 88, 2))
        self.Sg = sb("Sg", (128, 8, 128))
        self.Hb = sb("Hb", (128, 8, 128))
        self.NG = 13
        self.g = [sb("g%d" % i, (128, 1024)) for i in range(self.NG)]
        self.zeros = sb("zeros", (128, 512))
        self.cols = sb("cols", (128, 256))
        self.st48 = self.scrA
        self.ps = [nc.alloc_psum_tensor("ps%d" % i, [128, 2, 512], F32) for i in range(4)]
        self.wslot = 0
        self.pool_eng = 'pool'
        self._unused_hist = True

    def P(self, n):
        o, w = POFF[n]
        return self.params[:, o:o + w]

    def C(self, n):
        o, w = COFF[n]
        return self.consts[:, o:o + w]

    def tt(self, out, a, b, op, eng='dve'):
        self.em.op(eng, lambda e: e.tensor_tensor(out=out, in0=a, in1=b, op=op), [a, b], [out])

    def ts(self, out, a, s1, op0, s2=None, op1=None, eng='dve', accum=None):
        kw = dict(out=out, in0=a, scalar1=s1, scalar2=s2, op0=op0)
        if op1 is not None:
            kw['op1'] = op1
        if accum is not None:
            kw['accum_out'] = accum
        self.em.op(eng, lambda e: e.tensor_scalar(**kw), [a, s1, s2], [out, accum])

    def stt(self, out, a, s, b, op0, op1):
        self.em.op('dve', lambda e: e.scalar_tensor_tensor(out=out, in0=a, scalar=s, in1=b, op0=op0, op1=op1),
                   [a, s, b], [out])

    def act(self, out, in_, func, scale=1.0, bias=0.0, accum=None):
        kw = dict(out=out, in_=in_, func=func, scale=scale, bias=bias)
        if accum is not None:
            kw['accum_out'] = accum
        self.em.op('act', lambda e: e.activation(**kw), [in_, scale, bias], [out, accum])

    def cp(self, out, in_, eng='dve'):
        if eng == 'act':
            self.act(out, in_, AF.Copy)
        else:
            self.em.op(eng, lambda e: e.tensor_copy(out=out, in_=in_), [in_], [out])

    def rcp(self, out, in_):
        n = 1
        for d in out.shape[1:]:
            n *= d
        self.em.op('dve', lambda e: e.reciprocal(out=out, in_=in_), [in_], [out])

    def memset(self, ap, val, eng='dve'):
        self.em.op(eng, lambda e: e.memset(ap, val), [], [ap])

    def mm(self, out, lhsT, rhs, start=True, stop=True, skip=False):
        if FP32R and lhsT.dtype == F32:
            lhsT, rhs = lhsT.bitcast(mybir.dt.float32r), rhs.bitcast(mybir.dt.float32r)
        self.em.op('pe', lambda e: e.matmul(out=out, lhsT=lhsT, rhs=rhs, start=start, stop=stop,
                                            skip_group_check=skip), [lhsT, rhs], [out])

    def tr(self, out, in_):
        p = in_.shape[0]
        idn = self.C('ident')[0:p, 0:p]
        self.em.op('pe', lambda e: e.transpose(out=out, in_=in_, identity=idn), [in_, idn], [out])

    def scan(self, out, d0, d1):
        self.em.op('dve', lambda e: e.tensor_tensor_scan(out=out, data0=d0, data1=d1, initial=0.0,
                                                         op0=ALU.mult, op1=ALU.add), [d0, d1], [out])

    def reduce(self, out, in_, op=ALU.add):
        self.em.op('dve', lambda e: e.tensor_reduce(out=out, in_=in_, axis=AX.X, op=op), [in_], [out])

    def dma(self, out, in_, eng='sp', **kw):
        self.em.dma(eng, out, in_, **kw)

    def rsqrt(self, out, in_, scale, eps):
        n = 1
        for d in out.shape[1:]:
            n *= d
        if n >= 256 and FAST_RSQRT:
            self.act(out, in_, AF.Ln, scale=scale, bias=eps)
            self.act(out, out, AF.Exp, scale=-0.5)
        else:
            self.act(out, in_, AF.Sqrt, scale=scale, bias=eps)
            self.rcp(out, out)

    def bc(self, ap, shape):
        return ap.broadcast_to(list(shape))

    def convert_weights(self):
        for (src, dst) in ((self.w_in, self.c_in), (self.w_o, self.c_o), (self.w_up, self.c_up),
                           (self.w_down, self.c_down)):
            R = src.shape[0]
            for r in range(0, R, 256):
                self.em.dma('pool', dst[r:r + 256, :], src[r:r + 256, :], max_dma_last_dim=4096)

    def wslot_load(self, pieces):
        t = self.ws[self.wslot % self.NSLOT]
        self.wslot += 1
        for i, (o, ap) in enumerate(pieces):
            self.dma(t[:, o:o + ap.shape[1]], ap, eng='sp', chain=(i > 0))
        return t

    def acc_region(self, pair, j):
        return self.ps[pair * 2 + j // 8][:, (j % 8) // 4, (j % 4) * 128:(j % 4) * 128 + 128]

    def stream_up(self, wsc, ranges, pair):
        zz = self.zeros
        used = sorted(set((so // 128 + i) // 4 for (c0, n, so) in ranges for i in range(n // 128)))
        for bk in used:
            self.mm(self.ps[pair * 2 + bk // 2][:, bk % 2, :], zz[:, 0:128], zz[:, 0:512], start=True, stop=False,
                    skip=True)
        for k in range(16):
            t = self.wslot_load([(so, wsc[k * 128:(k + 1) * 128, c0:c0 + n]) for (c0, n, so) in ranges])
            for (c0, n, so) in ranges:
                for i in range(n // 128):
                    j = so // 128 + i
                    self.mm(self.acc_region(pair, j), t[:, so + i * 128:so + (i + 1) * 128], self.xnT[:, k, :],
                            start=False, stop=(k == 15), skip=True)

    def stream_down(self, wsc, nk, lhs, pair):
        for k in range(nk):
            t = self.wslot_load([(0, wsc[k * 128:(k + 1) * 128, :])])
            for n in range(4):
                self.mm(self.ps[pair * 2 + n // 2][:, n % 2, :], lhs[:, k, :], t[:, n * 512:(n + 1) * 512],
                        start=(k == 0), stop=(k == nk - 1))

    def norm_T(self, gname, dst=None):
        dst = self.xnT if dst is None else dst
        ps = self.ps
        c = self.cols
        self.act(self.xn[:], self.xt[:], AF.Square, accum=c[:, 0:1])
        self.rsqrt(c[:, 1:2], c[:, 0:1], 1.0 / D, 1e-6)
        self.ts(self.xn[:], self.xt[:], c[:, 1:2], ALU.mult)
        for half in range(2):
            for q in range(4):
                for j in range(2):
                    k = half * 8 + q * 2 + j
                    self.tr(ps[q][:, j, 0:128], self.xn[:, k * 128:(k + 1) * 128])
            for q in range(4):
                k0 = half * 8 + q * 2
                self.tt(dst[:, k0:k0 + 2, :], ps[q][:, :, 0:128],
                        self.bc(self.P(gname)[:, k0:k0 + 2].unsqueeze(2), (128, 2, 128)), ALU.mult)

    def solve(self, A0, B0, A1, B1, X, n, nlev, Xb=None, xparts=None):
        ps = self.ps
        A, B, An, Bn = A0, B0, A1, B1
        xparts = xparts or []
        if Xb is None:
            Xb = X
        for (xa, xb) in xparts:
            self.cp(xb, xa, eng='act')
        shadow = len(xparts) > 0
        for j in range(nlev):
            if n == 256:
                for h in range(8):
                    self.mm(ps[h // 4][:, (h % 4) // 2, (h % 2) * 256:(h % 2) * 256 + 256], B[:, h, :], Xb[:, h, :])
                for q in range(2):
                    pq = ps[q][:].rearrange("p a b -> p (a b)")
                    xq = X[:, 4 * q:4 * q + 4, :].rearrange("p a b -> p (a b)")
                    if j == 0:
                        self.stt(xq, pq, -1.0, xq, ALU.mult, ALU.add)
                    else:
                        self.tt(xq, pq, xq, ALU.add)
            else:
                bank = j % 2
                for h in range(8):
                    self.mm(ps[0][:, bank, h * 64:(h + 1) * 64], B[:, h, :], Xb[:, h, :])
                pf = ps[0][:, bank, :]
                xf = X.rearrange("p a b -> p (a b)")
                if shadow and j < nlev - 1:
                    xbf = Xb.rearrange("p a b -> p (a b)")
                    if j == 0:
                        self.stt(xbf, pf, -1.0, xf, ALU.mult, ALU.add)
                    else:
                        self.tt(xbf, pf, xf, ALU.add)
                if j == 0:
                    self.stt(xf, pf, -1.0, xf, ALU.mult, ALU.add)
                else:
                    self.tt(xf, pf, xf, ALU.add)
            if j < nlev - 1:
                for h in range(8):
                    self.mm(ps[2][:, h // 4, (h % 4) * 128:(h % 4) * 128 + 128], B[:, h, :], A[:, h, :])
                for h in range(8):
                    self.mm(ps[3][:, h // 4, (h % 4) * 128:(h % 4) * 128 + 128], A[:, h, :], B[:, h, :])
                self.cp(An, ps[2][:].rearrange("p a (b c) -> p (a b) c", b=4), eng='pool' if False else 'act')
                self.cp(Bn, ps[3][:].rearrange("p a (b c) -> p (a b) c", b=4), eng='dve')
                A, B, An, Bn = An, Bn, A, B

    def rowbc(self, dst_ps, src, n, scratch):
        self.tt(scratch, self.bc(src.unsqueeze(2), (128, n, 128)),
                self.bc(self.C('ident').unsqueeze(1), (128, n, 128)), ALU.mult)
        flat = scratch.rearrange("p a b -> p (a b)")
        for q in range((n * 128 + 511) // 512):
            w = min(512, n * 128 - q * 512)
            self.mm(dst_ps[:, q, 0:w], self.C('ones'), flat[:, q * 512:q * 512 + w])

    def tile(self, kind, x_src, y_dst, full, first, last, sample_state, pair_slot=None):
        nseq, L = (1, 128) if kind == 'p' else (16, 8)
        nlev = 7 if kind == 'p' else 3
        ps, g, c = self.ps, self.g, self.cols
        Lh3, Lh1, Lh2 = L + 3, L + 1, L + 2
        xpv = self.xp[:, 0:24 * nseq * Lh3].rearrange("p (c s t) -> p c s t", c=24, s=nseq)
        ppv = self.pp[:, 0:26 * nseq * Lh1].rearrange("p (c s t) -> p c s t", c=26, s=nseq)
        M = lambda n: self.C(n + '_' + kind)

        self.dma(self.xt[:], x_src)
        self.norm_T('ln1g')

        if self.STOP == 'A':
            return
        if kind == 'p':
            if first:
                self.memset(self.xhist[:], 0.0)
                self.memset(self.phist[:], 0.0)
                self.memset(self.Sg[:], 0.0)
                self.memset(self.Hb[:], 0.0)
                self.memset(self.fhist[:], 0.0)
            self.cp(xpv[:, :, 0, 0:3], self.xhist[:])
            self.cp(ppv[:, :, 0, 0:1], self.phist[:])
        else:
            self.dma(self.st48[0:48, 0:3072], self.sgc)
            for cg in range(3):
                for j in range(8):
                    cc = cg * 8 + j
                    self.tr(ps[j // 4][:, (j % 4) // 2, (j % 2) * 48:(j % 2) * 48 + 48],
                            self.st48[0:48, cc * 128:(cc + 1) * 128])
                for j in range(8):
                    cc = cg * 8 + j
                    self.cp(xpv[:, cc, :, 0:3],
                            ps[j // 4][:, (j % 4) // 2, (j % 2) * 48:(j % 2) * 48 + 48].rearrange("p (s t) -> p s t", s=16))
            self.dma(self.st48[0:16, 0:3328], self.ssh)
            for cg in range(4):
                for j in range(8):
                    cc = cg * 8 + j
                    if cc < 26:
                        self.tr(ps[j // 4][:, (j % 4) // 2, (j % 2) * 16:(j % 2) * 16 + 16],
                                self.st48[0:16, cc * 128:(cc + 1) * 128])
                for j in range(8):
                    cc = cg * 8 + j
                    if cc < 26:
                        self.cp(ppv[:, cc, :, 0:1],
                                ps[j // 4][:, (j % 4) // 2, (j % 2) * 16:(j % 2) * 16 + 16].unsqueeze(2))

        if self.STOP == 'A2':
            return
        p8v = lambda t: t[:].rearrange("p a (b t) -> p (a b) t", b=4)
        s4 = lambda a_: a_.rearrange("p c (s t) -> p c s t", s=nseq)
        RW = 4112
        if full:
            self.stream_up(self.c_in, [(0, 2048, 0)], 0)
            for q in range(2):
                self.cp(xpv[:, 8 * q:8 * q + 8, :, 3:3 + L], s4(p8v(ps[q])), eng='act' if q == 0 else 'dve')
            self.stream_up(self.c_in, [(2048, 2048, 0)], 1)
            self.cp(xpv[:, 16:24, :, 3:3 + L], s4(p8v(ps[2])), eng='act')
            self.cp(self.zT[:], p8v(ps[3]), eng='dve')
            self.stream_up(self.c_in, [(RW, 2048, 0)], 0)
            for q in range(2):
                self.cp(ppv[:, 8 * q:8 * q + 8, :, 1:1 + L], s4(p8v(ps[q])), eng='act' if q == 0 else 'dve')
            self.stream_up(self.c_in, [(RW + 2048, 1280, 0)], 1)
            self.cp(ppv[:, 16:24, :, 1:1 + L], s4(p8v(ps[2])), eng='act')
            self.cp(ppv[:, 24:26, :, 1:1 + L], s4(p8v(ps[3])[:, 0:2, :]), eng='dve')
        else:
            self.stream_up(self.c_in, [(1024, 2048, 0)], 0)
            for q in range(2):
                self.cp(xpv[:, 8 + 8 * q:16 + 8 * q, :, 3:3 + L], s4(p8v(ps[q])), eng='act' if q == 0 else 'dve')
            self.stream_up(self.c_in, [(RW + 1024, 2048, 0)], 1)
            for q in range(2):
                self.cp(ppv[:, 8 + 8 * q:16 + 8 * q, :, 1:1 + L], s4(p8v(ps[2 + q])), eng='act' if q == 0 else 'dve')
            self.stream_up(self.c_in, [(RW + 3072, 128, 0)], 0)
            self.cp(ppv[:, 24:25, :, 1:1 + L], s4(p8v(ps[0])[:, 0:1, :]), eng='act')
        self.dma(self.wba[:], self.c_in[:, 4096:4112].rearrange("(k p) c -> p k c", p=128), eng='sp')
        for k in range(16):
            self.mm(ps[0][:, 0, 0:16], self.xnT[:, k, :], self.wba[:, k, :], start=(k == 0), stop=(k == 15))
        ba = c[:, 8:24]
        self.cp(ba, ps[0][:, 0, 0:16])

        if self.STOP == 'B':
            return
        cw = self.P('convw').rearrange("p (c i) -> p c i", i=4)
        qv = self.qkv[:].rearrange("p c (s t) -> p c s t", s=nseq)
        tmp = self.scrA[:, 0:3072].rearrange("p (c s t) -> p c s t", c=24, s=nseq)
        wsh = (128, 24, nseq, L)
        sq = self.scrA[:, 0:2048].rearrange("p (c t) -> p c t", c=16)
        sqf = self.scrA[:, 0:2048]
        rn = g[0]
        qn, kn, va = self.qkv[:, 0:8, :], self.qkv[:, 8:16, :], self.qkv[:, 16:24, :]
        bet, gcol, Gc, gam, bgc, Gl, ekl, tmpc = (c[:, 32:40], c[:, 40:48], c[:, 48:56], c[:, 56:64], c[:, 64:72],
                                                  c[:, 72:80], c[:, 80:88], c[:, 88:96])
        scr8 = g[1][:].rearrange("p (h t) -> p h t", h=8)
        tdf = g[2][:].rearrange("p (h t) -> p h t", h=8)
        Dm = g[3][:].rearrange("p (h t) -> p h t", h=8)
        DTm = g[4][:].rearrange("p (h t) -> p h t", h=8)

        def chain_conv():
            if kind == 'p':
                self.cp(self.xhist[:], xpv[:, :, 0, L:L + 3])
                yield
            self.tt(qv, xpv[:, :, :, 3:3 + L], self.bc(cw[:, :, 3:4].unsqueeze(3), wsh), ALU.mult)
            yield
            for i in range(3):
                self.tt(tmp, xpv[:, :, :, i:i + L], self.bc(cw[:, :, i:i + 1].unsqueeze(3), wsh), ALU.mult,
                        eng=self.pool_eng)
                yield
                self.tt(qv, qv, tmp, ALU.add)
                yield
            self.act(self.qkv[:], self.qkv[:], AF.Silu)
            yield
            self.tt(sq, self.qkv[:, 0:16, :], self.qkv[:, 0:16, :], ALU.mult)
            yield
            for q in range(4):
                self.mm(ps[q // 2 + 2][:, q % 2, :], self.C('ones'), sqf[:, q * 512:(q + 1) * 512])
            yield
            for part in range(2):
                rv = rn[:].rearrange("p (c t) -> p c t", c=8)
                self.rsqrt(rv, ps[2 + part][:].rearrange("p a (b t) -> p (a b) t", b=4), 1.0, 1e-12)
                yield
                if part == 0:
                    self.stt(self.qkv[:, 0:8, :], self.qkv[:, 0:8, :], 128.0 ** -0.5, rv, ALU.mult, ALU.mult)
                else:
                    self.tt(self.qkv[:, 8:16, :], self.qkv[:, 8:16, :], rv, ALU.mult)
                yield

        def chain_cols():
            self.act(bet, ba[:, 0:8], AF.Sigmoid)
            self.tt(tmpc, ba[:, 8:16], self.P('dtb'), ALU.add)
            yield
            self.act(tmpc, tmpc, AF.Exp)
            self.act(tmpc, tmpc, AF.Ln, bias=1.0)
            self.act(gcol, self.P('alog'), AF.Exp)
            yield
            self.stt(gcol, gcol, -1.0, tmpc, ALU.mult, ALU.mult)
            self.mm(ps[0][:, 0, 0:8], M('causT'), gcol)
            self.mm(ps[0][:, 0, 8:16], M('seq'), gcol)
            yield
            self.cp(Gc, ps[0][:, 0, 0:8])
            self.cp(Gl, ps[0][:, 0, 8:16])
            yield
            self.act(gam, Gc, AF.Exp)
            self.tt(bgc, bet, gam, ALU.mult)
            self.tt(ekl, Gl, Gc, ALU.subtract)
            self.act(ekl, ekl, AF.Exp)
            yield
            self.rowbc(ps[0], Gc, 8, scr8)
            yield
            self.tt(tdf, ps[0][:].rearrange("p a (b t) -> p (a b) t", b=4), self.bc(Gc.unsqueeze(2), (128, 8, 128)),
                    ALU.subtract)
            yield
            self.ts(Dm, tdf, 0.0, ALU.max, -1.0, ALU.mult)
            self.ts(DTm, tdf, 0.0, ALU.min)
            yield
            self.act(Dm, Dm, AF.Exp)
            self.act(DTm, DTm, AF.Exp)
            yield
            self.tt(Dm, Dm, self.bc(M('caus').unsqueeze(1), (128, 8, 128)), ALU.mult)
            self.tt(DTm, DTm, self.bc(M('causT').unsqueeze(1), (128, 8, 128)), ALU.mult)
            yield

        gens = [chain_cols(), chain_conv()]
        while gens:
            for gen_ in list(gens):
                try:
                    next(gen_)
                except StopIteration:
                    gens.remove(gen_)
        if self.STOP == 'C2':
            return
        for h in range(8):
            self.mm(ps[1][:, h // 4, (h % 4) * 128:(h % 4) * 128 + 128], kn[:, h, :], kn[:, h, :])
        A0 = g[5][:].rearrange("p (h t) -> p h t", h=8)
        B0 = g[6][:].rearrange("p (h t) -> p h t", h=8)
        A1 = g[7][:].rearrange("p (h t) -> p h t", h=8)
        B1 = g[8][:].rearrange("p (h t) -> p h t", h=8)
        self.tt(A0, ps[1][:].rearrange("p a (b t) -> p (a b) t", b=4), Dm, ALU.mult)
        self.tt(A0, A0, self.bc(bet.unsqueeze(2), (128, 8, 128)), ALU.mult)
        self.tt(A0, A0, self.bc(M('strict').unsqueeze(1), (128, 8, 128)), ALU.mult)
        for h in range(8):
            self.tr(ps[2][:, h // 4, (h % 4) * 128:(h % 4) * 128 + 128], A0[:, h, :])
        self.cp(B0, ps[2][:].rearrange("p a (b t) -> p (a b) t", b=4))
        X = g[9:11]
        Xv = [X[0][:].rearrange("p (h t) -> p h t", h=4), X[1][:].rearrange("p (h t) -> p h t", h=4)]
        ktm = g[11][:].rearrange("p (h t) -> p h t", h=8)
        for h in range(8):
            self.tr(ps[1][:, h // 4, (h % 4) * 128:(h % 4) * 128 + 128], kn[:, h, :])
        for h in range(8):
            self.tr(ps[3][:, h // 4, (h % 4) * 128:(h % 4) * 128 + 128], va[:, h, :])
        for q in range(2):
            self.tt(Xv[q][:, :, 0:128], ps[1][:, q, :].rearrange("p (h t) -> p h t", h=4),
                    self.bc(bgc[:, 4 * q:4 * q + 4].unsqueeze(2), (128, 4, 128)), ALU.mult)
            self.tt(Xv[q][:, :, 128:256], ps[3][:, q, :].rearrange("p (h t) -> p h t", h=4),
                    self.bc(bet[:, 4 * q:4 * q + 4].unsqueeze(2), (128, 4, 128)), ALU.mult)
        self.tt(ktm, ps[1][:].rearrange("p a (b t) -> p (a b) t", b=4), self.bc(ekl.unsqueeze(2), (128, 8, 128)),
                ALU.mult)

        class XW:
            def __getitem__(s2, idx):
                _, hs, cs = idx
                if isinstance(hs, int):
                    return Xv[hs // 4][:, hs % 4, cs]
                q = hs.start // 4
                return Xv[q][:, :, cs]
        XX = XW()
        self.solve(A0, B0, A1, B1, XX, 256, nlev)
        if self.STOP == 'D':
            return
        so_ = (full == 0)
        qkT = g[5][:].rearrange("p (h t) -> p h t", h=8)
        if not so_:
            for h in range(8):
                self.mm(ps[2][:, h // 4, (h % 4) * 128:(h % 4) * 128 + 128], kn[:, h, :], qn[:, h, :])
            self.tt(qkT, ps[2][:].rearrange("p a (b t) -> p (a b) t", b=4), DTm, ALU.mult)
        WT = g[6][:].rearrange("p (h t) -> p h t", h=8)
        for h in range(8):
            self.tr(ps[3][:, h // 4, (h % 4) * 128:(h % 4) * 128 + 128], XX[:, h, 0:128])
        self.cp(WT, ps[3][:].rearrange("p a (b t) -> p (a b) t", b=4), eng='act')
        qgT = g[7][:].rearrange("p (h t) -> p h t", h=8)
        if not so_:
            self.rowbc(ps[0], gam, 8, scr8)
            self.tt(qgT, ps[0][:].rearrange("p a (b t) -> p (a b) t", b=4), qn, ALU.mult)
        u = g[8][:].rearrange("p (h t) -> p h t", h=8)
        pws = ps[0][:].rearrange("p a (b t) -> p (a b) t", b=4)
        po = ps[1][:].rearrange("p a (b t) -> p (a b) t", b=4)
        zz = self.zeros
        if not so_:
            for a in range(2):
                self.mm(ps[1][:, a, :], zz[:, 0:128], zz[:, 0:512], start=True, stop=False, skip=True)
        if kind == 'p':
            for h in range(8):
                self.mm(pws[:, h, :], WT[:, h, :], self.Sg[:, h, :])
            if not so_:
                for h in range(8):
                    self.mm(po[:, h, :], qgT[:, h, :], self.Sg[:, h, :], start=False, stop=False, skip=True)
        else:
            WTm = g[1][:].rearrange("p (h t) -> p h t", h=8)
            qgTm = g[2][:].rearrange("p (h t) -> p h t", h=8)
            self.memset(WTm, 0.0)
            self.memset(qgTm, 0.0)
            for a in range(2):
                self.mm(ps[0][:, a, :], zz[:, 0:128], zz[:, 0:512], start=True, stop=False, skip=True)
            for s in range(16):
                Sb = (self.Sg, self.Hb)[s % 2]
                self.dma(Sb[:], self.sg[s].rearrange("h k v -> k h v"))
                if s > 0:
                    self.memset(WTm[:, :, 8 * s - 8:8 * s], 0.0)
                    self.memset(qgTm[:, :, 8 * s - 8:8 * s], 0.0)
                self.cp(WTm[:, :, 8 * s:8 * s + 8], WT[:, :, 8 * s:8 * s + 8])
                self.cp(qgTm[:, :, 8 * s:8 * s + 8], qgT[:, :, 8 * s:8 * s + 8])
                for h in range(8):
                    self.mm(pws[:, h, :], WTm[:, h, :], Sb[:, h, :], start=False, stop=False, skip=True)
                for h in range(8):
                    self.mm(po[:, h, :], qgTm[:, h, :], Sb[:, h, :], start=False, stop=False, skip=True)
        for q in range(2):
            self.tt(u[:, 4 * q:4 * q + 4, :], Xv[q][:, :, 128:256], pws[:, 4 * q:4 * q + 4, :], ALU.subtract)
        if not so_:
            for h in range(8):
                self.mm(po[:, h, :], qkT[:, h, :], u[:, h, :], start=False, stop=True, skip=True)
            o = g[5][:].rearrange("p (h t) -> p h t", h=8)
            self.cp(o, po, eng='act')
            osq = g[6][:].rearrange("p (h t) -> p h t", h=8)
            self.tt(osq, o, o, ALU.mult)
            self.reduce(tmpc, osq)
            self.rsqrt(tmpc, tmpc, 1.0 / 128, 1e-6)
            self.tt(o, o, self.bc(tmpc.unsqueeze(2), (128, 8, 128)), ALU.mult)
            for h in range(8):
                self.tr(ps[2][:, h // 4, (h % 4) * 128:(h % 4) * 128 + 128], o[:, h, :])
            self.act(self.zT[:], self.zT[:], AF.Silu)
            oT = g[6][:].rearrange("p (h t) -> p h t", h=8)
            self.ts(oT, ps[2][:].rearrange("p a (b t) -> p (a b) t", b=4), self.P('gng')[:, 0:1], ALU.mult)
            self.tt(self.mixedT[:, 0:8, :], oT, self.zT[:], ALU.mult)
        pS = ps[3][:].rearrange("p a (b t) -> p (a b) t", b=4)
        if kind == 'p':
            for h in range(8):
                self.mm(pS[:, h, :], ktm[:, h, :], u[:, h, :])
            self.act(tmpc, Gl, AF.Exp)
            self.tt(self.Sg[:], self.Sg[:], self.bc(tmpc.unsqueeze(2), (128, 8, 128)), ALU.mult)
            self.tt(self.Sg[:], self.Sg[:], pS, ALU.add)
            if last:
                self.dma(self.o_gdn_p.rearrange("h k v -> k h v"), self.Sg[:])
        else:
            egl = c[:, 96:104]
            self.act(egl, Gl, AF.Exp)
            glall = c[:, 112:240].rearrange("p (h s) -> p h s", h=8)
            gsc = g[3][:, 0:128].rearrange("p (h s) -> p h s", h=8)
            self.tt(gsc, self.bc(egl.unsqueeze(2), (128, 8, 16)), self.bc(M('rowm').unsqueeze(1), (128, 8, 16)), ALU.mult)
            self.mm(ps[0][:, 0, 0:128], self.C('ones'), g[3][:, 0:128])
            self.ts(glall, ps[0][:, 0, 0:128].rearrange("p (h s) -> p h s", h=8), 1.0 / 8, ALU.mult)
            v8_ = lambda t: t[:].rearrange("p (h t) -> p h t", h=8)
            for s in range(16):
                b = s % 2
                Sb = (self.Sg, self.Hb)[b]
                ktmm = v8_((g[1], g[4])[b])
                outb = v8_((g[0], g[2])[b])
                pSb = (ps[3], ps[2])[b][:].rearrange("p a (b t) -> p (a b) t", b=4)
                self.dma(Sb[:], self.sg[s].rearrange("h k v -> k h v"))
                self.act(ktmm, ktm, AF.Copy, scale=M('rowm')[:, s:s + 1])
                for h in range(8):
                    self.mm(pSb[:, h, :], ktmm[:, h, :], u[:, h, :])
                for h in range(8):
                    self.act(outb[:, h, :], Sb[:, h, :], AF.Copy, scale=glall[:, h, s:s + 1])
                self.tt(outb, outb, pSb, ALU.add)
                self.dma(self.o_gdn_s[s].rearrange("h k v -> k h v"), outb)
        if kind == 's' or last:
            nr = 3 * nseq
            cst = self.qkv[:].rearrange("p c t -> p (c t)")[:, 0:24 * nr].rearrange("p (c s r) -> p c s r", c=24, s=nseq)
            self.cp(cst, xpv[:, :, :, L:L + 3])
            cst2 = self.qkv[:].rearrange("p c t -> p (c t)")[:, 0:24 * nr].rearrange("p (c r) -> p c r", c=24)
            for cg in range(3):
                for j in range(8):
                    cc = cg * 8 + j
                    self.tr(ps[j // 4][0:nr, (j % 4) // 2, (j % 2) * 128:(j % 2) * 128 + 128],
                            cst2[:, cc, :])
                for j in range(8):
                    cc = cg * 8 + j
                    self.cp(self.st48[0:nr, cc * 128:(cc + 1) * 128],
                            ps[j // 4][0:nr, (j % 4) // 2, (j % 2) * 128:(j % 2) * 128 + 128])
            self.dma(self.o_gconv_p if kind == 'p' else self.o_gconv_s, self.st48[0:nr, 0:3072])

        if self.STOP == 'E':
            return
        self.rwkv(kind, nseq, L, nlev, ppv, last, so_=(full == 0))

        if not full:
            return
        if self.DBG:
            self.cp(self.xn[:], self.mixedT[:].rearrange("p a b -> p (a b)"))
            self.dma(self.dbg[0 if kind == 'p' else 1], self.xn[:])
        if self.STOP == 'F':
            return
        self.stream_down(self.c_o, 16, self.mixedT, 0)
        for q in range(2):
            self.tt(self.xt[:, q * 1024:(q + 1) * 1024], self.xt[:, q * 1024:(q + 1) * 1024],
                    ps[q][:].rearrange("p a b -> p (a b)"), ALU.add)
        if self.STOP == 'G':
            return
        if pair_slot is not None:
            self.norm_T('ln2g', dst=self.xnT2[:, :, pair_slot * 128:(pair_slot + 1) * 128])
            return
        self.ffn(kind, nseq, L, last, hist_only=(full == 1))
        if full == 1:
            self.ts(self.fhist[:], self.fhist[:], self.P('hflag')[:, 0:1], ALU.mult)
            return
        self.final(y_dst)

    def final(self, y_dst):
        c = self.cols
        self.act(self.xn[:], self.xt[:], AF.Square, accum=c[:, 0:1])
        self.rsqrt(c[:, 1:2], c[:, 0:1], 1.0 / D, 1e-6)
        self.stt(self.xn[:], self.xt[:], c[:, 1:2], self.fg[:], ALU.mult, ALU.mult)
        self.dma(y_dst, self.xn[:])

    def rwkv(self, kind, nseq, L, nlev, ppv, last, so_=False):
        ps, g, c = self.ps, self.g, self.cols
        M = lambda n: self.C(n + '_' + kind)
        T = 128
        v8 = lambda t: t[:].rearrange("p (h t) -> p h t", h=8)
        if kind == 'p':
            self.cp(self.phist[:], ppv[:, :, 0, L:L + 1])
        if kind == 's' or last:
            for cg in range(4):
                for j in range(8):
                    cc = cg * 8 + j
                    if cc < 26:
                        self.tr(ps[j // 4][0:nseq, (j % 4) // 2, (j % 2) * 128:(j % 2) * 128 + 128],
                                ppv[:, cc, :, L])
                for j in range(8):
                    cc = cg * 8 + j
                    if cc < 26:
                        self.cp(self.st48[0:nseq, cc * 128:(cc + 1) * 128],
                                ps[j // 4][0:nseq, (j % 4) // 2, (j % 2) * 128:(j % 2) * 128 + 128])
            self.dma(self.o_shift_p if kind == 'p' else self.o_shift_s, self.st48[0:nseq, 0:3328])
        xs = self.scrA[:, 0:26 * 128].rearrange("p (c s t) -> p c s t", c=26, s=nseq)
        xsf = self.scrA[:, 0:26 * 128].rearrange("p (c t) -> p c t", c=26)
        self.tt(xs, ppv[:, :, :, 0:L], ppv[:, :, :, 1:1 + L], ALU.subtract)
        self.tt(xsf, xsf, self.bc(self.P('mu').unsqueeze(2), (128, 26, 128)), ALU.mult)
        self.tt(xs, xs, ppv[:, :, :, 1:1 + L], ALU.add)
        r, k, v = xsf[:, 0:8, :], xsf[:, 8:16, :], xsf[:, 16:24, :]
        twd = g[0]
        self.act(twd[0:64, 0:128], xsf[0:64, 24, :], AF.Tanh)
        for ch in range(8):
            self.mm(ps[0][:, ch // 4, (ch % 4) * 128:(ch % 4) * 128 + 128], self.wab[0:64, ch * 128:(ch + 1) * 128],
                    twd[0:64, 0:128])
        for ch in range(8):
            self.mm(ps[1][:, ch // 4, (ch % 4) * 128:(ch % 4) * 128 + 128], self.wab[64:128, ch * 128:(ch + 1) * 128],
                    xsf[64:128, 24, :])
        p8 = lambda t: t[:].rearrange("p a (b t) -> p (a b) t", b=4)
        ew = v8(g[1])
        self.tt(ew, p8(ps[0]), self.bc(self.P('w0').unsqueeze(2), (128, 8, 128)), ALU.add)
        self.act(ew, ew, AF.Exp, scale=-1.0)
        self.act(ew, ew, AF.Ln, bias=1.0)
        self.act(ew, ew, AF.Exp, scale=-1.0, bias=-0.5)
        av = v8(g[2])
        self.tt(av, p8(ps[1]), self.bc(self.P('a0').unsqueeze(2), (128, 8, 128)), ALU.add)
        self.act(av, av, AF.Sigmoid)
        sgd = g[0]
        gate = v8(g[3])
        if not so_:
            self.act(sgd[:, 128:256], xsf[:, 25, :], AF.Sigmoid)
            for ch in range(8):
                self.mm(ps[2][:, ch // 4, (ch % 4) * 128:(ch % 4) * 128 + 128], self.gb[:, ch * 128:(ch + 1) * 128],
                        sgd[:, 128:256])
            self.cp(gate, p8(ps[2]), eng='act')
        kkv = v8(g[4])
        self.tt(kkv, k, self.bc(self.P('kk').unsqueeze(2), (128, 8, 128)), ALU.mult)
        sq = v8(g[5])
        self.tt(sq, kkv, kkv, ALU.mult)
        for q in range(2):
            self.mm(ps[3][:, q, :], self.C('blk'), g[5][:, q * 512:(q + 1) * 512])
        rn = v8(g[5])
        self.rsqrt(rn, p8(ps[3]), 1.0, 1e-12)
        self.tt(kkv, kkv, rn, ALU.mult)
        k2 = v8(g[5])
        self.ts(k2, av, -1.0, ALU.add)
        self.tt(k2, k2, self.bc(self.P('ka').unsqueeze(2), (128, 8, 128)), ALU.mult)
        self.stt(k2, k2, 1.0, k, ALU.add, ALU.mult)
        bon = v8(g[6])
        if not so_:
            self.tt(bon, r, k2, ALU.mult)
            self.tt(bon, bon, self.bc(self.P('rk').unsqueeze(2), (128, 8, 128)), ALU.mult)
            for q in range(2):
                self.mm(ps[0][:, q, :], self.C('blk'), g[6][:, q * 512:(q + 1) * 512])
            self.tt(bon, p8(ps[0]), v, ALU.mult)
        if self.STOP == 'E1':
            return
        cs = v8(g[7])
        for ch in range(8):
            self.scan(cs[:, ch, :], M('seg'), ew[:, ch, :])
        at = v8(g[8])
        rt = v8(g[9])
        bt = v8(g[10])
        kt = v8(g[11])
        e1 = v8(g[12])
        self.tt(e1, cs, ew, ALU.subtract)
        self.act(e1, e1, AF.Exp, scale=-1.0)
        self.tt(at, kkv, e1, ALU.mult)
        self.act(e1, cs, AF.Exp, scale=-1.0)
        self.tt(rt, r, e1, ALU.mult)
        self.act(e1, cs, AF.Exp)
        self.tt(bt, kkv, av, ALU.mult)
        self.tt(bt, bt, e1, ALU.mult)
        self.tt(kt, k2, e1, ALU.mult)
        pc = c[:, 112:112 + 8 * nseq].rearrange("p (h s) -> p h s", h=8)
        csv = g[7][:].rearrange("p (h s t) -> p h s t", h=8, s=nseq)
        self.act(pc, csv[:, :, :, L - 1], AF.Exp, scale=-1.0)
        pcb = self.bc(pc.unsqueeze(3), (128, 8, nseq, L))
        bh = v8(g[12])
        kh = v8(g[4])
        v4 = lambda t: t[:].rearrange("p (h s t) -> p h s t", h=8, s=nseq)
        self.tt(v4(g[12]), v4(g[10]), pcb, ALU.mult)
        self.tt(v4(g[4]), v4(g[11]), pcb, ALU.mult)
        Vtm, Bh, Kh = g[0], g[1], g[2]
        for (dst, src) in ((Vtm, v), (Bh, bh), (Kh, kh)):
            for ch in range(8):
                self.tr(ps[1][:, ch // 4, (ch % 4) * 128:(ch % 4) * 128 + 128], src[:, ch, :])
            self.cp(dst[:], ps[1][:].rearrange("p a b -> p (a b)"), eng='act')
        if self.STOP == 'E2':
            return
        ytm = self.qkv[:, 0:8, :]
        Utm = self.qkv[:, 8:16, :]
        ytf = self.qkv[:, 0:8, :].rearrange("p a b -> p (a b)")
        Utf = self.qkv[:, 8:16, :].rearrange("p a b -> p (a b)")
        Sld = self.qkv[:, 16:24, :]
        AkT = self.xp[:, 0:2048].rearrange("p (h t) -> p h t", h=16)
        RkT = self.xp[:, 2048:4096].rearrange("p (h t) -> p h t", h=16)
        RbT = self.pp[:, 0:2048].rearrange("p (h t) -> p h t", h=16)
        hb = lambda t, i: t[:].bitcast(BF16)[:, i * 1024:(i + 1) * 1024].rearrange("p (h t) -> p h t", h=8)
        A0, B0 = hb(g[5], 0), hb(g[5], 1)
        bonz = self.zT[:]
        if not so_:
            self.cp(bonz, bon, eng='pool')
        A1, B1 = hb(g[7], 0), hb(g[7], 1)
        Xb8 = g[12][:].bitcast(BF16)[:, 0:512].rearrange("p (h t) -> p h t", h=8)
        if kind == 's':
            atm = self.pp[:, 2048:3072].rearrange("p (h t) -> p h t", h=8)
            rtm = self.scrA[:, 0:1024].rearrange("p (h t) -> p h t", h=8)
        for half in range(2):
            hs = range(half * 8, half * 8 + 8)
            def slot(t, i):
                return t[:, i % 2, (i // 2) * 128:(i // 2) * 128 + 128]
            pperm = lambda t: t[:].rearrange("p a (b t) -> p a b t", b=4)
            dperm = lambda d: d.rearrange("p (i two) t -> p two i t", two=2)
            m8 = lambda n: self.bc(M(n).unsqueeze(1).unsqueeze(1), (128, 2, 4, 128))
            for i, h in enumerate(hs):
                ch, o64 = h // 2, (h % 2) * 64
                sl = slice(o64, o64 + 64)
                self.mm(slot(ps[0], i), bt[sl, ch, :], at[sl, ch, :])
                if not so_:
                    self.mm(slot(ps[1], i), bt[sl, ch, :], rt[sl, ch, :])
                self.mm(slot(ps[2], i), at[sl, ch, :], bt[sl, ch, :])
            self.tt(dperm(B0), pperm(ps[0]), m8('strictT'), ALU.mult)
            if not so_:
                self.tt(dperm(RbT[:, half * 8:half * 8 + 8, :]), pperm(ps[1]), m8('causT'), ALU.mult)
            self.tt(dperm(A0), pperm(ps[2]), m8('strict'), ALU.mult)
            for i, h in enumerate(hs):
                ch, o64 = h // 2, (h % 2) * 64
                sl = slice(o64, o64 + 64)
                self.mm(slot(ps[0], i), kt[sl, ch, :], at[sl, ch, :])
                if not so_:
                    self.mm(slot(ps[1], i), kt[sl, ch, :], rt[sl, ch, :])
            self.tt(dperm(AkT[:, half * 8:half * 8 + 8, :]), pperm(ps[0]), m8('strictT'), ALU.mult)
            if not so_:
                self.tt(dperm(RkT[:, half * 8:half * 8 + 8, :]), pperm(ps[1]), m8('causT'), ALU.mult)
            prh = ps[1][:, 0, :]
            py = ps[1][:, 1, :]
            zz = self.zeros
            self.mm(prh, zz[:, 0:128], zz[:, 0:512], start=True, stop=False, skip=True)
            if not so_:
                self.mm(py, zz[:, 0:128], zz[:, 0:512], start=True, stop=False, skip=True)
            if kind == 'p':
                for cc in range(4):
                    ch = half * 4 + cc
                    self.mm(prh[:, cc * 128:(cc + 1) * 128], at[:, ch, :], self.Hb[:, ch, :], start=False, stop=False,
                            skip=True)
                    if not so_:
                        self.mm(py[:, cc * 128:(cc + 1) * 128], rt[:, ch, :], self.Hb[:, ch, :], start=False,
                                stop=False, skip=True)
            else:
                if half == 0:
                    self.memset(atm, 0.0)
                    self.memset(rtm, 0.0)
                for s in range(16):
                    Hs = (self.Hb, self.Sg)[s % 2]
                    self.dma(Hs[:], self.sr[s])
                    if s > 0:
                        self.memset(atm[:, :, 8 * s - 8:8 * s], 0.0)
                        self.memset(rtm[:, :, 8 * s - 8:8 * s], 0.0)
                    self.cp(atm[:, :, 8 * s:8 * s + 8], at[:, :, 8 * s:8 * s + 8])
                    self.cp(rtm[:, :, 8 * s:8 * s + 8], rt[:, :, 8 * s:8 * s + 8])
                    for cc in range(4):
                        ch = half * 4 + cc
                        self.mm(prh[:, cc * 128:(cc + 1) * 128], atm[:, ch, :], Hs[:, ch, :], start=False,
                                stop=False, skip=True)
                        self.mm(py[:, cc * 128:(cc + 1) * 128], rtm[:, ch, :], Hs[:, ch, :], start=False,
                                stop=False, skip=True)
                if half == 0:
                    self.memset(atm[:, :, 120:128], 0.0)
                    self.memset(rtm[:, :, 120:128], 0.0)
            for i, h in enumerate(hs):
                self.mm(prh[:, i * 64:(i + 1) * 64], AkT[:, h, :], Vtm[:, h * 64:(h + 1) * 64], start=False,
                        stop=True, skip=True)
            X = Utm[:, half * 4:half * 4 + 4, :].rearrange("p a (b c) -> p (a b) c", b=2)
            self.ts(X, prh.rearrange("p (h c) -> p h c", h=8), -1.0, ALU.mult)
            self.solve(A0, B0, A1, B1, X, 64, nlev, Xb=Xb8, xparts=[(X, Xb8)])
            for i, h in enumerate(hs if not so_ else []):
                self.mm(py[:, i * 64:(i + 1) * 64], RbT[:, h, :], Utf[:, h * 64:(h + 1) * 64], start=False, stop=False,
                        skip=True)
                self.mm(py[:, i * 64:(i + 1) * 64], RkT[:, h, :], Vtm[:, h * 64:(h + 1) * 64], start=False, stop=True,
                        skip=True)
            if not so_:
                self.cp(ytf[:, half * 512:(half + 1) * 512], py, eng='act')
        if self.STOP == 'E3':
            return
        if kind == 'p':
            for ch in range(8):
                po = ps[ch // 4][:, (ch % 4) // 2, (ch % 2) * 128:(ch % 2) * 128 + 128]
                self.mm(po, Bh[:, ch * 128:(ch + 1) * 128], Utf[:, ch * 128:(ch + 1) * 128], start=True, stop=False)
                self.mm(po, Kh[:, ch * 128:(ch + 1) * 128], Vtm[:, ch * 128:(ch + 1) * 128], start=False, stop=True)
            hn = v8(g[5])
            for q in range(2):
                self.tt(hn[:, 4 * q:4 * q + 4, :].rearrange("p (a b) t -> p a b t", a=2),
                        ps[q][:, :, 0:256].rearrange("p a (b t) -> p a b t", b=2),
                        self.bc(self.C('blk').unsqueeze(1).unsqueeze(1), (128, 2, 2, 128)), ALU.mult)
            self.tt(self.Hb[:], self.Hb[:], self.bc(pc[:, :, 0:1], (128, 8, 128)), ALU.mult)
            self.tt(self.Hb[:], self.Hb[:], hn, ALU.add)
            if last:
                self.store_H(self.o_rwkv_p)
        else:
            for s in range(16):
                b = s % 2
                Bm, Km = (g[5], g[4])[b], (g[6], g[12])[b]
                hn, t2 = v8((g[7], g[8])[b]), v8((g[9], g[10])[b])
                Hs = (self.Hb, self.Sg)[b]
                self.dma(Hs[:], self.sr[s])
                self.act(Bm[:], Bh[:], AF.Copy, scale=M('rowm')[:, s:s + 1])
                self.act(Km[:], Kh[:], AF.Copy, scale=M('rowm')[:, s:s + 1])
                for ch in range(8):
                    po = ps[2 * b + ch // 4][:, (ch % 4) // 2, (ch % 2) * 128:(ch % 2) * 128 + 128]
                    self.mm(po, Bm[:, ch * 128:(ch + 1) * 128], Utf[:, ch * 128:(ch + 1) * 128], start=True, stop=False)
                    self.mm(po, Km[:, ch * 128:(ch + 1) * 128], Vtm[:, ch * 128:(ch + 1) * 128], start=False, stop=True)
                for ch in range(8):
                    self.act(t2[:, ch, :], Hs[:, ch, :], AF.Copy, scale=pc[:, ch, s:s + 1])
                for q in range(2):
                    self.tt(hn[:, 4 * q:4 * q + 4, :].rearrange("p (a b) t -> p a b t", a=2),
                            ps[2 * b + q][:, :, 0:256].rearrange("p a (b t) -> p a b t", b=2),
                            self.bc(self.C('blk').unsqueeze(1).unsqueeze(1), (128, 2, 2, 128)), ALU.mult)
                self.tt(hn, hn, t2, ALU.add)
                self.dma(self.o_rwkv_s[s], hn)
        if so_:
            return
        y3 = ytf.rearrange("p (h c) -> p h c", h=16)
        mean = c[:, 0:16]
        var = c[:, 16:32]
        self.reduce(mean, y3)
        self.ts(mean, mean, 1.0 / 64, ALU.mult)
        self.tt(y3, y3, self.bc(mean.unsqueeze(2), (128, 16, 64)), ALU.subtract)
        ysq = g[5][:].rearrange("p (h c) -> p h c", h=16)
        self.tt(ysq, y3, y3, ALU.mult)
        self.reduce(var, ysq)
        self.rsqrt(var, var, 1.0 / 64, 64e-5)
        self.tt(y3, y3, self.bc(var.unsqueeze(2), (128, 16, 64)), ALU.mult)
        for ch in range(8):
            self.tr(ps[2][:, ch // 4, (ch % 4) * 128:(ch % 4) * 128 + 128], ytf[:, ch * 128:(ch + 1) * 128])
        yT = v8(g[5])
        self.tt(yT, p8(ps[2]), self.bc(self.P('gnw').unsqueeze(2), (128, 8, 128)), ALU.mult)
        self.tt(yT, yT, self.bc(self.P('gnb').unsqueeze(2), (128, 8, 128)), ALU.add)
        self.tt(yT, yT, bonz, ALU.add)
        self.tt(self.mixedT[:, 8:16, :], yT, gate, ALU.mult)

    def load_H(self, s):
        self.dma(self.Hb[:], self.sr[s])

    def store_H(self, dst):
        self.dma(dst, self.Hb[:])

    def ffn(self, kind, nseq, L, last, hist_only=False):
        ps, g, c = self.ps, self.g, self.cols
        self.norm_T('ln2g')
        Lh = L + 2
        actT = self.xp[:, 0:2816].bitcast(BF16).rearrange("p (j t) -> p j t", j=44)
        fw = self.P('fconvw').rearrange("p (c i) -> p c i", i=3)
        qflat = self.qkv[:].rearrange("p c t -> p (c t)")
        for gi in range(6):
            j0 = gi * 8
            ng = min(8, 44 - j0)
            self.stream_up(self.c_up, [(j0 * 128, ng * 128, 0), (DFF + j0 * 128, ng * 128, 1024)], 0)
            for which in range(2):
                ch0 = which * 44 + j0
                c0 = ch0 * 128
                h = self.pp[:, which * 1920:which * 1920 + ng * nseq * Lh].rearrange("p (c s t) -> p c s t", c=ng, s=nseq)
                pt = ps[which][:].rearrange("p a (b t) -> p (a b) t", b=4)[:, 0:ng, :]
                if kind == 'p':
                    self.cp(h[:, :, 0, 0:2], self.fhist[:, ch0:ch0 + ng, :])
                else:
                    st = g[which][0:32, 0:ng * 128]
                    self.dma(st, self.sfc[:, c0:c0 + ng * 128])
                    for j in range(ng):
                        self.tr(ps[2][:, which, j * 32:(j + 1) * 32], st[:, j * 128:(j + 1) * 128])
                    self.cp(h[:, :, :, 0:2], ps[2][:, which, 0:ng * 32].rearrange("p (c s t) -> p c s t", c=ng, s=16))
                self.cp(h[:, :, :, 2:2 + L], pt.rearrange("p c (s t) -> p c s t", s=nseq), eng='act')
                if kind == 'p':
                    self.cp(self.fhist[:, ch0:ch0 + ng, :], h[:, :, 0, L:L + 2])
                    if last:
                        so = g[2 + which][0:2, 0:ng * 128]
                        for j in range(ng):
                            self.tr(ps[3][0:2, j // 4, (j % 4) * 128:(j % 4) * 128 + 128], h[:, j, 0, L:L + 2])
                        self.cp(so, ps[3][0:2, :, :].rearrange("p a b -> p (a b)")[:, 0:ng * 128])
                        self.dma(self.o_ffn_p[:, c0:c0 + ng * 128], so)
                else:
                    hs = self.scrA[:, 2048 + which * 256:2048 + which * 256 + ng * 32]
                    self.cp(hs.rearrange("p (c s r) -> p c s r", c=ng, s=16), h[:, :, :, L:L + 2])
                    hs3 = hs.rearrange("p (c r) -> p c r", c=ng)
                    so = g[2 + which][0:32, 0:ng * 128]
                    for j in range(ng):
                        self.tr(ps[3][0:32, j // 4, (j % 4) * 128:(j % 4) * 128 + 128], hs3[:, j, :])
                    self.cp(so, ps[3][0:32, :, :].rearrange("p a b -> p (a b)")[:, 0:ng * 128])
                    self.dma(self.o_ffn_s[:, c0:c0 + ng * 128], so)
                if hist_only:
                    continue
                wsh = (128, ng, nseq, L)
                o = self.scrA[:, which * 1024:which * 1024 + ng * 128].rearrange("p (c s t) -> p c s t", c=ng, s=nseq)
                tmp = qflat[:, which * 1024:which * 1024 + ng * 128].rearrange("p (c s t) -> p c s t", c=ng, s=nseq)
                self.tt(o, h[:, :, :, 2:2 + L], self.bc(fw[:, ch0:ch0 + ng, 2:3].unsqueeze(3), wsh), ALU.mult)
                for i in range(2):
                    self.tt(tmp, h[:, :, :, i:i + L], self.bc(fw[:, ch0:ch0 + ng, i:i + 1].unsqueeze(3), wsh), ALU.mult,
                            eng='pool')
                    self.tt(o, o, tmp, ALU.add)
            if hist_only:
                continue
            gt = self.scrA[:, 0:ng * 128]
            up = self.scrA[:, 1024:1024 + ng * 128]
            self.act(gt, gt, AF.Silu)
            self.tt(actT[:, j0:j0 + ng, :], gt.rearrange("p (c t) -> p c t", c=ng),
                    up.rearrange("p (c t) -> p c t", c=ng), ALU.mult)
        if hist_only:
            return
        self.stream_down(self.c_down, 44, actT, 1)
        for q in range(2):
            self.tt(self.xt[:, q * 1024:(q + 1) * 1024], self.xt[:, q * 1024:(q + 1) * 1024],
                    ps[2 + q][:].rearrange("p a b -> p (a b)"), ALU.add)

    def ffn_pair(self, last):
        ps, g = self.ps, self.g
        zz = self.zeros
        fw = self.P('fconvw').rearrange("p (c i) -> p c i", i=3)
        actA = self.xp[:, 0:2816].bitcast(BF16).rearrange("p (j t) -> p j t", j=22)
        actB = self.pp[:, 0:2816].bitcast(BF16).rearrange("p (j t) -> p j t", j=22)
        act_ap = lambda j: (actA if j < 22 else actB)[:, j % 22, :]
        hbuf = [self.qkv[:].rearrange("p c t -> p (c t)"), self.scrA]
        grp = 0
        for gi in range(6):
            j0 = gi * 8
            ng = min(8, 44 - j0)
            nq = (ng + 3) // 4
            for which in range(2):
                ch0 = which * 44 + j0
                pair = grp % 2
                grp += 1
                for bk in range(ng // 2):
                    self.mm(ps[pair * 2 + bk // 2][:, bk % 2, :], zz[:, 0:128], zz[:, 0:512], start=True, stop=False,
                            skip=True)
                for k in range(16):
                    t = self.wslot_load([(0, self.c_up[k * 128:(k + 1) * 128, ch0 * 128:(ch0 + ng) * 128])])
                    for j in range(ng):
                        self.mm(ps[pair * 2 + j // 4][:, (j % 4) // 2, (j % 2) * 256:(j % 2) * 256 + 256],
                                t[:, j * 128:(j + 1) * 128], self.xnT2[:, k, :], start=False, stop=(k == 15), skip=True)
                h = hbuf[which][:, 0:ng * 258].rearrange("p (c t) -> p c t", c=ng)
                self.cp(h[:, :, 0:2], self.fhist[:, ch0:ch0 + ng, :])
                for q in range(nq):
                    n4 = min(4, ng - 4 * q)
                    self.cp(h[:, 4 * q:4 * q + n4, 2:258],
                            ps[pair * 2 + q][:].rearrange("p a (b t) -> p (a b) t", b=2)[:, 0:n4, :],
                            eng='act' if q == 0 else 'dve')
                self.cp(self.fhist[:, ch0:ch0 + ng, :], h[:, :, 256:258])
                if last:
                    po_ = ps[(pair ^ 1) * 2]
                    so = g[12][0:2, 0:ng * 128]
                    for j in range(ng):
                        self.tr(po_[0:2, j // 4, (j % 4) * 128:(j % 4) * 128 + 128], h[:, j, 256:258])
                    self.cp(so, po_[0:2, :, :].rearrange("p a b -> p (a b)")[:, 0:ng * 128])
                    self.dma(self.o_ffn_p[:, ch0 * 128:(ch0 + ng) * 128], so)
                for q in range(nq):
                    n4 = min(4, ng - 4 * q)
                    wsh = (128, n4, 256)
                    o = g[which * 2 + q][:, 0:n4 * 256].rearrange("p (c t) -> p c t", c=n4)
                    tmp = g[4 + q][:, 0:n4 * 256].rearrange("p (c t) -> p c t", c=n4)
                    cs_ = slice(ch0 + 4 * q, ch0 + 4 * q + n4)
                    hq = h[:, 4 * q:4 * q + n4, :]
                    self.tt(o, hq[:, :, 2:258], self.bc(fw[:, cs_, 2:3], wsh), ALU.mult)
                    for i in range(2):
                        self.tt(tmp, hq[:, :, i:i + 256], self.bc(fw[:, cs_, i:i + 1], wsh), ALU.mult, eng='pool')
                        self.tt(o, o, tmp, ALU.add)
            for q in range(nq):
                n4 = min(4, ng - 4 * q)
                gt = g[q][:, 0:n4 * 256]
                up = g[2 + q][:, 0:n4 * 256]
                self.act(gt, gt, AF.Silu)
                for i in range(n4):
                    j = j0 + 4 * q + i
                    self.tt(act_ap(j), gt[:, i * 256:(i + 1) * 256], up[:, i * 256:(i + 1) * 256], ALU.mult,
                            eng='dve' if i % 2 == 0 else 'pool')
        for k in range(44):
            t = self.wslot_load([(0, self.c_down[k * 128:(k + 1) * 128, :])])
            for i in range(2):
                for n in range(4):
                    self.mm(ps[i * 2 + n // 2][:, n % 2, :], act_ap(k)[:, i * 128:(i + 1) * 128],
                            t[:, n * 512:(n + 1) * 512], start=(k == 0), stop=(k == 43))
        for i in range(2):
            xt = self.xts[i]
            for q in range(2):
                self.tt(xt[:, q * 1024:(q + 1) * 1024], xt[:, q * 1024:(q + 1) * 1024],
                        ps[i * 2 + q][:].rearrange("p a b -> p (a b)"), ALU.add)

    def build(self, do_sample=True):
        self.dma(self.params[:], self.params_d)
        self.dma(self.consts[:], self.consts_d)
        self.dma(self.fg[:], self.fg_d)
        self.dma(self.wab[:], self.wab_d)
        self.dma(self.gb[:], self.gb_d)
        self.memset(self.zeros[:], 0.0)
        self.convert_weights()
        ntile = self.NPRE + self.NMAIN
        paired = (self.NMAIN % 2 == 0) and PAIR_FFN
        for t in range(ntile):
            full = 2 if t >= self.NPRE else (1 if t == self.NPRE - 1 else 0)
            self.xt = self.xts[0]
            self.pool_eng = 'pool' if t >= 2 else 'dve'
            if full == 2:
                i = t - self.NPRE
                src, dst = self.xmain[i * 128:(i + 1) * 128, :], self.y_main[i * 128:(i + 1) * 128, :]
                if paired:
                    self.xt = self.xts[i % 2]
                    self.tile('p', src, dst, full, t == 0, t == ntile - 1, None, pair_slot=i % 2)
                    if i % 2 == 1:
                        self.ffn_pair(t == ntile - 1)
                        for m in range(2):
                            self.xt = self.xts[m]
                            self.final(self.y_main[(i - 1 + m) * 128:(i + m) * 128, :])
                    continue
            else:
                src, dst = self.xpre[t * 128:(t + 1) * 128, :], None
            self.tile('p', src, dst, full, t == 0, t == ntile - 1, None)
        self.xt = self.xts[0]
        if do_sample:
            self.tile('s', self.xs, self.y_s, 2, False, False, True)
        self.em.finish()
        self.em.replay()
        return self.nc


def rwkv_to_blockdiag(st):
    n = st.shape[0]
    out = np.zeros((n, 128, 8, 128), np.float32)
    t = np.asarray(st, np.float32).reshape(n, 8, 2, 64, 64)
    for two in range(2):
        out[:, two * 64:(two + 1) * 64, :, two * 64:(two + 1) * 64] = t[:, :, two].transpose(0, 3, 1, 2)
    return out


def rwkv_from_blockdiag(o):
    n = o.shape[0]
    res = np.zeros((n, 8, 2, 64, 64), np.float32)
    for two in range(2):
        res[:, :, two] = o[:, two * 64:(two + 1) * 64, :, two * 64:(two + 1) * 64].transpose(0, 2, 3, 1)
    return res.reshape(n, 16, 64, 64)


_CACHE = {}


def get_program(NPRE, NMAIN):
    key = (NPRE, NMAIN)
    if key not in _CACHE:
        _CACHE[key] = Builder(NPRE, NMAIN).build()
    return _CACHE[key]


def kernel(**inp):
    inp = {k: np.asarray(v) for k, v in inp.items()}
    NPRE, NMAIN = 8, 8
    nc = get_program(NPRE, NMAIN)
    consts = make_consts()
    params = make_params(inp)
    fg = np.ascontiguousarray(np.broadcast_to(inp['final_g'][None, :], (128, D))).astype(np.float32)
    wab = np.ascontiguousarray(np.concatenate([inp['rwkv_w_b'][0], inp['rwkv_a_b'][0]], axis=0)).astype(np.float32)
    gb = np.ascontiguousarray(inp['rwkv_g_b'][0]).astype(np.float32)
    shared = dict(w_in=np.ascontiguousarray(inp['w_in'][0]), w_o=np.ascontiguousarray(inp['w_o'][0]),
                  w_up=np.ascontiguousarray(inp['ffn_w_up'][0]), w_down=np.ascontiguousarray(inp['ffn_w_down'][0]),
                  params=params, consts=consts, fg=fg, wab=wab, gb=gb)
    xp_, xs_ = inp['x_prompt'], inp['x_sample']
    in_maps = []
    for cid in range(NCORES):
        b, half = cid // 2, cid % 2
        m = dict(shared)
        pc_ = params.copy()
        pc_[:, POFF['hflag'][0]] = float(half)
        m['params'] = pc_
        if half == 0:
            m['xpre'] = np.zeros((NPRE * 128, D), np.float32)
            m['xmain'] = np.ascontiguousarray(xp_[b, 0:1024])
        else:
            m['xpre'] = np.ascontiguousarray(xp_[b, 0:1024])
            m['xmain'] = np.ascontiguousarray(xp_[b, 1024:2048])
        sl = slice(cid * 16, cid * 16 + 16)
        m['xs'] = np.ascontiguousarray(xs_[sl].reshape(128, D))
        m['sg'] = np.ascontiguousarray(inp['state_gdn'][0, sl])
        m['sgc'] = np.ascontiguousarray(inp['state_gdn_conv'][0, sl].reshape(48, 3072))
        m['sr'] = rwkv_to_blockdiag(inp['state_rwkv'][0, sl])
        m['ssh'] = np.ascontiguousarray(inp['state_rwkv_shift'][0, sl])
        m['sfc'] = np.ascontiguousarray(inp['state_ffn_conv'][0, sl].reshape(32, 11264))
        in_maps.append(m)
    res = run_bass_kernel_spmd(nc, in_maps, core_ids=list(range(NCORES)))
    R = res.results
    y_prompt = np.zeros((4, 2048, D), np.float32)
    for cid in range(NCORES):
        b, half = cid // 2, cid % 2
        y_prompt[b, half * 1024:(half + 1) * 1024] = R[cid]['y_main']
    y_sample = np.concatenate([R[c]['y_s'].reshape(16, 8, D) for c in range(NCORES)], 0)
    odd = [1, 3, 5, 7]
    gdn_p = np.stack([R[c]['gdn_p'] for c in odd])[None]
    gconv_p = np.stack([R[c]['gconv_p'] for c in odd])[None]
    rwkv_p = np.stack([rwkv_from_blockdiag(R[c]['rwkv_p'][None])[0] for c in odd])[None]
    shift_p = np.stack([R[c]['shift_p'].reshape(3328) for c in odd])[None]
    ffn_p = np.stack([R[c]['ffn_p'] for c in odd])[None]
    gdn_s = np.concatenate([R[c]['gdn_s'] for c in range(NCORES)], 0)[None]
    gconv_s = np.concatenate([R[c]['gconv_s'].reshape(16, 3, 3072) for c in range(NCORES)], 0)[None]
    rwkv_s = np.concatenate([rwkv_from_blockdiag(R[c]['rwkv_s']) for c in range(NCORES)], 0)[None]
    shift_s = np.concatenate([R[c]['shift_s'] for c in range(NCORES)], 0)[None]
    ffn_s = np.concatenate([R[c]['ffn_s'].reshape(16, 2, 11264) for c in range(NCORES)], 0)[None]
    return (y_prompt, y_sample, gdn_p, gconv_p, rwkv_p, shift_p, ffn_p, gdn_s, gconv_s, rwkv_s, shift_s, ffn_s)
```

```python
import numpy as np
import concourse.bass as bass
import concourse.mybir as mybir
from concourse.bass_utils import run_bass_kernel_spmd

F32 = mybir.dt.float32
BF16 = mybir.dt.bfloat16
AF = mybir.ActivationFunctionType
ALU = mybir.AluOpType
AX = mybir.AxisListType

D = 2048
DFF = 5632
INW = 7440
NCORES = 8
SELF_SYNC = True
RELAX_SELF_WAR = True
FP32R = False
PAIR_FFN = True
FAST_RSQRT = True


class Em:
    ENG = ['pe', 'act', 'dve', 'pool', 'sp']

    def __init__(self, nc):
        self.nc = nc
        self.stream = {e: [] for e in self.ENG}
        self.sems = {}
        for e in self.ENG:
            self.sems["cnt_" + e] = nc.alloc_semaphore("cnt_" + e)
        self.ecnt = {e: 0 for e in self.ENG}
        self.epoch = {e: 0 for e in self.ENG}
        self.cur = {e: "cnt_" + e for e in self.ENG}
        self.waited = {}
        self.state = {}
        self.dcount = {}
        self.ninst = 0

    @staticmethod
    def _keys(aps):
        ks = []
        for a in aps:
            if a is None or isinstance(a, (int, float)):
                continue
            ks.append(a if isinstance(a, str) else a.tensor.name)
        return ks

    def _need(self, eng, tok):
        semname, val = tok
        if semname.startswith("cnt_" + eng) and (eng == 'pe' or not SELF_SYNC):
            return
        k = (eng, semname)
        if self.waited.get(k, 0) >= val:
            return
        self.waited[k] = val
        self.stream[eng].append(('wait', semname, val))

    def _deps(self, eng, reads, writes):
        for k in reads:
            st = self.state.get(k)
            if st and st[0]:
                self._need(eng, st[0])
        skip_self = eng in ('act', 'dve') and RELAX_SELF_WAR
        for k in writes:
            st = self.state.get(k)
            if st:
                if st[0] and not (skip_self and st[0][0].startswith("cnt_" + eng)):
                    self._need(eng, st[0])
                for s, v in st[1].items():
                    if skip_self and s.startswith("cnt_" + eng):
                        continue
                    self._need(eng, (s, v))

    def _commit(self, tok, reads, writes):
        for k in reads:
            st = self.state.setdefault(k, [None, {}])
            st[1][tok[0]] = max(st[1].get(tok[0], 0), tok[1])
        for k in writes:
            self.state[k] = [tok, {}]

    def op(self, eng, fn, ins, outs):
        reads, writes = self._keys(ins), self._keys(outs)
        self._deps(eng, reads, writes)
        if self.ecnt[eng] >= 30000:
            self.epoch[eng] += 1
            self.ecnt[eng] = 0
            nm = "cnt_%s_%d" % (eng, self.epoch[eng])
            self.sems[nm] = self.nc.alloc_semaphore(nm)
            self.cur[eng] = nm
        self.ecnt[eng] += 1
        tok = (self.cur[eng], self.ecnt[eng])
        self.stream[eng].append(('inst', fn, self.cur[eng], 1))
        self._commit(tok, reads, writes)
        self.ninst += 1

    def dma(self, eng, out, in_, chain=False, **kw):
        sb = out if out.space == 'SB' else in_
        semname = "dma_" + sb.tensor.name
        if semname not in self.sems:
            self.sems[semname] = self.nc.alloc_semaphore(semname)
            self.dcount[semname] = 0
        reads, writes = self._keys([in_]), self._keys([out])
        self._deps(eng, reads, [] if chain else writes)
        if self.dcount[semname] > 0 and not chain:
            self._need(eng, (semname, 16 * self.dcount[semname]))
        self.dcount[semname] += 1
        tok = (semname, 16 * self.dcount[semname])
        self.stream[eng].append(('inst', lambda e: e.dma_start(out=out, in_=in_, **kw), semname, 16))
        self._commit(tok, reads, writes)
        self.ninst += 1

    def finish(self, eng='sp'):
        for semname, c in self.dcount.items():
            self._need(eng, (semname, 16 * c))
        for e in self.ENG:
            if self.ecnt[e] > 0 and e != eng:
                self._need(eng, (self.cur[e], self.ecnt[e]))

    def replay(self):
        nc = self.nc
        emap = {'pe': 'tensor', 'act': 'scalar', 'dve': 'vector', 'pool': 'gpsimd', 'sp': 'sync'}
        with nc.Block() as block:
            for en in self.ENG:
                items = self.stream[en]

                def body(engine, items=items):
                    for it in items:
                        if it[0] == 'wait':
                            engine.wait_ge(self.sems[it[1]], it[2])
                        else:
                            it[1](engine).then_inc(self.sems[it[2]], it[3])
                getattr(block, emap[en])(body)


PARAM_SPEC = [('ln1g', 16), ('ln2g', 16), ('convw', 96), ('gng', 1), ('alog', 8), ('dtb', 8), ('mu', 26),
              ('w0', 8), ('a0', 8), ('kk', 8), ('ka', 8), ('rk', 8), ('gnw', 8), ('gnb', 8), ('fconvw', 264), ('hflag', 1)]
CONST_SPEC = [('ident', 128), ('ones', 128), ('blk', 128),
              ('caus_p', 128), ('strict_p', 128), ('causT_p', 128), ('strictT_p', 128), ('seq_p', 128),
              ('caus_s', 128), ('strict_s', 128), ('causT_s', 128), ('strictT_s', 128), ('seq_s', 128),
              ('seg_p', 128), ('seg_s', 128), ('rowm_p', 16), ('rowm_s', 16)]


def _offsets(spec):
    o, d = 0, {}
    for n, w in spec:
        d[n] = (o, w)
        o += w
    return d, o


POFF, PW = _offsets(PARAM_SPEC)
COFF, CW = _offsets(CONST_SPEC)


def make_consts():
    c = np.zeros((128, CW), np.float32)

    def put(n, a):
        o, w = COFF[n]
        c[:, o:o + w] = a
    i = np.arange(128)
    put('ident', np.eye(128))
    put('ones', np.ones((128, 128)))
    blk = (i[:, None] // 64) == (i[None, :] // 64)
    put('blk', blk)
    for kind, L in (('p', 128), ('s', 8)):
        same = (i[:, None] // L) == (i[None, :] // L)
        caus = same & (i[:, None] >= i[None, :])
        strict = same & (i[:, None] > i[None, :])
        put('caus_' + kind, caus)
        put('strict_' + kind, strict)
        put('causT_' + kind, caus.T)
        put('strictT_' + kind, strict.T)
        put('seq_' + kind, same)
        put('seg_' + kind, np.broadcast_to((i % L != 0)[None, :], (128, 128)))
        rm = np.zeros((128, 16))
        nseq = 128 // L
        rm[i, (i // L)] = 1.0
        put('rowm_' + kind, rm[:, :16])
    return c


def make_params(inp):
    p = np.zeros((128, PW), np.float32)

    def put(n, a):
        o, w = POFF[n]
        p[:, o:o + w] = np.asarray(a, np.float32).reshape(128, w)

    def fm(v):
        v = np.asarray(v, np.float32).reshape(-1, 128)
        return v.T
    put('ln1g', fm(inp['ln1_g'][0]))
    put('ln2g', fm(inp['ln2_g'][0]))
    cw = np.asarray(inp['gdn_conv_w'][0], np.float32)
    put('convw', cw.reshape(4, 24, 128).transpose(2, 1, 0).reshape(128, 96))
    put('gng', np.asarray(inp['gdn_norm_g'][0]).reshape(128, 1))
    put('alog', np.broadcast_to(np.asarray(inp['gdn_a_log'][0])[None, :], (128, 8)))
    put('dtb', np.broadcast_to(np.asarray(inp['gdn_dt_bias'][0])[None, :], (128, 8)))
    put('mu', fm(inp['rwkv_mu'][0]))
    put('w0', fm(inp['rwkv_w0'][0]))
    put('a0', fm(inp['rwkv_a0'][0]))
    put('kk', fm(inp['rwkv_k_k'][0]))
    put('ka', fm(inp['rwkv_k_a'][0]))
    put('rk', fm(np.asarray(inp['rwkv_r_k'][0]).reshape(-1)))
    put('gnw', fm(inp['rwkv_gn_w'][0]))
    put('gnb', fm(inp['rwkv_gn_b'][0]))
    fw = np.asarray(inp['ffn_conv_w'][0], np.float32)
    put('fconvw', fw.reshape(3, 88, 128).transpose(2, 1, 0).reshape(128, 264))
    return p


class Builder:
    def __init__(self, NPRE, NMAIN):
        self.NPRE, self.NMAIN = NPRE, NMAIN
        import os
        self.STOP = os.environ.get('KSTOP', '')
        self.DBG = bool(os.environ.get('KDBG', ''))
        nc = self.nc = bass.Bass("TRN2", target_bir_lowering=False)
        self.em = Em(nc)
        din = lambda n, s: nc.dram_tensor(n, list(s), F32, kind="ExternalInput").ap()
        dout = lambda n, s: nc.dram_tensor(n, list(s), F32, kind="ExternalOutput").ap()
        self.xpre = din("xpre", (max(NPRE, 1) * 128, D))
        self.xmain = din("xmain", (NMAIN * 128, D))
        self.xs = din("xs", (128, D))
        self.sg = din("sg", (16, 8, 128, 128))
        self.sgc = din("sgc", (48, 3072))
        self.sr = din("sr", (16, 128, 8, 128))
        self.ssh = din("ssh", (16, 3328))
        self.sfc = din("sfc", (32, 11264))
        self.w_in = din("w_in", (D, INW))
        self.w_o = din("w_o", (D, D))
        self.w_up = din("w_up", (D, 2 * DFF))
        self.w_down = din("w_down", (DFF, D))
        self.params_d = din("params", (128, PW))
        self.consts_d = din("consts", (128, CW))
        self.fg_d = din("fg", (128, D))
        self.wab_d = din("wab", (128, 1024))
        self.gb_d = din("gb", (128, 1024))
        if self.DBG:
            self.dbg = nc.dram_tensor("dbg", [2, 128, D], F32, kind="ExternalOutput").ap()
        self.y_main = dout("y_main", (NMAIN * 128, D))
        self.y_s = dout("y_s", (128, D))
        self.o_gdn_p = dout("gdn_p", (8, 128, 128))
        self.o_gconv_p = dout("gconv_p", (3, 3072))
        self.o_rwkv_p = dout("rwkv_p", (128, 8, 128))
        self.o_shift_p = dout("shift_p", (1, 3328))
        self.o_ffn_p = dout("ffn_p", (2, 11264))
        self.o_gdn_s = dout("gdn_s", (16, 8, 128, 128))
        self.o_gconv_s = dout("gconv_s", (48, 3072))
        self.o_rwkv_s = dout("rwkv_s", (16, 128, 8, 128))
        self.o_shift_s = dout("shift_s", (16, 3328))
        self.o_ffn_s = dout("ffn_s", (32, 11264))

        sb = lambda n, s, dt=F32: nc.alloc_sbuf_tensor(n, list(s), dt)
        self.params = sb("params_sb", (128, PW))
        self.consts = sb("consts_sb", (128, CW))
        self.fg = sb("fg_sb", (128, D))
        self.wab = sb("wab_sb", (128, 1024))
        self.gb = sb("gb_sb", (128, 1024))
        self.xts = [sb("xt", (128, D)), sb("xtB", (128, D))]
        self.xt = self.xts[0]
        self.xnT2 = sb("xnT2", (128, 16, 256), BF16)
        self.xn = sb("xn", (128, D))
        self.xnT = sb("xnT", (128, 16, 128), BF16)
        self.NSLOT = 4
        self.ws = [sb("ws%d" % i, (128, 2048), BF16) for i in range(self.NSLOT)]
        dsc = lambda n, s: nc.dram_tensor(n, list(s), BF16, kind="Internal").ap()
        self.c_in = dsc("wsc_in", (D, INW))
        self.c_o = dsc("wsc_o", (D, D))
        self.c_up = dsc("wsc_up", (D, 2 * DFF))
        self.c_down = dsc("wsc_down", (DFF, D))
        self.wba = sb("wba", (128, 16, 16), BF16)
        self.xp = sb("xp", (128, 24 * 176))
        self.pp = sb("pp", (128, 26 * 144))
        self.qkv = sb("qkv", (128, 24, 128))
        self.scrA = sb("scrA", (128, 3328))
        self.zT = sb("zT", (128, 8, 128))
        self.mixedT = sb("mixedT", (128, 16, 128), BF16)
        self.xhist = sb("xhist", (128, 24, 3))
        self.phist = sb("phist", (128, 26, 1))
        self.fhist = sb("fhist", (128, 88, 2))
        self.Sg = sb("Sg", (128, 8, 128))
        self.Hb = sb("Hb", (128, 8, 128))
        self.NG = 13
        self.g = [sb("g%d" % i, (128, 1024)) for i in range(self.NG)]
        self.zeros = sb("zeros", (128, 512))
        self.cols = sb("cols", (128, 256))
        self.st48 = self.scrA
        self.ps = [nc.alloc_psum_tensor("ps%d" % i, [128, 2, 512], F32) for i in range(4)]
        self.wslot = 0
        self.pool_eng = 'pool'
        self._unused_hist = True

    def P(self, n):
        o, w = POFF[n]
        return self.params[:, o:o + w]

    def C(self, n):
        o, w = COFF[n]
        return self.consts[:, o:o + w]

    def tt(self, out, a, b, op, eng='dve'):
        self.em.op(eng, lambda e: e.tensor_tensor(out=out, in0=a, in1=b, op=op), [a, b], [out])

    def ts(self, out, a, s1, op0, s2=None, op1=None, eng='dve', accum=None):
        kw = dict(out=out, in0=a, scalar1=s1, scalar2=s2, op0=op0)
        if op1 is not None:
            kw['op1'] = op1
        if accum is not None:
            kw['accum_out'] = accum
        self.em.op(eng, lambda e: e.tensor_scalar(**kw), [a, s1, s2], [out, accum])

    def stt(self, out, a, s, b, op0, op1):
        self.em.op('dve', lambda e: e.scalar_tensor_tensor(out=out, in0=a, scalar=s, in1=b, op0=op0, op1=op1),
                   [a, s, b], [out])

    def act(self, out, in_, func, scale=1.0, bias=0.0, accum=None):
        kw = dict(out=out, in_=in_, func=func, scale=scale, bias=bias)
        if accum is not None:
            kw['accum_out'] = accum
        self.em.op('act', lambda e: e.activation(**kw), [in_, scale, bias], [out, accum])

    def cp(self, out, in_, eng='dve'):
        if eng == 'act':
            self.act(out, in_, AF.Copy)
        else:
            self.em.op(eng, lambda e: e.tensor_copy(out=out, in_=in_), [in_], [out])

    def rcp(self, out, in_):
        n = 1
        for d in out.shape[1:]:
            n *= d
        self.em.op('dve', lambda e: e.reciprocal(out=out, in_=in_), [in_], [out])

    def memset(self, ap, val, eng='dve'):
        self.em.op(eng, lambda e: e.memset(ap, val), [], [ap])

    def mm(self, out, lhsT, rhs, start=True, stop=True, skip=False):
        if FP32R and lhsT.dtype == F32:
            lhsT, rhs = lhsT.bitcast(mybir.dt.float32r), rhs.bitcast(mybir.dt.float32r)
        self.em.op('pe', lambda e: e.matmul(out=out, lhsT=lhsT, rhs=rhs, start=start, stop=stop,
                                            skip_group_check=skip), [lhsT, rhs], [out])

    def tr(self, out, in_):
        p = in_.shape[0]
        idn = self.C('ident')[0:p, 0:p]
        self.em.op('pe', lambda e: e.transpose(out=out, in_=in_, identity=idn), [in_, idn], [out])

    def scan(self, out, d0, d1):
        self.em.op('dve', lambda e: e.tensor_tensor_scan(out=out, data0=d0, data1=d1, initial=0.0,
                                                         op0=ALU.mult, op1=ALU.add), [d0, d1], [out])

    def reduce(self, out, in_, op=ALU.add):
        self.em.op('dve', lambda e: e.tensor_reduce(out=out, in_=in_, axis=AX.X, op=op), [in_], [out])

    def dma(self, out, in_, eng='sp', **kw):
        self.em.dma(eng, out, in_, **kw)

    def rsqrt(self, out, in_, scale, eps):
        n = 1
        for d in out.shape[1:]:
            n *= d
        if n >= 256 and FAST_RSQRT:
            self.act(out, in_, AF.Ln, scale=scale, bias=eps)
            self.act(out, out, AF.Exp, scale=-0.5)
        else:
            self.act(out, in_, AF.Sqrt, scale=scale, bias=eps)
            self.rcp(out, out)

    def bc(self, ap, shape):
        return ap.broadcast_to(list(shape))

    def convert_weights(self):
        for (src, dst) in ((self.w_in, self.c_in), (self.w_o, self.c_o), (self.w_up, self.c_up),
                           (self.w_down, self.c_down)):
            R = src.shape[0]
            for r in range(0, R, 256):
                self.em.dma('pool', dst[r:r + 256, :], src[r:r + 256, :], max_dma_last_dim=4096)

    def wslot_load(self, pieces):
        t = self.ws[self.wslot % self.NSLOT]
        self.wslot += 1
        for i, (o, ap) in enumerate(pieces):
            self.dma(t[:, o:o + ap.shape[1]], ap, eng='sp', chain=(i > 0))
        return t

    def acc_region(self, pair, j):
        return self.ps[pair * 2 + j // 8][:, (j % 8) // 4, (j % 4) * 128:(j % 4) * 128 + 128]

    def stream_up(self, wsc, ranges, pair):
        zz = self.zeros
        used = sorted(set((so // 128 + i) // 4 for (c0, n, so) in ranges for i in range(n // 128)))
        for bk in used:
            self.mm(self.ps[pair * 2 + bk // 2][:, bk % 2, :], zz[:, 0:128], zz[:, 0:512], start=True, stop=False,
                    skip=True)
        for k in range(16):
            t = self.wslot_load([(so, wsc[k * 128:(k + 1) * 128, c0:c0 + n]) for (c0, n, so) in ranges])
            for (c0, n, so) in ranges:
                for i in range(n // 128):
                    j = so // 128 + i
                    self.mm(self.acc_region(pair, j), t[:, so + i * 128:so + (i + 1) * 128], self.xnT[:, k, :],
                            start=False, stop=(k == 15), skip=True)

    def stream_down(self, wsc, nk, lhs, pair):
        for k in range(nk):
            t = self.wslot_load([(0, wsc[k * 128:(k + 1) * 128, :])])
            for n in range(4):
                self.mm(self.ps[pair * 2 + n // 2][:, n % 2, :], lhs[:, k, :], t[:, n * 512:(n + 1) * 512],
                        start=(k == 0), stop=(k == nk - 1))

    def norm_T(self, gname, dst=None):
        dst = self.xnT if dst is None else dst
        ps = self.ps
        c = self.cols
        self.act(self.xn[:], self.xt[:], AF.Square, accum=c[:, 0:1])
        self.rsqrt(c[:, 1:2], c[:, 0:1], 1.0 / D, 1e-6)
        self.ts(self.xn[:], self.xt[:], c[:, 1:2], ALU.mult)
        for half in range(2):
            for q in range(4):
                for j in range(2):
                    k = half * 8 + q * 2 + j
                    self.tr(ps[q][:, j, 0:128], self.xn[:, k * 128:(k + 1) * 128])
            for q in range(4):
                k0 = half * 8 + q * 2
                self.tt(dst[:, k0:k0 + 2, :], ps[q][:, :, 0:128],
                        self.bc(self.P(gname)[:, k0:k0 + 2].unsqueeze(2), (128, 2, 128)), ALU.mult)

    def solve(self, A0, B0, A1, B1, X, n, nlev, Xb=None, xparts=None):
        ps = self.ps
        A, B, An, Bn = A0, B0, A1, B1
        xparts = xparts or []
        if Xb is None:
            Xb = X
        for (xa, xb) in xparts:
            self.cp(xb, xa, eng='act')
        shadow = len(xparts) > 0
        for j in range(nlev):
            if n == 256:
                for h in range(8):
                    self.mm(ps[h // 4][:, (h % 4) // 2, (h % 2) * 256:(h % 2) * 256 + 256], B[:, h, :], Xb[:, h, :])
                for q in range(2):
                    pq = ps[q][:].rearrange("p a b -> p (a b)")
                    xq = X[:, 4 * q:4 * q + 4, :].rearrange("p a b -> p (a b)")
                    if j == 0:
                        self.stt(xq, pq, -1.0, xq, ALU.mult, ALU.add)
                    else:
                        self.tt(xq, pq, xq, ALU.add)
            else:
                bank = j % 2
                for h in range(8):
                    self.mm(ps[0][:, bank, h * 64:(h + 1) * 64], B[:, h, :], Xb[:, h, :])
                pf = ps[0][:, bank, :]
                xf = X.rearrange("p a b -> p (a b)")
                if shadow and j < nlev - 1:
                    xbf = Xb.rearrange("p a b -> p (a b)")
                    if j == 0:
                        self.stt(xbf, pf, -1.0, xf, ALU.mult, ALU.add)
                    else:
                        self.tt(xbf, pf, xf, ALU.add)
                if j == 0:
                    self.stt(xf, pf, -1.0, xf, ALU.mult, ALU.add)
                else:
                    self.tt(xf, pf, xf, ALU.add)
            if j < nlev - 1:
                for h in range(8):
                    self.mm(ps[2][:, h // 4, (h % 4) * 128:(h % 4) * 128 + 128], B[:, h, :], A[:, h, :])
                for h in range(8):
                    self.mm(ps[3][:, h // 4, (h % 4) * 128:(h % 4) * 128 + 128], A[:, h, :], B[:, h, :])
                self.cp(An, ps[2][:].rearrange("p a (b c) -> p (a b) c", b=4), eng='pool' if False else 'act')
                self.cp(Bn, ps[3][:].rearrange("p a (b c) -> p (a b) c", b=4), eng='dve')
                A, B, An, Bn = An, Bn, A, B

    def rowbc(self, dst_ps, src, n, scratch):
        self.tt(scratch, self.bc(src.unsqueeze(2), (128, n, 128)),
                self.bc(self.C('ident').unsqueeze(1), (128, n, 128)), ALU.mult)
        flat = scratch.rearrange("p a b -> p (a b)")
        for q in range((n * 128 + 511) // 512):
            w = min(512, n * 128 - q * 512)
            self.mm(dst_ps[:, q, 0:w], self.C('ones'), flat[:, q * 512:q * 512 + w])

    def tile(self, kind, x_src, y_dst, full, first, last, sample_state, pair_slot=None):
        nseq, L = (1, 128) if kind == 'p' else (16, 8)
        nlev = 7 if kind == 'p' else 3
        ps, g, c = self.ps, self.g, self.cols
        Lh3, Lh1, Lh2 = L + 3, L + 1, L + 2
        xpv = self.xp[:, 0:24 * nseq * Lh3].rearrange("p (c s t) -> p c s t", c=24, s=nseq)
        ppv = self.pp[:, 0:26 * nseq * Lh1].rearrange("p (c s t) -> p c s t", c=26, s=nseq)
        M = lambda n: self.C(n + '_' + kind)

        self.dma(self.xt[:], x_src)
        self.norm_T('ln1g')

        if self.STOP == 'A':
            return
        if kind == 'p':
            if first:
                self.memset(self.xhist[:], 0.0)
                self.memset(self.phist[:], 0.0)
                self.memset(self.Sg[:], 0.0)
                self.memset(self.Hb[:], 0.0)
                self.memset(self.fhist[:], 0.0)
            self.cp(xpv[:, :, 0, 0:3], self.xhist[:])
            self.cp(ppv[:, :, 0, 0:1], self.phist[:])
        else:
            self.dma(self.st48[0:48, 0:3072], self.sgc)
            for cg in range(3):
                for j in range(8):
                    cc = cg * 8 + j
                    self.tr(ps[j // 4][:, (j % 4) // 2, (j % 2) * 48:(j % 2) * 48 + 48],
                            self.st48[0:48, cc * 128:(cc + 1) * 128])
                for j in range(8):
                    cc = cg * 8 + j
                    self.cp(xpv[:, cc, :, 0:3],
                            ps[j // 4][:, (j % 4) // 2, (j % 2) * 48:(j % 2) * 48 + 48].rearrange("p (s t) -> p s t", s=16))
            self.dma(self.st48[0:16, 0:3328], self.ssh)
            for cg in range(4):
                for j in range(8):
                    cc = cg * 8 + j
                    if cc < 26:
                        self.tr(ps[j // 4][:, (j % 4) // 2, (j % 2) * 16:(j % 2) * 16 + 16],
                                self.st48[0:16, cc * 128:(cc + 1) * 128])
                for j in range(8):
                    cc = cg * 8 + j
                    if cc < 26:
                        self.cp(ppv[:, cc, :, 0:1],
                                ps[j // 4][:, (j % 4) // 2, (j % 2) * 16:(j % 2) * 16 + 16].unsqueeze(2))

        if self.STOP == 'A2':
            return
        p8v = lambda t: t[:].rearrange("p a (b t) -> p (a b) t", b=4)
        s4 = lambda a_: a_.rearrange("p c (s t) -> p c s t", s=nseq)
        RW = 4112
        if full:
            self.stream_up(self.c_in, [(0, 2048, 0)], 0)
            for q in range(2):
                self.cp(xpv[:, 8 * q:8 * q + 8, :, 3:3 + L], s4(p8v(ps[q])), eng='act' if q == 0 else 'dve')
            self.stream_up(self.c_in, [(2048, 2048, 0)], 1)
            self.cp(xpv[:, 16:24, :, 3:3 + L], s4(p8v(ps[2])), eng='act')
            self.cp(self.zT[:], p8v(ps[3]), eng='dve')
            self.stream_up(self.c_in, [(RW, 2048, 0)], 0)
            for q in range(2):
                self.cp(ppv[:, 8 * q:8 * q + 8, :, 1:1 + L], s4(p8v(ps[q])), eng='act' if q == 0 else 'dve')
            self.stream_up(self.c_in, [(RW + 2048, 1280, 0)], 1)
            self.cp(ppv[:, 16:24, :, 1:1 + L], s4(p8v(ps[2])), eng='act')
            self.cp(ppv[:, 24:26, :, 1:1 + L], s4(p8v(ps[3])[:, 0:2, :]), eng='dve')
        else:
            self.stream_up(self.c_in, [(1024, 2048, 0)], 0)
            for q in range(2):
                self.cp(xpv[:, 8 + 8 * q:16 + 8 * q, :, 3:3 + L], s4(p8v(ps[q])), eng='act' if q == 0 else 'dve')
            self.stream_up(self.c_in, [(RW + 1024, 2048, 0)], 1)
            for q in range(2):
                self.cp(ppv[:, 8 + 8 * q:16 + 8 * q, :, 1:1 + L], s4(p8v(ps[2 + q])), eng='act' if q == 0 else 'dve')
            self.stream_up(self.c_in, [(RW + 3072, 128, 0)], 0)
            self.cp(ppv[:, 24:25, :, 1:1 + L], s4(p8v(ps[0])[:, 0:1, :]), eng='act')
        self.dma(self.wba[:], self.c_in[:, 4096:4112].rearrange("(k p) c -> p k c", p=128), eng='sp')
        for k in range(16):
            self.mm(ps[0][:, 0, 0:16], self.xnT[:, k, :], self.wba[:, k, :], start=(k == 0), stop=(k == 15))
        ba = c[:, 8:24]
        self.cp(ba, ps[0][:, 0, 0:16])

        if self.STOP == 'B':
            return
        cw = self.P('convw').rearrange("p (c i) -> p c i", i=4)
        qv = self.qkv[:].rearrange("p c (s t) -> p c s t", s=nseq)
        tmp = self.scrA[:, 0:3072].rearrange("p (c s t) -> p c s t", c=24, s=nseq)
        wsh = (128, 24, nseq, L)
        sq = self.scrA[:, 0:2048].rearrange("p (c t) -> p c t", c=16)
        sqf = self.scrA[:, 0:2048]
        rn = g[0]
        qn, kn, va = self.qkv[:, 0:8, :], self.qkv[:, 8:16, :], self.qkv[:, 16:24, :]
        bet, gcol, Gc, gam, bgc, Gl, ekl, tmpc = (c[:, 32:40], c[:, 40:48], c[:, 48:56], c[:, 56:64], c[:, 64:72],
                                                  c[:, 72:80], c[:, 80:88], c[:, 88:96])
        scr8 = g[1][:].rearrange("p (h t) -> p h t", h=8)
        tdf = g[2][:].rearrange("p (h t) -> p h t", h=8)
        Dm = g[3][:].rearrange("p (h t) -> p h t", h=8)
        DTm = g[4][:].rearrange("p (h t) -> p h t", h=8)

        def chain_conv():
            if kind == 'p':
                self.cp(self.xhist[:], xpv[:, :, 0, L:L + 3])
                yield
            self.tt(qv, xpv[:, :, :, 3:3 + L], self.bc(cw[:, :, 3:4].unsqueeze(3), wsh), ALU.mult)
            yield
            for i in range(3):
                self.tt(tmp, xpv[:, :, :, i:i + L], self.bc(cw[:, :, i:i + 1].unsqueeze(3), wsh), ALU.mult,
                        eng=self.pool_eng)
                yield
                self.tt(qv, qv, tmp, ALU.add)
                yield
            self.act(self.qkv[:], self.qkv[:], AF.Silu)
            yield
            self.tt(sq, self.qkv[:, 0:16, :], self.qkv[:, 0:16, :], ALU.mult)
            yield
            for q in range(4):
                self.mm(ps[q // 2 + 2][:, q % 2, :], self.C('ones'), sqf[:, q * 512:(q + 1) * 512])
            yield
            for part in range(2):
                rv = rn[:].rearrange("p (c t) -> p c t", c=8)
                self.rsqrt(rv, ps[2 + part][:].rearrange("p a (b t) -> p (a b) t", b=4), 1.0, 1e-12)
                yield
                if part == 0:
                    self.stt(self.qkv[:, 0:8, :], self.qkv[:, 0:8, :], 128.0 ** -0.5, rv, ALU.mult, ALU.mult)
                else:
                    self.tt(self.qkv[:, 8:16, :], self.qkv[:, 8:16, :], rv, ALU.mult)
                yield

        def chain_cols():
            self.act(bet, ba[:, 0:8], AF.Sigmoid)
            self.tt(tmpc, ba[:, 8:16], self.P('dtb'), ALU.add)
            yield
            self.act(tmpc, tmpc, AF.Exp)
            self.act(tmpc, tmpc, AF.Ln, bias=1.0)
            self.act(gcol, self.P('alog'), AF.Exp)
            yield
            self.stt(gcol, gcol, -1.0, tmpc, ALU.mult, ALU.mult)
            self.mm(ps[0][:, 0, 0:8], M('causT'), gcol)
            self.mm(ps[0][:, 0, 8:16], M('seq'), gcol)
            yield
            self.cp(Gc, ps[0][:, 0, 0:8])
            self.cp(Gl, ps[0][:, 0, 8:16])
            yield
            self.act(gam, Gc, AF.Exp)
            self.tt(bgc, bet, gam, ALU.mult)
            self.tt(ekl, Gl, Gc, ALU.subtract)
            self.act(ekl, ekl, AF.Exp)
            yield
            self.rowbc(ps[0], Gc, 8, scr8)
            yield
            self.tt(tdf, ps[0][:].rearrange("p a (b t) -> p (a b) t", b=4), self.bc(Gc.unsqueeze(2), (128, 8, 128)),
                    ALU.subtract)
            yield
            self.ts(Dm, tdf, 0.0, ALU.max, -1.0, ALU.mult)
            self.ts(DTm, tdf, 0.0, ALU.min)
            yield
            self.act(Dm, Dm, AF.Exp)
            self.act(DTm, DTm, AF.Exp)
            yield
            self.tt(Dm, Dm, self.bc(M('caus').unsqueeze(1), (128, 8, 128)), ALU.mult)
            self.tt(DTm, DTm, self.bc(M('causT').unsqueeze(1), (128, 8, 128)), ALU.mult)
            yield

        gens = [chain_cols(), chain_conv()]
        while gens:
            for gen_ in list(gens):
                try:
                    next(gen_)
                except StopIteration:
                    gens.remove(gen_)
        if self.STOP == 'C2':
            return
        for h in range(8):
            self.mm(ps[1][:, h // 4, (h % 4) * 128:(h % 4) * 128 + 128], kn[:, h, :], kn[:, h, :])
        A0 = g[5][:].rearrange("p (h t) -> p h t", h=8)
        B0 = g[6][:].rearrange("p (h t) -> p h t", h=8)
        A1 = g[7][:].rearrange("p (h t) -> p h t", h=8)
        B1 = g[8][:].rearrange("p (h t) -> p h t", h=8)
        self.tt(A0, ps[1][:].rearrange("p a (b t) -> p (a b) t", b=4), Dm, ALU.mult)
        self.tt(A0, A0, self.bc(bet.unsqueeze(2), (128, 8, 128)), ALU.mult)
        self.tt(A0, A0, self.bc(M('strict').unsqueeze(1), (128, 8, 128)), ALU.mult)
        for h in range(8):
            self.tr(ps[2][:, h // 4, (h % 4) * 128:(h % 4) * 128 + 128], A0[:, h, :])
        self.cp(B0, ps[2][:].rearrange("p a (b t) -> p (a b) t", b=4), eng='act')
        X = g[9:11]
        Xv = [X[0][:].rearrange("p (h t) -> p h t", h=4), X[1][:].rearrange("p (h t) -> p h t", h=4)]
        ktm = g[11][:].rearrange("p (h t) -> p h t", h=8)
        for h in range(8):
            self.tr(ps[1][:, h // 4, (h % 4) * 128:(h % 4) * 128 + 128], kn[:, h, :])
        for h in range(8):
            self.tr(ps[3][:, h // 4, (h % 4) * 128:(h % 4) * 128 + 128], va[:, h, :])
        for q in range(2):
            self.tt(Xv[q][:, :, 0:128], ps[1][:, q, :].rearrange("p (h t) -> p h t", h=4),
                    self.bc(bgc[:, 4 * q:4 * q + 4].unsqueeze(2), (128, 4, 128)), ALU.mult)
            self.tt(Xv[q][:, :, 128:256], ps[3][:, q, :].rearrange("p (h t) -> p h t", h=4),
                    self.bc(bet[:, 4 * q:4 * q + 4].unsqueeze(2), (128, 4, 128)), ALU.mult)
        self.tt(ktm, ps[1][:].rearrange("p a (b t) -> p (a b) t", b=4), self.bc(ekl.unsqueeze(2), (128, 8, 128)),
                ALU.mult)

        class XW:
            def __getitem__(s2, idx):
                _, hs, cs = idx
                if isinstance(hs, int):
                    return Xv[hs // 4][:, hs % 4, cs]
                q = hs.start // 4
                return Xv[q][:, :, cs]
        XX = XW()
        self.solve(A0, B0, A1, B1, XX, 256, nlev)
        if self.STOP == 'D':
            return
        so_ = (full == 0)
        qkT = g[5][:].rearrange("p (h t) -> p h t", h=8)
        if not so_:
            for h in range(8):
                self.mm(ps[2][:, h // 4, (h % 4) * 128:(h % 4) * 128 + 128], kn[:, h, :], qn[:, h, :])
            self.tt(qkT, ps[2][:].rearrange("p a (b t) -> p (a b) t", b=4), DTm, ALU.mult)
        WT = g[6][:].rearrange("p (h t) -> p h t", h=8)
        for h in range(8):
            self.tr(ps[3][:, h // 4, (h % 4) * 128:(h % 4) * 128 + 128], XX[:, h, 0:128])
        self.cp(WT, ps[3][:].rearrange("p a (b t) -> p (a b) t", b=4), eng='act')
        qgT = g[7][:].rearrange("p (h t) -> p h t", h=8)
        if not so_:
            self.rowbc(ps[0], gam, 8, scr8)
            self.tt(qgT, ps[0][:].rearrange("p a (b t) -> p (a b) t", b=4), qn, ALU.mult)
        u = g[8][:].rearrange("p (h t) -> p h t", h=8)
        pws = ps[0][:].rearrange("p a (b t) -> p (a b) t", b=4)
        po = ps[1][:].rearrange("p a (b t) -> p (a b) t", b=4)
        zz = self.zeros
        if not so_:
            for a in range(2):
                self.mm(ps[1][:, a, :], zz[:, 0:128], zz[:, 0:512], start=True, stop=False, skip=True)
        if kind == 'p':
            for h in range(8):
                self.mm(pws[:, h, :], WT[:, h, :], self.Sg[:, h, :])
            if not so_:
                for h in range(8):
                    self.mm(po[:, h, :], qgT[:, h, :], self.Sg[:, h, :], start=False, stop=False, skip=True)
        else:
            WTm = g[1][:].rearrange("p (h t) -> p h t", h=8)
            qgTm = g[2][:].rearrange("p (h t) -> p h t", h=8)
            self.memset(WTm, 0.0)
            self.memset(qgTm, 0.0)
            for a in range(2):
                self.mm(ps[0][:, a, :], zz[:, 0:128], zz[:, 0:512], start=True, stop=False, skip=True)
            for s in range(16):
                Sb = (self.Sg, self.Hb)[s % 2]
                self.dma(Sb[:], self.sg[s].rearrange("h k v -> k h v"))
                if s > 0:
                    self.memset(WTm[:, :, 8 * s - 8:8 * s], 0.0)
                    self.memset(qgTm[:, :, 8 * s - 8:8 * s], 0.0)
                self.cp(WTm[:, :, 8 * s:8 * s + 8], WT[:, :, 8 * s:8 * s + 8])
                self.cp(qgTm[:, :, 8 * s:8 * s + 8], qgT[:, :, 8 * s:8 * s + 8])
                for h in range(8):
                    self.mm(pws[:, h, :], WTm[:, h, :], Sb[:, h, :], start=False, stop=False, skip=True)
                for h in range(8):
                    self.mm(po[:, h, :], qgTm[:, h, :], Sb[:, h, :], start=False, stop=False, skip=True)
        for q in range(2):
            self.tt(u[:, 4 * q:4 * q + 4, :], Xv[q][:, :, 128:256], pws[:, 4 * q:4 * q + 4, :], ALU.subtract)
        if not so_:
            for h in range(8):
                self.mm(po[:, h, :], qkT[:, h, :], u[:, h, :], start=False, stop=True, skip=True)
            o = g[5][:].rearrange("p (h t) -> p h t", h=8)
            self.cp(o, po, eng='act')
            osq = g[6][:].rearrange("p (h t) -> p h t", h=8)
            self.tt(osq, o, o, ALU.mult)
            self.reduce(tmpc, osq)
            self.rsqrt(tmpc, tmpc, 1.0 / 128, 1e-6)
            self.tt(o, o, self.bc(tmpc.unsqueeze(2), (128, 8, 128)), ALU.mult)
            for h in range(8):
                self.tr(ps[2][:, h // 4, (h % 4) * 128:(h % 4) * 128 + 128], o[:, h, :])
            self.act(self.zT[:], self.zT[:], AF.Silu)
            oT = g[6][:].rearrange("p (h t) -> p h t", h=8)
            self.ts(oT, ps[2][:].rearrange("p a (b t) -> p (a b) t", b=4), self.P('gng')[:, 0:1], ALU.mult)
            self.tt(self.mixedT[:, 0:8, :], oT, self.zT[:], ALU.mult)
        pS = ps[3][:].rearrange("p a (b t) -> p (a b) t", b=4)
        if kind == 'p':
            for h in range(8):
                self.mm(pS[:, h, :], ktm[:, h, :], u[:, h, :])
            self.act(tmpc, Gl, AF.Exp)
            self.tt(self.Sg[:], self.Sg[:], self.bc(tmpc.unsqueeze(2), (128, 8, 128)), ALU.mult)
            self.tt(self.Sg[:], self.Sg[:], pS, ALU.add)
            if last:
                self.dma(self.o_gdn_p.rearrange("h k v -> k h v"), self.Sg[:])
        else:
            egl = c[:, 96:104]
            self.act(egl, Gl, AF.Exp)
            glall = c[:, 112:240].rearrange("p (h s) -> p h s", h=8)
            gsc = g[3][:, 0:128].rearrange("p (h s) -> p h s", h=8)
            self.tt(gsc, self.bc(egl.unsqueeze(2), (128, 8, 16)), self.bc(M('rowm').unsqueeze(1), (128, 8, 16)), ALU.mult)
            self.mm(ps[0][:, 0, 0:128], self.C('ones'), g[3][:, 0:128])
            self.ts(glall, ps[0][:, 0, 0:128].rearrange("p (h s) -> p h s", h=8), 1.0 / 8, ALU.mult)
            v8_ = lambda t: t[:].rearrange("p (h t) -> p h t", h=8)
            for s in range(16):
                b = s % 2
                Sb = (self.Sg, self.Hb)[b]
                ktmm = v8_((g[1], g[4])[b])
                outb = v8_((g[0], g[2])[b])
                pSb = (ps[3], ps[2])[b][:].rearrange("p a (b t) -> p (a b) t", b=4)
                self.dma(Sb[:], self.sg[s].rearrange("h k v -> k h v"))
                self.act(ktmm, ktm, AF.Copy, scale=M('rowm')[:, s:s + 1])
                for h in range(8):
                    self.mm(pSb[:, h, :], ktmm[:, h, :], u[:, h, :])
                self.tt(outb, Sb[:], self.bc(glall[:, :, s:s + 1], (128, 8, 128)), ALU.mult)
                self.tt(outb, outb, pSb, ALU.add)
                self.dma(self.o_gdn_s[s].rearrange("h k v -> k h v"), outb)
        if kind == 's' or last:
            nr = 3 * nseq
            cst = self.qkv[:].rearrange("p c t -> p (c t)")[:, 0:24 * nr].rearrange("p (c s r) -> p c s r", c=24, s=nseq)
            self.cp(cst, xpv[:, :, :, L:L + 3])
            cst2 = self.qkv[:].rearrange("p c t -> p (c t)")[:, 0:24 * nr].rearrange("p (c r) -> p c r", c=24)
            for cg in range(3):
                for j in range(8):
                    cc = cg * 8 + j
                    self.tr(ps[j // 4][0:nr, (j % 4) // 2, (j % 2) * 128:(j % 2) * 128 + 128],
                            cst2[:, cc, :])
                for j in range(8):
                    cc = cg * 8 + j
                    self.cp(self.st48[0:nr, cc * 128:(cc + 1) * 128],
                            ps[j // 4][0:nr, (j % 4) // 2, (j % 2) * 128:(j % 2) * 128 + 128])
            self.dma(self.o_gconv_p if kind == 'p' else self.o_gconv_s, self.st48[0:nr, 0:3072])

        if self.STOP == 'E':
            return
        self.rwkv(kind, nseq, L, nlev, ppv, last, so_=(full == 0))

        if not full:
            return
        if self.DBG:
            self.cp(self.xn[:], self.mixedT[:].rearrange("p a b -> p (a b)"))
            self.dma(self.dbg[0 if kind == 'p' else 1], self.xn[:])
        if self.STOP == 'F':
            return
        self.stream_down(self.c_o, 16, self.mixedT, 0)
        for q in range(2):
            self.tt(self.xt[:, q * 1024:(q + 1) * 1024], self.xt[:, q * 1024:(q + 1) * 1024],
                    ps[q][:].rearrange("p a b -> p (a b)"), ALU.add)
        if self.STOP == 'G':
            return
        if pair_slot is not None:
            self.norm_T('ln2g', dst=self.xnT2[:, :, pair_slot * 128:(pair_slot + 1) * 128])
            return
        self.ffn(kind, nseq, L, last, hist_only=(full == 1))
        if full == 1:
            self.ts(self.fhist[:], self.fhist[:], self.P('hflag')[:, 0:1], ALU.mult)
            return
        self.final(y_dst)

    def final(self, y_dst):
        c = self.cols
        self.act(self.xn[:], self.xt[:], AF.Square, accum=c[:, 0:1])
        self.rsqrt(c[:, 1:2], c[:, 0:1], 1.0 / D, 1e-6)
        self.stt(self.xn[:], self.xt[:], c[:, 1:2], self.fg[:], ALU.mult, ALU.mult)
        self.dma(y_dst, self.xn[:])

    def rwkv(self, kind, nseq, L, nlev, ppv, last, so_=False):
        ps, g, c = self.ps, self.g, self.cols
        M = lambda n: self.C(n + '_' + kind)
        T = 128
        v8 = lambda t: t[:].rearrange("p (h t) -> p h t", h=8)
        if kind == 'p':
            self.cp(self.phist[:], ppv[:, :, 0, L:L + 1])
        if kind == 's' or last:
            for cg in range(4):
                for j in range(8):
                    cc = cg * 8 + j
                    if cc < 26:
                        self.tr(ps[j // 4][0:nseq, (j % 4) // 2, (j % 2) * 128:(j % 2) * 128 + 128],
                                ppv[:, cc, :, L])
                for j in range(8):
                    cc = cg * 8 + j
                    if cc < 26:
                        self.cp(self.st48[0:nseq, cc * 128:(cc + 1) * 128],
                                ps[j // 4][0:nseq, (j % 4) // 2, (j % 2) * 128:(j % 2) * 128 + 128])
            self.dma(self.o_shift_p if kind == 'p' else self.o_shift_s, self.st48[0:nseq, 0:3328])
        xs = self.scrA[:, 0:26 * 128].rearrange("p (c s t) -> p c s t", c=26, s=nseq)
        xsf = self.scrA[:, 0:26 * 128].rearrange("p (c t) -> p c t", c=26)
        self.tt(xs, ppv[:, :, :, 0:L], ppv[:, :, :, 1:1 + L], ALU.subtract)
        self.tt(xsf, xsf, self.bc(self.P('mu').unsqueeze(2), (128, 26, 128)), ALU.mult)
        self.tt(xs, xs, ppv[:, :, :, 1:1 + L], ALU.add)
        r, k, v = xsf[:, 0:8, :], xsf[:, 8:16, :], xsf[:, 16:24, :]
        twd = g[0]
        self.act(twd[0:64, 0:128], xsf[0:64, 24, :], AF.Tanh)
        for ch in range(8):
            self.mm(ps[0][:, ch // 4, (ch % 4) * 128:(ch % 4) * 128 + 128], self.wab[0:64, ch * 128:(ch + 1) * 128],
                    twd[0:64, 0:128])
        for ch in range(8):
            self.mm(ps[1][:, ch // 4, (ch % 4) * 128:(ch % 4) * 128 + 128], self.wab[64:128, ch * 128:(ch + 1) * 128],
                    xsf[64:128, 24, :])
        p8 = lambda t: t[:].rearrange("p a (b t) -> p (a b) t", b=4)
        ew = v8(g[1])
        self.tt(ew, p8(ps[0]), self.bc(self.P('w0').unsqueeze(2), (128, 8, 128)), ALU.add)
        self.act(ew, ew, AF.Exp, scale=-1.0)
        self.act(ew, ew, AF.Ln, bias=1.0)
        self.act(ew, ew, AF.Exp, scale=-1.0, bias=-0.5)
        av = v8(g[2])
        self.tt(av, p8(ps[1]), self.bc(self.P('a0').unsqueeze(2), (128, 8, 128)), ALU.add)
        self.act(av, av, AF.Sigmoid)
        sgd = g[0]
        gate = v8(g[3])
        if not so_:
            self.act(sgd[:, 128:256], xsf[:, 25, :], AF.Sigmoid)
            for ch in range(8):
                self.mm(ps[2][:, ch // 4, (ch % 4) * 128:(ch % 4) * 128 + 128], self.gb[:, ch * 128:(ch + 1) * 128],
                        sgd[:, 128:256])
            self.cp(gate, p8(ps[2]), eng='act')
        kkv = v8(g[4])
        self.tt(kkv, k, self.bc(self.P('kk').unsqueeze(2), (128, 8, 128)), ALU.mult)
        sq = v8(g[5])
        self.tt(sq, kkv, kkv, ALU.mult)
        for q in range(2):
            self.mm(ps[3][:, q, :], self.C('blk'), g[5][:, q * 512:(q + 1) * 512])
        rn = v8(g[5])
        self.rsqrt(rn, p8(ps[3]), 1.0, 1e-12)
        self.tt(kkv, kkv, rn, ALU.mult)
        k2 = v8(g[5])
        self.ts(k2, av, -1.0, ALU.add)
        self.tt(k2, k2, self.bc(self.P('ka').unsqueeze(2), (128, 8, 128)), ALU.mult)
        self.stt(k2, k2, 1.0, k, ALU.add, ALU.mult)
        bon = v8(g[6])
        if not so_:
            self.tt(bon, r, k2, ALU.mult)
            self.tt(bon, bon, self.bc(self.P('rk').unsqueeze(2), (128, 8, 128)), ALU.mult)
            for q in range(2):
                self.mm(ps[0][:, q, :], self.C('blk'), g[6][:, q * 512:(q + 1) * 512])
            self.tt(bon, p8(ps[0]), v, ALU.mult)
        if self.STOP == 'E1':
            return
        cs = v8(g[7])
        for ch in range(8):
            self.scan(cs[:, ch, :], M('seg'), ew[:, ch, :])
        at = v8(g[8])
        rt = v8(g[9])
        bt = v8(g[10])
        kt = v8(g[11])
        e1 = v8(g[12])
        self.tt(e1, cs, ew, ALU.subtract)
        self.act(e1, e1, AF.Exp, scale=-1.0)
        self.tt(at, kkv, e1, ALU.mult)
        self.act(e1, cs, AF.Exp, scale=-1.0)
        self.tt(rt, r, e1, ALU.mult)
        self.act(e1, cs, AF.Exp)
        self.tt(bt, kkv, av, ALU.mult)
        self.tt(bt, bt, e1, ALU.mult)
        self.tt(kt, k2, e1, ALU.mult)
        pc = c[:, 112:112 + 8 * nseq].rearrange("p (h s) -> p h s", h=8)
        csv = g[7][:].rearrange("p (h s t) -> p h s t", h=8, s=nseq)
        self.act(pc, csv[:, :, :, L - 1], AF.Exp, scale=-1.0)
        pcb = self.bc(pc.unsqueeze(3), (128, 8, nseq, L))
        bh = v8(g[12])
        kh = v8(g[4])
        v4 = lambda t: t[:].rearrange("p (h s t) -> p h s t", h=8, s=nseq)
        self.tt(v4(g[12]), v4(g[10]), pcb, ALU.mult)
        self.tt(v4(g[4]), v4(g[11]), pcb, ALU.mult)
        Vtm, Bh, Kh = g[0], g[1], g[2]
        for (dst, src) in ((Vtm, v), (Bh, bh), (Kh, kh)):
            for ch in range(8):
                self.tr(ps[1][:, ch // 4, (ch % 4) * 128:(ch % 4) * 128 + 128], src[:, ch, :])
            self.cp(dst[:], ps[1][:].rearrange("p a b -> p (a b)"), eng='act')
        if self.STOP == 'E2':
            return
        ytm = self.qkv[:, 0:8, :]
        Utm = self.qkv[:, 8:16, :]
        ytf = self.qkv[:, 0:8, :].rearrange("p a b -> p (a b)")
        Utf = self.qkv[:, 8:16, :].rearrange("p a b -> p (a b)")
        Sld = self.qkv[:, 16:24, :]
        AkT = self.xp[:, 0:2048].rearrange("p (h t) -> p h t", h=16)
        RkT = self.xp[:, 2048:4096].rearrange("p (h t) -> p h t", h=16)
        RbT = self.pp[:, 0:2048].rearrange("p (h t) -> p h t", h=16)
        hb = lambda t, i: t[:].bitcast(BF16)[:, i * 1024:(i + 1) * 1024].rearrange("p (h t) -> p h t", h=8)
        A0, B0 = hb(g[5], 0), hb(g[5], 1)
        bonz = self.zT[:]
        if not so_:
            self.cp(bonz, bon, eng='act')
        A1, B1 = hb(g[7], 0), hb(g[7], 1)
        Xb8 = g[12][:].bitcast(BF16)[:, 0:512].rearrange("p (h t) -> p h t", h=8)
        if kind == 's':
            atm = self.pp[:, 2048:3072].rearrange("p (h t) -> p h t", h=8)
            rtm = self.scrA[:, 0:1024].rearrange("p (h t) -> p h t", h=8)
        for half in range(2):
            hs = range(half * 8, half * 8 + 8)
            def slot(t, i):
                return t[:, i % 2, (i // 2) * 128:(i // 2) * 128 + 128]
            pperm = lambda t: t[:].rearrange("p a (b t) -> p a b t", b=4)
            dperm = lambda d: d.rearrange("p (i two) t -> p two i t", two=2)
            m8 = lambda n: self.bc(M(n).unsqueeze(1).unsqueeze(1), (128, 2, 4, 128))
            for i, h in enumerate(hs):
                ch, o64 = h // 2, (h % 2) * 64
                sl = slice(o64, o64 + 64)
                self.mm(slot(ps[0], i), bt[sl, ch, :], at[sl, ch, :])
                if not so_:
                    self.mm(slot(ps[1], i), bt[sl, ch, :], rt[sl, ch, :])
                self.mm(slot(ps[2], i), at[sl, ch, :], bt[sl, ch, :])
            self.tt(dperm(B0), pperm(ps[0]), m8('strictT'), ALU.mult)
            if not so_:
                self.tt(dperm(RbT[:, half * 8:half * 8 + 8, :]), pperm(ps[1]), m8('causT'), ALU.mult)
            self.tt(dperm(A0), pperm(ps[2]), m8('strict'), ALU.mult)
            for i, h in enumerate(hs):
                ch, o64 = h // 2, (h % 2) * 64
                sl = slice(o64, o64 + 64)
                self.mm(slot(ps[0], i), kt[sl, ch, :], at[sl, ch, :])
                if not so_:
                    self.mm(slot(ps[1], i), kt[sl, ch, :], rt[sl, ch, :])
            self.tt(dperm(AkT[:, half * 8:half * 8 + 8, :]), pperm(ps[0]), m8('strictT'), ALU.mult)
            if not so_:
                self.tt(dperm(RkT[:, half * 8:half * 8 + 8, :]), pperm(ps[1]), m8('causT'), ALU.mult)
            prh = ps[1][:, 0, :]
            py = ps[1][:, 1, :]
            zz = self.zeros
            self.mm(prh, zz[:, 0:128], zz[:, 0:512], start=True, stop=False, skip=True)
            if not so_:
                self.mm(py, zz[:, 0:128], zz[:, 0:512], start=True, stop=False, skip=True)
            if kind == 'p':
                for cc in range(4):
                    ch = half * 4 + cc
                    self.mm(prh[:, cc * 128:(cc + 1) * 128], at[:, ch, :], self.Hb[:, ch, :], start=False, stop=False,
                            skip=True)
                    if not so_:
                        self.mm(py[:, cc * 128:(cc + 1) * 128], rt[:, ch, :], self.Hb[:, ch, :], start=False,
                                stop=False, skip=True)
            else:
                if half == 0:
                    self.memset(atm, 0.0)
                    self.memset(rtm, 0.0)
                for s in range(16):
                    Hs = (self.Hb, self.Sg)[s % 2]
                    self.dma(Hs[:], self.sr[s])
                    if s > 0:
                        self.memset(atm[:, :, 8 * s - 8:8 * s], 0.0)
                        self.memset(rtm[:, :, 8 * s - 8:8 * s], 0.0)
                    self.cp(atm[:, :, 8 * s:8 * s + 8], at[:, :, 8 * s:8 * s + 8])
                    self.cp(rtm[:, :, 8 * s:8 * s + 8], rt[:, :, 8 * s:8 * s + 8])
                    for cc in range(4):
                        ch = half * 4 + cc
                        self.mm(prh[:, cc * 128:(cc + 1) * 128], atm[:, ch, :], Hs[:, ch, :], start=False,
                                stop=False, skip=True)
                        self.mm(py[:, cc * 128:(cc + 1) * 128], rtm[:, ch, :], Hs[:, ch, :], start=False,
                                stop=False, skip=True)
                if half == 0:
                    self.memset(atm[:, :, 120:128], 0.0)
                    self.memset(rtm[:, :, 120:128], 0.0)
            for i, h in enumerate(hs):
                self.mm(prh[:, i * 64:(i + 1) * 64], AkT[:, h, :], Vtm[:, h * 64:(h + 1) * 64], start=False,
                        stop=True, skip=True)
            X = Utm[:, half * 4:half * 4 + 4, :].rearrange("p a (b c) -> p (a b) c", b=2)
            self.ts(X, prh.rearrange("p (h c) -> p h c", h=8), -1.0, ALU.mult)
            self.solve(A0, B0, A1, B1, X, 64, nlev, Xb=Xb8, xparts=[(X, Xb8)])
            for i, h in enumerate(hs if not so_ else []):
                self.mm(py[:, i * 64:(i + 1) * 64], RbT[:, h, :], Utf[:, h * 64:(h + 1) * 64], start=False, stop=False,
                        skip=True)
                self.mm(py[:, i * 64:(i + 1) * 64], RkT[:, h, :], Vtm[:, h * 64:(h + 1) * 64], start=False, stop=True,
                        skip=True)
            if not so_:
                self.cp(ytf[:, half * 512:(half + 1) * 512], py, eng='act')
        if self.STOP == 'E3':
            return
        if kind == 'p':
            for ch in range(8):
                po = ps[ch // 4][:, (ch % 4) // 2, (ch % 2) * 128:(ch % 2) * 128 + 128]
                self.mm(po, Bh[:, ch * 128:(ch + 1) * 128], Utf[:, ch * 128:(ch + 1) * 128], start=True, stop=False)
                self.mm(po, Kh[:, ch * 128:(ch + 1) * 128], Vtm[:, ch * 128:(ch + 1) * 128], start=False, stop=True)
            hn = v8(g[5])
            for q in range(2):
                self.tt(hn[:, 4 * q:4 * q + 4, :].rearrange("p (a b) t -> p a b t", a=2),
                        ps[q][:, :, 0:256].rearrange("p a (b t) -> p a b t", b=2),
                        self.bc(self.C('blk').unsqueeze(1).unsqueeze(1), (128, 2, 2, 128)), ALU.mult)
            self.tt(self.Hb[:], self.Hb[:], self.bc(pc[:, :, 0:1], (128, 8, 128)), ALU.mult)
            self.tt(self.Hb[:], self.Hb[:], hn, ALU.add)
            if last:
                self.store_H(self.o_rwkv_p)
        else:
            for s in range(16):
                b = s % 2
                Bm, Km = (g[5], g[4])[b], (g[6], g[12])[b]
                hn, t2 = v8((g[7], g[8])[b]), v8((g[9], g[10])[b])
                Hs = (self.Hb, self.Sg)[b]
                self.dma(Hs[:], self.sr[s])
                self.act(Bm[:], Bh[:], AF.Copy, scale=M('rowm')[:, s:s + 1])
                self.act(Km[:], Kh[:], AF.Copy, scale=M('rowm')[:, s:s + 1])
                for ch in range(8):
                    po = ps[2 * b + ch // 4][:, (ch % 4) // 2, (ch % 2) * 128:(ch % 2) * 128 + 128]
                    self.mm(po, Bm[:, ch * 128:(ch + 1) * 128], Utf[:, ch * 128:(ch + 1) * 128], start=True, stop=False)
                    self.mm(po, Km[:, ch * 128:(ch + 1) * 128], Vtm[:, ch * 128:(ch + 1) * 128], start=False, stop=True)
                self.tt(t2, Hs[:], self.bc(pc[:, :, s:s + 1], (128, 8, 128)), ALU.mult)
                for q in range(2):
                    self.tt(hn[:, 4 * q:4 * q + 4, :].rearrange("p (a b) t -> p a b t", a=2),
                            ps[2 * b + q][:, :, 0:256].rearrange("p a (b t) -> p a b t", b=2),
                            self.bc(self.C('blk').unsqueeze(1).unsqueeze(1), (128, 2, 2, 128)), ALU.mult)
                self.tt(hn, hn, t2, ALU.add)
                self.dma(self.o_rwkv_s[s], hn)
        if so_:
            return
        y3 = ytf.rearrange("p (h c) -> p h c", h=16)
        mean = c[:, 0:16]
        var = c[:, 16:32]
        self.reduce(mean, y3)
        self.ts(mean, mean, 1.0 / 64, ALU.mult)
        self.tt(y3, y3, self.bc(mean.unsqueeze(2), (128, 16, 64)), ALU.subtract)
        ysq = g[5][:].rearrange("p (h c) -> p h c", h=16)
        self.tt(ysq, y3, y3, ALU.mult)
        self.reduce(var, ysq)
        self.rsqrt(var, var, 1.0 / 64, 64e-5)
        self.tt(y3, y3, self.bc(var.unsqueeze(2), (128, 16, 64)), ALU.mult)
        for ch in range(8):
            self.tr(ps[2][:, ch // 4, (ch % 4) * 128:(ch % 4) * 128 + 128], ytf[:, ch * 128:(ch + 1) * 128])
        yT = v8(g[5])
        self.tt(yT, p8(ps[2]), self.bc(self.P('gnw').unsqueeze(2), (128, 8, 128)), ALU.mult)
        self.tt(yT, yT, self.bc(self.P('gnb').unsqueeze(2), (128, 8, 128)), ALU.add)
        self.tt(yT, yT, bonz, ALU.add)
        self.tt(self.mixedT[:, 8:16, :], yT, gate, ALU.mult)

    def load_H(self, s):
        self.dma(self.Hb[:], self.sr[s])

    def store_H(self, dst):
        self.dma(dst, self.Hb[:])

    def ffn(self, kind, nseq, L, last, hist_only=False):
        ps, g, c = self.ps, self.g, self.cols
        self.norm_T('ln2g')
        Lh = L + 2
        actT = self.xp[:, 0:2816].bitcast(BF16).rearrange("p (j t) -> p j t", j=44)
        fw = self.P('fconvw').rearrange("p (c i) -> p c i", i=3)
        qflat = self.qkv[:].rearrange("p c t -> p (c t)")
        for gi in range(6):
            j0 = gi * 8
            ng = min(8, 44 - j0)
            self.stream_up(self.c_up, [(j0 * 128, ng * 128, 0), (DFF + j0 * 128, ng * 128, 1024)], 0)
            for which in range(2):
                ch0 = which * 44 + j0
                c0 = ch0 * 128
                h = self.pp[:, which * 1920:which * 1920 + ng * nseq * Lh].rearrange("p (c s t) -> p c s t", c=ng, s=nseq)
                pt = ps[which][:].rearrange("p a (b t) -> p (a b) t", b=4)[:, 0:ng, :]
                if kind == 'p':
                    self.cp(h[:, :, 0, 0:2], self.fhist[:, ch0:ch0 + ng, :])
                else:
                    st = g[which][0:32, 0:ng * 128]
                    self.dma(st, self.sfc[:, c0:c0 + ng * 128])
                    for j in range(ng):
                        self.tr(ps[2][:, which, j * 32:(j + 1) * 32], st[:, j * 128:(j + 1) * 128])
                    self.cp(h[:, :, :, 0:2], ps[2][:, which, 0:ng * 32].rearrange("p (c s t) -> p c s t", c=ng, s=16))
                self.cp(h[:, :, :, 2:2 + L], pt.rearrange("p c (s t) -> p c s t", s=nseq), eng='act')
                if kind == 'p':
                    self.cp(self.fhist[:, ch0:ch0 + ng, :], h[:, :, 0, L:L + 2])
                    if last:
                        so = g[2 + which][0:2, 0:ng * 128]
                        for j in range(ng):
                            self.tr(ps[3][0:2, j // 4, (j % 4) * 128:(j % 4) * 128 + 128], h[:, j, 0, L:L + 2])
                        self.cp(so, ps[3][0:2, :, :].rearrange("p a b -> p (a b)")[:, 0:ng * 128])
                        self.dma(self.o_ffn_p[:, c0:c0 + ng * 128], so)
                else:
                    hs = self.scrA[:, 2048 + which * 256:2048 + which * 256 + ng * 32]
                    self.cp(hs.rearrange("p (c s r) -> p c s r", c=ng, s=16), h[:, :, :, L:L + 2])
                    hs3 = hs.rearrange("p (c r) -> p c r", c=ng)
                    so = g[2 + which][0:32, 0:ng * 128]
                    for j in range(ng):
                        self.tr(ps[3][0:32, j // 4, (j % 4) * 128:(j % 4) * 128 + 128], hs3[:, j, :])
                    self.cp(so, ps[3][0:32, :, :].rearrange("p a b -> p (a b)")[:, 0:ng * 128])
                    self.dma(self.o_ffn_s[:, c0:c0 + ng * 128], so)
                if hist_only:
                    continue
                wsh = (128, ng, nseq, L)
                o = self.scrA[:, which * 1024:which * 1024 + ng * 128].rearrange("p (c s t) -> p c s t", c=ng, s=nseq)
                tmp = qflat[:, which * 1024:which * 1024 + ng * 128].rearrange("p (c s t) -> p c s t", c=ng, s=nseq)
                self.tt(o, h[:, :, :, 2:2 + L], self.bc(fw[:, ch0:ch0 + ng, 2:3].unsqueeze(3), wsh), ALU.mult)
                for i in range(2):
                    self.tt(tmp, h[:, :, :, i:i + L], self.bc(fw[:, ch0:ch0 + ng, i:i + 1].unsqueeze(3), wsh), ALU.mult,
                            eng='pool')
                    self.tt(o, o, tmp, ALU.add)
            if hist_only:
                continue
            gt = self.scrA[:, 0:ng * 128]
            up = self.scrA[:, 1024:1024 + ng * 128]
            self.act(gt, gt, AF.Silu)
            self.tt(actT[:, j0:j0 + ng, :], gt.rearrange("p (c t) -> p c t", c=ng),
                    up.rearrange("p (c t) -> p c t", c=ng), ALU.mult)
        if hist_only:
            return
        self.stream_down(self.c_down, 44, actT, 1)
        for q in range(2):
            self.tt(self.xt[:, q * 1024:(q + 1) * 1024], self.xt[:, q * 1024:(q + 1) * 1024],
                    ps[2 + q][:].rearrange("p a b -> p (a b)"), ALU.add)

    def ffn_pair(self, last):
        ps, g = self.ps, self.g
        zz = self.zeros
        fw = self.P('fconvw').rearrange("p (c i) -> p c i", i=3)
        actA = self.xp[:, 0:2816].bitcast(BF16).rearrange("p (j t) -> p j t", j=22)
        actB = self.pp[:, 0:2816].bitcast(BF16).rearrange("p (j t) -> p j t", j=22)
        act_ap = lambda j: (actA if j < 22 else actB)[:, j % 22, :]
        hbuf = [self.qkv[:].rearrange("p c t -> p (c t)"), self.scrA]
        grp = 0
        for gi in range(6):
            j0 = gi * 8
            ng = min(8, 44 - j0)
            nq = (ng + 3) // 4
            for which in range(2):
                ch0 = which * 44 + j0
                pair = grp % 2
                grp += 1
                for bk in range(ng // 2):
                    self.mm(ps[pair * 2 + bk // 2][:, bk % 2, :], zz[:, 0:128], zz[:, 0:512], start=True, stop=False,
                            skip=True)
                for k in range(16):
                    t = self.wslot_load([(0, self.c_up[k * 128:(k + 1) * 128, ch0 * 128:(ch0 + ng) * 128])])
                    for j in range(ng):
                        self.mm(ps[pair * 2 + j // 4][:, (j % 4) // 2, (j % 2) * 256:(j % 2) * 256 + 256],
                                t[:, j * 128:(j + 1) * 128], self.xnT2[:, k, :], start=False, stop=(k == 15), skip=True)
                h = hbuf[which][:, 0:ng * 258].rearrange("p (c t) -> p c t", c=ng)
                self.cp(h[:, :, 0:2], self.fhist[:, ch0:ch0 + ng, :])
                for q in range(nq):
                    n4 = min(4, ng - 4 * q)
                    self.cp(h[:, 4 * q:4 * q + n4, 2:258],
                            ps[pair * 2 + q][:].rearrange("p a (b t) -> p (a b) t", b=2)[:, 0:n4, :],
                            eng='act' if q == 0 else 'dve')
                self.cp(self.fhist[:, ch0:ch0 + ng, :], h[:, :, 256:258])
                if last:
                    po_ = ps[(pair ^ 1) * 2]
                    so = g[12][0:2, 0:ng * 128]
                    for j in range(ng):
                        self.tr(po_[0:2, j // 4, (j % 4) * 128:(j % 4) * 128 + 128], h[:, j, 256:258])
                    self.cp(so, po_[0:2, :, :].rearrange("p a b -> p (a b)")[:, 0:ng * 128])
                    self.dma(self.o_ffn_p[:, ch0 * 128:(ch0 + ng) * 128], so)
                for q in range(nq):
                    n4 = min(4, ng - 4 * q)
                    wsh = (128, n4, 256)
                    o = g[which * 2 + q][:, 0:n4 * 256].rearrange("p (c t) -> p c t", c=n4)
                    tmp = g[4 + q][:, 0:n4 * 256].rearrange("p (c t) -> p c t", c=n4)
                    cs_ = slice(ch0 + 4 * q, ch0 + 4 * q + n4)
                    hq = h[:, 4 * q:4 * q + n4, :]
                    self.tt(o, hq[:, :, 2:258], self.bc(fw[:, cs_, 2:3], wsh), ALU.mult)
                    for i in range(2):
                        self.tt(tmp, hq[:, :, i:i + 256], self.bc(fw[:, cs_, i:i + 1], wsh), ALU.mult, eng='pool')
                        self.tt(o, o, tmp, ALU.add)
            for q in range(nq):
                n4 = min(4, ng - 4 * q)
                gt = g[q][:, 0:n4 * 256]
                up = g[2 + q][:, 0:n4 * 256]
                self.act(gt, gt, AF.Silu)
                for i in range(n4):
                    j = j0 + 4 * q + i
                    self.tt(act_ap(j), gt[:, i * 256:(i + 1) * 256], up[:, i * 256:(i + 1) * 256], ALU.mult,
                            eng='dve' if i % 2 == 0 else 'pool')
        for k in range(44):
            t = self.wslot_load([(0, self.c_down[k * 128:(k + 1) * 128, :])])
            for i in range(2):
                for n in range(4):
                    self.mm(ps[i * 2 + n // 2][:, n % 2, :], act_ap(k)[:, i * 128:(i + 1) * 128],
                            t[:, n * 512:(n + 1) * 512], start=(k == 0), stop=(k == 43))
        for i in range(2):
            xt = self.xts[i]
            for q in range(2):
                self.tt(xt[:, q * 1024:(q + 1) * 1024], xt[:, q * 1024:(q + 1) * 1024],
                        ps[i * 2 + q][:].rearrange("p a b -> p (a b)"), ALU.add)

    def build(self, do_sample=True):
        self.dma(self.params[:], self.params_d)
        self.dma(self.consts[:], self.consts_d)
        self.dma(self.fg[:], self.fg_d)
        self.dma(self.wab[:], self.wab_d)
        self.dma(self.gb[:], self.gb_d)
        self.memset(self.zeros[:], 0.0)
        self.convert_weights()
        ntile = self.NPRE + self.NMAIN
        paired = (self.NMAIN % 2 == 0) and PAIR_FFN
        for t in range(ntile):
            full = 2 if t >= self.NPRE else (1 if t == self.NPRE - 1 else 0)
            self.xt = self.xts[0]
            self.pool_eng = 'pool' if t >= 2 else 'dve'
            if full == 2:
                i = t - self.NPRE
                src, dst = self.xmain[i * 128:(i + 1) * 128, :], self.y_main[i * 128:(i + 1) * 128, :]
                if paired:
                    self.xt = self.xts[i % 2]
                    self.tile('p', src, dst, full, t == 0, t == ntile - 1, None, pair_slot=i % 2)
                    if i % 2 == 1:
                        self.ffn_pair(t == ntile - 1)
                        for m in range(2):
                            self.xt = self.xts[m]
                            self.final(self.y_main[(i - 1 + m) * 128:(i + m) * 128, :])
                    continue
            else:
                src, dst = self.xpre[t * 128:(t + 1) * 128, :], None
            self.tile('p', src, dst, full, t == 0, t == ntile - 1, None)
        self.xt = self.xts[0]
        if do_sample:
            self.tile('s', self.xs, self.y_s, 2, False, False, True)
        self.em.finish()
        self.em.replay()
        return self.nc


def rwkv_to_blockdiag(st):
    n = st.shape[0]
    out = np.zeros((n, 128, 8, 128), np.float32)
    t = np.asarray(st, np.float32).reshape(n, 8, 2, 64, 64)
    for two in range(2):
        out[:, two * 64:(two + 1) * 64, :, two * 64:(two + 1) * 64] = t[:, :, two].transpose(0, 3, 1, 2)
    return out


def rwkv_from_blockdiag(o):
    n = o.shape[0]
    res = np.zeros((n, 8, 2, 64, 64), np.float32)
    for two in range(2):
        res[:, :, two] = o[:, two * 64:(two + 1) * 64, :, two * 64:(two + 1) * 64].transpose(0, 2, 3, 1)
    return res.reshape(n, 16, 64, 64)


_CACHE = {}


def get_program(NPRE, NMAIN):
    key = (NPRE, NMAIN)
    if key not in _CACHE:
        _CACHE[key] = Builder(NPRE, NMAIN).build()
    return _CACHE[key]


def kernel(**inp):
    inp = {k: np.asarray(v) for k, v in inp.items()}
    NPRE, NMAIN = 8, 8
    nc = get_program(NPRE, NMAIN)
    consts = make_consts()
    params = make_params(inp)
    fg = np.ascontiguousarray(np.broadcast_to(inp['final_g'][None, :], (128, D))).astype(np.float32)
    wab = np.ascontiguousarray(np.concatenate([inp['rwkv_w_b'][0], inp['rwkv_a_b'][0]], axis=0)).astype(np.float32)
    gb = np.ascontiguousarray(inp['rwkv_g_b'][0]).astype(np.float32)
    shared = dict(w_in=np.ascontiguousarray(inp['w_in'][0]), w_o=np.ascontiguousarray(inp['w_o'][0]),
                  w_up=np.ascontiguousarray(inp['ffn_w_up'][0]), w_down=np.ascontiguousarray(inp['ffn_w_down'][0]),
                  params=params, consts=consts, fg=fg, wab=wab, gb=gb)
    xp_, xs_ = inp['x_prompt'], inp['x_sample']
    in_maps = []
    for cid in range(NCORES):
        b, half = cid // 2, cid % 2
        m = dict(shared)
        pc_ = params.copy()
        pc_[:, POFF['hflag'][0]] = float(half)
        m['params'] = pc_
        if half == 0:
            m['xpre'] = np.zeros((NPRE * 128, D), np.float32)
            m['xmain'] = np.ascontiguousarray(xp_[b, 0:1024])
        else:
            m['xpre'] = np.ascontiguousarray(xp_[b, 0:1024])
            m['xmain'] = np.ascontiguousarray(xp_[b, 1024:2048])
        sl = slice(cid * 16, cid * 16 + 16)
        m['xs'] = np.ascontiguousarray(xs_[sl].reshape(128, D))
        m['sg'] = np.ascontiguousarray(inp['state_gdn'][0, sl])
        m['sgc'] = np.ascontiguousarray(inp['state_gdn_conv'][0, sl].reshape(48, 3072))
        m['sr'] = rwkv_to_blockdiag(inp['state_rwkv'][0, sl])
        m['ssh'] = np.ascontiguousarray(inp['state_rwkv_shift'][0, sl])
        m['sfc'] = np.ascontiguousarray(inp['state_ffn_conv'][0, sl].reshape(32, 11264))
        in_maps.append(m)
    res = run_bass_kernel_spmd(nc, in_maps, core_ids=list(range(NCORES)))
    R = res.results
    y_prompt = np.zeros((4, 2048, D), np.float32)
    for cid in range(NCORES):
        b, half = cid // 2, cid % 2
        y_prompt[b, half * 1024:(half + 1) * 1024] = R[cid]['y_main']
    y_sample = np.concatenate([R[c]['y_s'].reshape(16, 8, D) for c in range(NCORES)], 0)
    odd = [1, 3, 5, 7]
    gdn_p = np.stack([R[c]['gdn_p'] for c in odd])[None]
    gconv_p = np.stack([R[c]['gconv_p'] for c in odd])[None]
    rwkv_p = np.stack([rwkv_from_blockdiag(R[c]['rwkv_p'][None])[0] for c in odd])[None]
    shift_p = np.stack([R[c]['shift_p'].reshape(3328) for c in odd])[None]
    ffn_p = np.stack([R[c]['ffn_p'] for c in odd])[None]
    gdn_s = np.concatenate([R[c]['gdn_s'] for c in range(NCORES)], 0)[None]
    gconv_s = np.concatenate([R[c]['gconv_s'].reshape(16, 3, 3072) for c in range(NCORES)], 0)[None]
    rwkv_s = np.concatenate([rwkv_from_blockdiag(R[c]['rwkv_s']) for c in range(NCORES)], 0)[None]
    shift_s = np.concatenate([R[c]['shift_s'] for c in range(NCORES)], 0)[None]
    ffn_s = np.concatenate([R[c]['ffn_s'].reshape(16, 2, 11264) for c in range(NCORES)], 0)[None]
    return (y_prompt, y_sample, gdn_p, gconv_p, rwkv_p, shift_p, ffn_p, gdn_s, gconv_s, rwkv_s, shift_s, ffn_s)
```

```python
import numpy as np
import concourse.bass as bass
import concourse.mybir as mybir
from concourse.bass_utils import run_bass_kernel_spmd

F32 = mybir.dt.float32
BF16 = mybir.dt.bfloat16
AF = mybir.ActivationFunctionType
ALU = mybir.AluOpType
AX = mybir.AxisListType

D = 2048
DFF = 5632
INW = 7440
NCORES = 8
SELF_SYNC = True
RELAX_SELF_WAR = True
FP32R = False
PAIR_FFN = True
FAST_RSQRT = True


class Em:
    ENG = ['pe', 'act', 'dve', 'pool', 'sp']

    def __init__(self, nc):
        self.nc = nc
        self.stream = {e: [] for e in self.ENG}
        self.sems = {}
        for e in self.ENG:
            self.sems["cnt_" + e] = nc.alloc_semaphore("cnt_" + e)
        self.ecnt = {e: 0 for e in self.ENG}
        self.epoch = {e: 0 for e in self.ENG}
        self.cur = {e: "cnt_" + e for e in self.ENG}
        self.waited = {}
        self.state = {}
        self.dcount = {}
        self.ninst = 0

    @staticmethod
    def _keys(aps):
        ks = []
        for a in aps:
            if a is None or isinstance(a, (int, float)):
                continue
            ks.append(a if isinstance(a, str) else a.tensor.name)
        return ks

    def _need(self, eng, tok):
        semname, val = tok
        if semname.startswith("cnt_" + eng) and (eng == 'pe' or not SELF_SYNC):
            return
        k = (eng, semname)
        if self.waited.get(k, 0) >= val:
            return
        self.waited[k] = val
        self.stream[eng].append(('wait', semname, val))

    def _deps(self, eng, reads, writes):
        for k in reads:
            st = self.state.get(k)
            if st and st[0]:
                self._need(eng, st[0])
        skip_self = eng in ('act', 'dve') and RELAX_SELF_WAR
        for k in writes:
            st = self.state.get(k)
            if st:
                if st[0] and not (skip_self and st[0][0].startswith("cnt_" + eng)):
                    self._need(eng, st[0])
                for s, v in st[1].items():
                    if skip_self and s.startswith("cnt_" + eng):
                        continue
                    self._need(eng, (s, v))

    def _commit(self, tok, reads, writes):
        for k in reads:
            st = self.state.setdefault(k, [None, {}])
            st[1][tok[0]] = max(st[1].get(tok[0], 0), tok[1])
        for k in writes:
            self.state[k] = [tok, {}]

    def op(self, eng, fn, ins, outs):
        reads, writes = self._keys(ins), self._keys(outs)
        self._deps(eng, reads, writes)
        if self.ecnt[eng] >= 30000:
            self.epoch[eng] += 1
            self.ecnt[eng] = 0
            nm = "cnt_%s_%d" % (eng, self.epoch[eng])
            self.sems[nm] = self.nc.alloc_semaphore(nm)
            self.cur[eng] = nm
        self.ecnt[eng] += 1
        tok = (self.cur[eng], self.ecnt[eng])
        self.stream[eng].append(('inst', fn, self.cur[eng], 1))
        self._commit(tok, reads, writes)
        self.ninst += 1

    def dma(self, eng, out, in_, chain=False, **kw):
        sb = out if out.space == 'SB' else in_
        semname = "dma_" + sb.tensor.name
        if semname not in self.sems:
            self.sems[semname] = self.nc.alloc_semaphore(semname)
            self.dcount[semname] = 0
        reads, writes = self._keys([in_]), self._keys([out])
        self._deps(eng, reads, [] if chain else writes)
        if self.dcount[semname] > 0 and not chain:
            self._need(eng, (semname, 16 * self.dcount[semname]))
        self.dcount[semname] += 1
        tok = (semname, 16 * self.dcount[semname])
        self.stream[eng].append(('inst', lambda e: e.dma_start(out=out, in_=in_, **kw), semname, 16))
        self._commit(tok, reads, writes)
        self.ninst += 1

    def finish(self, eng='sp'):
        for semname, c in self.dcount.items():
            self._need(eng, (semname, 16 * c))
        for e in self.ENG:
            if self.ecnt[e] > 0 and e != eng:
                self._need(eng, (self.cur[e], self.ecnt[e]))

    def replay(self):
        nc = self.nc
        emap = {'pe': 'tensor', 'act': 'scalar', 'dve': 'vector', 'pool': 'gpsimd', 'sp': 'sync'}
        with nc.Block() as block:
            for en in self.ENG:
                items = self.stream[en]

                def body(engine, items=items):
                    for it in items:
                        if it[0] == 'wait':
                            engine.wait_ge(self.sems[it[1]], it[2])
                        else:
                            it[1](engine).then_inc(self.sems[it[2]], it[3])
                getattr(block, emap[en])(body)


PARAM_SPEC = [('ln1g', 16), ('ln2g', 16), ('convw', 96), ('gng', 1), ('alog', 8), ('dtb', 8), ('mu', 26),
              ('w0', 8), ('a0', 8), ('kk', 8), ('ka', 8), ('rk', 8), ('gnw', 8), ('gnb', 8), ('fconvw', 264), ('hflag', 1)]
CONST_SPEC = [('ident', 128), ('ones', 128), ('blk', 128),
              ('caus_p', 128), ('strict_p', 128), ('causT_p', 128), ('strictT_p', 128), ('seq_p', 128),
              ('caus_s', 128), ('strict_s', 128), ('causT_s', 128), ('strictT_s', 128), ('seq_s', 128),
              ('seg_p', 128), ('seg_s', 128), ('rowm_p', 16), ('rowm_s', 16)]


def _offsets(spec):
    o, d = 0, {}
    for n, w in spec:
        d[n] = (o, w)
        o += w
    return d, o


POFF, PW = _offsets(PARAM_SPEC)
COFF, CW = _offsets(CONST_SPEC)


def make_consts():
    c = np.zeros((128, CW), np.float32)

    def put(n, a):
        o, w = COFF[n]
        c[:, o:o + w] = a
    i = np.arange(128)
    put('ident', np.eye(128))
    put('ones', np.ones((128, 128)))
    blk = (i[:, None] // 64) == (i[None, :] // 64)
    put('blk', blk)
    for kind, L in (('p', 128), ('s', 8)):
        same = (i[:, None] // L) == (i[None, :] // L)
        caus = same & (i[:, None] >= i[None, :])
        strict = same & (i[:, None] > i[None, :])
        put('caus_' + kind, caus)
        put('strict_' + kind, strict)
        put('causT_' + kind, caus.T)
        put('strictT_' + kind, strict.T)
        put('seq_' + kind, same)
        put('seg_' + kind, np.broadcast_to((i % L != 0)[None, :], (128, 128)))
        rm = np.zeros((128, 16))
        nseq = 128 // L
        rm[i, (i // L)] = 1.0
        put('rowm_' + kind, rm[:, :16])
    return c


def make_params(inp):
    p = np.zeros((128, PW), np.float32)

    def put(n, a):
        o, w = POFF[n]
        p[:, o:o + w] = np.asarray(a, np.float32).reshape(128, w)

    def fm(v):
        v = np.asarray(v, np.float32).reshape(-1, 128)
        return v.T
    put('ln1g', fm(inp['ln1_g'][0]))
    put('ln2g', fm(inp['ln2_g'][0]))
    cw = np.asarray(inp['gdn_conv_w'][0], np.float32)
    put('convw', cw.reshape(4, 24, 128).transpose(2, 1, 0).reshape(128, 96))
    put('gng', np.asarray(inp['gdn_norm_g'][0]).reshape(128, 1))
    put('alog', np.broadcast_to(np.asarray(inp['gdn_a_log'][0])[None, :], (128, 8)))
    put('dtb', np.broadcast_to(np.asarray(inp['gdn_dt_bias'][0])[None, :], (128, 8)))
    put('mu', fm(inp['rwkv_mu'][0]))
    put('w0', fm(inp['rwkv_w0'][0]))
    put('a0', fm(inp['rwkv_a0'][0]))
    put('kk', fm(inp['rwkv_k_k'][0]))
    put('ka', fm(inp['rwkv_k_a'][0]))
    put('rk', fm(np.asarray(inp['rwkv_r_k'][0]).reshape(-1)))
    put('gnw', fm(inp['rwkv_gn_w'][0]))
    put('gnb', fm(inp['rwkv_gn_b'][0]))
    fw = np.asarray(inp['ffn_conv_w'][0], np.float32)
    put('fconvw', fw.reshape(3, 88, 128).transpose(2, 1, 0).reshape(128, 264))
    return p


class Builder:
    def __init__(self, NPRE, NMAIN):
        self.NPRE, self.NMAIN = NPRE, NMAIN
        import os
        self.STOP = os.environ.get('KSTOP', '')
        self.DBG = bool(os.environ.get('KDBG', ''))
        nc = self.nc = bass.Bass("TRN2", target_bir_lowering=False)
        self.em = Em(nc)
        din = lambda n, s: nc.dram_tensor(n, list(s), F32, kind="ExternalInput").ap()
        dout = lambda n, s: nc.dram_tensor(n, list(s), F32, kind="ExternalOutput").ap()
        self.xpre = din("xpre", (max(NPRE, 1) * 128, D))
        self.xmain = din("xmain", (NMAIN * 128, D))
        self.xs = din("xs", (128, D))
        self.sg = din("sg", (16, 8, 128, 128))
        self.sgc = din("sgc", (48, 3072))
        self.sr = din("sr", (16, 128, 8, 128))
        self.ssh = din("ssh", (16, 3328))
        self.sfc = din("sfc", (32, 11264))
        self.w_in = din("w_in", (D, INW))
        self.w_o = din("w_o", (D, D))
        self.w_up = din("w_up", (D, 2 * DFF))
        self.w_down = din("w_down", (DFF, D))
        self.params_d = din("params", (128, PW))
        self.consts_d = din("consts", (128, CW))
        self.fg_d = din("fg", (128, D))
        self.wab_d = din("wab", (128, 1024))
        self.gb_d = din("gb", (128, 1024))
        if self.DBG:
            self.dbg = nc.dram_tensor("dbg", [2, 128, D], F32, kind="ExternalOutput").ap()
        self.y_main = dout("y_main", (NMAIN * 128, D))
        self.y_s = dout("y_s", (128, D))
        self.o_gdn_p = dout("gdn_p", (8, 128, 128))
        self.o_gconv_p = dout("gconv_p", (3, 3072))
        self.o_rwkv_p = dout("rwkv_p", (128, 8, 128))
        self.o_shift_p = dout("shift_p", (1, 3328))
        self.o_ffn_p = dout("ffn_p", (2, 11264))
        self.o_gdn_s = dout("gdn_s", (16, 8, 128, 128))
        self.o_gconv_s = dout("gconv_s", (48, 3072))
        self.o_rwkv_s = dout("rwkv_s", (16, 128, 8, 128))
        self.o_shift_s = dout("shift_s", (16, 3328))
        self.o_ffn_s = dout("ffn_s", (32, 11264))

        sb = lambda n, s, dt=F32: nc.alloc_sbuf_tensor(n, list(s), dt)
        self.params = sb("params_sb", (128, PW))
        self.consts = sb("consts_sb", (128, CW))
        self.fg = sb("fg_sb", (128, D))
        self.wab = sb("wab_sb", (128, 1024))
        self.gb = sb("gb_sb", (128, 1024))
        self.xts = [sb("xt", (128, D)), sb("xtB", (128, D))]
        self.xt = self.xts[0]
        self.xnT2 = sb("xnT2", (128, 16, 256), BF16)
        self.xn = sb("xn", (128, D))
        self.xnT = sb("xnT", (128, 16, 128), BF16)
        self.NSLOT = 4
        self.ws = [sb("ws%d" % i, (128, 2048), BF16) for i in range(self.NSLOT)]
        dsc = lambda n, s: nc.dram_tensor(n, list(s), BF16, kind="Internal").ap()
        self.c_in = dsc("wsc_in", (D, INW))
        self.c_o = dsc("wsc_o", (D, D))
        self.c_up = dsc("wsc_up", (D, 2 * DFF))
        self.c_down = dsc("wsc_down", (DFF, D))
        self.wba = sb("wba", (128, 16, 16), BF16)
        self.xp = sb("xp", (128, 24 * 176))
        self.pp = sb("pp", (128, 26 * 144))
        self.qkv = sb("qkv", (128, 24, 128))
        self.scrA = sb("scrA", (128, 3328))
        self.zT = sb("zT", (128, 8, 128))
        self.mixedT = sb("mixedT", (128, 16, 128), BF16)
        self.xhist = sb("xhist", (128, 24, 3))
        self.phist = sb("phist", (128, 26, 1))
        self.fhist = sb("fhist", (128, 88, 2))
        self.Sg = sb("Sg", (128, 8, 128))
        self.Hb = sb("Hb", (128, 8, 128))
        self.NG = 13
        self.g = [sb("g%d" % i, (128, 1024)) for i in range(self.NG)]
        self.zeros = sb("zeros", (128, 512))
        self.cols = sb("cols", (128, 256))
        self.st48 = self.scrA
        self.ps = [nc.alloc_psum_tensor("ps%d" % i, [128, 2, 512], F32) for i in range(4)]
        self.wslot = 0
        self.pool_eng = 'pool'
        self._unused_hist = True

    def P(self, n):
        o, w = POFF[n]
        return self.params[:, o:o + w]

    def C(self, n):
        o, w = COFF[n]
        return self.consts[:, o:o + w]

    def tt(self, out, a, b, op, eng='dve'):
        self.em.op(eng, lambda e: e.tensor_tensor(out=out, in0=a, in1=b, op=op), [a, b], [out])

    def ts(self, out, a, s1, op0, s2=None, op1=None, eng='dve', accum=None):
        kw = dict(out=out, in0=a, scalar1=s1, scalar2=s2, op0=op0)
        if op1 is not None:
            kw['op1'] = op1
        if accum is not None:
            kw['accum_out'] = accum
        self.em.op(eng, lambda e: e.tensor_scalar(**kw), [a, s1, s2], [out, accum])

    def stt(self, out, a, s, b, op0, op1):
        self.em.op('dve', lambda e: e.scalar_tensor_tensor(out=out, in0=a, scalar=s, in1=b, op0=op0, op1=op1),
                   [a, s, b], [out])

    def act(self, out, in_, func, scale=1.0, bias=0.0, accum=None):
        kw = dict(out=out, in_=in_, func=func, scale=scale, bias=bias)
        if accum is not None:
            kw['accum_out'] = accum
        self.em.op('act', lambda e: e.activation(**kw), [in_, scale, bias], [out, accum])

    def cp(self, out, in_, eng='dve'):
        if eng == 'act':
            self.act(out, in_, AF.Copy)
        else:
            self.em.op(eng, lambda e: e.tensor_copy(out=out, in_=in_), [in_], [out])

    def rcp(self, out, in_):
        n = 1
        for d in out.shape[1:]:
            n *= d
        self.em.op('dve', lambda e: e.reciprocal(out=out, in_=in_), [in_], [out])

    def memset(self, ap, val, eng='dve'):
        self.em.op(eng, lambda e: e.memset(ap, val), [], [ap])

    def mm(self, out, lhsT, rhs, start=True, stop=True, skip=False):
        if FP32R and lhsT.dtype == F32:
            lhsT, rhs = lhsT.bitcast(mybir.dt.float32r), rhs.bitcast(mybir.dt.float32r)
        self.em.op('pe', lambda e: e.matmul(out=out, lhsT=lhsT, rhs=rhs, start=start, stop=stop,
                                            skip_group_check=skip), [lhsT, rhs], [out])

    def tr(self, out, in_):
        p = in_.shape[0]
        idn = self.C('ident')[0:p, 0:p]
        self.em.op('pe', lambda e: e.transpose(out=out, in_=in_, identity=idn), [in_, idn], [out])

    def scan(self, out, d0, d1):
        self.em.op('dve', lambda e: e.tensor_tensor_scan(out=out, data0=d0, data1=d1, initial=0.0,
                                                         op0=ALU.mult, op1=ALU.add), [d0, d1], [out])

    def reduce(self, out, in_, op=ALU.add):
        self.em.op('dve', lambda e: e.tensor_reduce(out=out, in_=in_, axis=AX.X, op=op), [in_], [out])

    def dma(self, out, in_, eng='sp', **kw):
        self.em.dma(eng, out, in_, **kw)

    def rsqrt(self, out, in_, scale, eps):
        n = 1
        for d in out.shape[1:]:
            n *= d
        if n >= 256 and FAST_RSQRT:
            self.act(out, in_, AF.Ln, scale=scale, bias=eps)
            self.act(out, out, AF.Exp, scale=-0.5)
        else:
            self.act(out, in_, AF.Sqrt, scale=scale, bias=eps)
            self.rcp(out, out)

    def bc(self, ap, shape):
        return ap.broadcast_to(list(shape))

    def convert_weights(self):
        for (src, dst) in ((self.w_in, self.c_in), (self.w_o, self.c_o), (self.w_up, self.c_up),
                           (self.w_down, self.c_down)):
            R = src.shape[0]
            for r in range(0, R, 256):
                self.em.dma('pool', dst[r:r + 256, :], src[r:r + 256, :], max_dma_last_dim=4096)

    def wslot_load(self, pieces):
        t = self.ws[self.wslot % self.NSLOT]
        self.wslot += 1
        for i, (o, ap) in enumerate(pieces):
            self.dma(t[:, o:o + ap.shape[1]], ap, eng='sp', chain=(i > 0))
        return t

    def acc_region(self, pair, j):
        return self.ps[pair * 2 + j // 8][:, (j % 8) // 4, (j % 4) * 128:(j % 4) * 128 + 128]

    def stream_up(self, wsc, ranges, pair):
        zz = self.zeros
        used = sorted(set((so // 128 + i) // 4 for (c0, n, so) in ranges for i in range(n // 128)))
        for bk in used:
            self.mm(self.ps[pair * 2 + bk // 2][:, bk % 2, :], zz[:, 0:128], zz[:, 0:512], start=True, stop=False,
                    skip=True)
        for k in range(16):
            t = self.wslot_load([(so, wsc[k * 128:(k + 1) * 128, c0:c0 + n]) for (c0, n, so) in ranges])
            for (c0, n, so) in ranges:
                for i in range(n // 128):
                    j = so // 128 + i
                    self.mm(self.acc_region(pair, j), t[:, so + i * 128:so + (i + 1) * 128], self.xnT[:, k, :],
                            start=False, stop=(k == 15), skip=True)

    def stream_down(self, wsc, nk, lhs, pair):
        for k in range(nk):
            t = self.wslot_load([(0, wsc[k * 128:(k + 1) * 128, :])])
            for n in range(4):
                self.mm(self.ps[pair * 2 + n // 2][:, n % 2, :], lhs[:, k, :], t[:, n * 512:(n + 1) * 512],
                        start=(k == 0), stop=(k == nk - 1))

    def norm_T(self, gname, dst=None):
        dst = self.xnT if dst is None else dst
        ps = self.ps
        c = self.cols
        self.act(self.xn[:], self.xt[:], AF.Square, accum=c[:, 0:1])
        self.rsqrt(c[:, 1:2], c[:, 0:1], 1.0 / D, 1e-6)
        self.ts(self.xn[:], self.xt[:], c[:, 1:2], ALU.mult)
        for half in range(2):
            for q in range(4):
                for j in range(2):
                    k = half * 8 + q * 2 + j
                    self.tr(ps[q][:, j, 0:128], self.xn[:, k * 128:(k + 1) * 128])
            for q in range(4):
                k0 = half * 8 + q * 2
                self.tt(dst[:, k0:k0 + 2, :], ps[q][:, :, 0:128],
                        self.bc(self.P(gname)[:, k0:k0 + 2].unsqueeze(2), (128, 2, 128)), ALU.mult)

    def solve(self, A0, B0, A1, B1, X, n, nlev, Xb=None, xparts=None):
        ps = self.ps
        A, B, An, Bn = A0, B0, A1, B1
        xparts = xparts or []
        if Xb is None:
            Xb = X
        for (xa, xb) in xparts:
            self.cp(xb, xa, eng='act')
        shadow = len(xparts) > 0
        for j in range(nlev):
            if n == 256:
                for h in range(8):
                    self.mm(ps[h // 4][:, (h % 4) // 2, (h % 2) * 256:(h % 2) * 256 + 256], B[:, h, :], Xb[:, h, :])
                for q in range(2):
                    pq = ps[q][:].rearrange("p a b -> p (a b)")
                    xq = X[:, 4 * q:4 * q + 4, :].rearrange("p a b -> p (a b)")
                    if j == 0:
                        self.stt(xq, pq, -1.0, xq, ALU.mult, ALU.add)
                    else:
                        self.tt(xq, pq, xq, ALU.add)
            else:
                bank = j % 2
                for h in range(8):
                    self.mm(ps[0][:, bank, h * 64:(h + 1) * 64], B[:, h, :], Xb[:, h, :])
                pf = ps[0][:, bank, :]
                xf = X.rearrange("p a b -> p (a b)")
                if shadow and j < nlev - 1:
                    xbf = Xb.rearrange("p a b -> p (a b)")
                    if j == 0:
                        self.stt(xbf, pf, -1.0, xf, ALU.mult, ALU.add)
                    else:
                        self.tt(xbf, pf, xf, ALU.add)
                if j == 0:
                    self.stt(xf, pf, -1.0, xf, ALU.mult, ALU.add)
                else:
                    self.tt(xf, pf, xf, ALU.add)
            if j < nlev - 1:
                for h in range(8):
                    self.mm(ps[2][:, h // 4, (h % 4) * 128:(h % 4) * 128 + 128], B[:, h, :], A[:, h, :])
                for h in range(8):
                    self.mm(ps[3][:, h // 4, (h % 4) * 128:(h % 4) * 128 + 128], A[:, h, :], B[:, h, :])
                self.cp(An, ps[2][:].rearrange("p a (b c) -> p (a b) c", b=4), eng='pool' if False else 'act')
                self.cp(Bn, ps[3][:].rearrange("p a (b c) -> p (a b) c", b=4), eng='dve')
                A, B, An, Bn = An, Bn, A, B

    def rowbc(self, dst_ps, src, n, scratch):
        self.tt(scratch, self.bc(src.unsqueeze(2), (128, n, 128)),
                self.bc(self.C('ident').unsqueeze(1), (128, n, 128)), ALU.mult)
        flat = scratch.rearrange("p a b -> p (a b)")
        for q in range((n * 128 + 511) // 512):
            w = min(512, n * 128 - q * 512)
            self.mm(dst_ps[:, q, 0:w], self.C('ones'), flat[:, q * 512:q * 512 + w])

    def tile(self, kind, x_src, y_dst, full, first, last, sample_state, pair_slot=None):
        nseq, L = (1, 128) if kind == 'p' else (16, 8)
        nlev = 7 if kind == 'p' else 3
        ps, g, c = self.ps, self.g, self.cols
        Lh3, Lh1, Lh2 = L + 3, L + 1, L + 2
        xpv = self.xp[:, 0:24 * nseq * Lh3].rearrange("p (c s t) -> p c s t", c=24, s=nseq)
        ppv = self.pp[:, 0:26 * nseq * Lh1].rearrange("p (c s t) -> p c s t", c=26, s=nseq)
        M = lambda n: self.C(n + '_' + kind)

        self.dma(self.xt[:], x_src)
        self.norm_T('ln1g')

        if self.STOP == 'A':
            return
        if kind == 'p':
            if first:
                self.memset(self.xhist[:], 0.0)
                self.memset(self.phist[:], 0.0)
                self.memset(self.Sg[:], 0.0)
                self.memset(self.Hb[:], 0.0)
                self.memset(self.fhist[:], 0.0)
            self.cp(xpv[:, :, 0, 0:3], self.xhist[:])
            self.cp(ppv[:, :, 0, 0:1], self.phist[:])
        else:
            self.dma(self.st48[0:48, 0:3072], self.sgc)
            for cg in range(3):
                for j in range(8):
                    cc = cg * 8 + j
                    self.tr(ps[j // 4][:, (j % 4) // 2, (j % 2) * 48:(j % 2) * 48 + 48],
                            self.st48[0:48, cc * 128:(cc + 1) * 128])
                for j in range(8):
                    cc = cg * 8 + j
                    self.cp(xpv[:, cc, :, 0:3],
                            ps[j // 4][:, (j % 4) // 2, (j % 2) * 48:(j % 2) * 48 + 48].rearrange("p (s t) -> p s t", s=16))
            self.dma(self.st48[0:16, 0:3328], self.ssh)
            for cg in range(4):
                for j in range(8):
                    cc = cg * 8 + j
                    if cc < 26:
                        self.tr(ps[j // 4][:, (j % 4) // 2, (j % 2) * 16:(j % 2) * 16 + 16],
                                self.st48[0:16, cc * 128:(cc + 1) * 128])
                for j in range(8):
                    cc = cg * 8 + j
                    if cc < 26:
                        self.cp(ppv[:, cc, :, 0:1],
                                ps[j // 4][:, (j % 4) // 2, (j % 2) * 16:(j % 2) * 16 + 16].unsqueeze(2))

        if self.STOP == 'A2':
            return
        p8v = lambda t: t[:].rearrange("p a (b t) -> p (a b) t", b=4)
        s4 = lambda a_: a_.rearrange("p c (s t) -> p c s t", s=nseq)
        RW = 4112
        if full:
            self.stream_up(self.c_in, [(0, 2048, 0)], 0)
            for q in range(2):
                self.cp(xpv[:, 8 * q:8 * q + 8, :, 3:3 + L], s4(p8v(ps[q])), eng='act' if q == 0 else 'dve')
            self.stream_up(self.c_in, [(2048, 2048, 0)], 1)
            self.cp(xpv[:, 16:24, :, 3:3 + L], s4(p8v(ps[2])), eng='act')
            self.cp(self.zT[:], p8v(ps[3]), eng='dve')
            self.stream_up(self.c_in, [(RW, 2048, 0)], 0)
            for q in range(2):
                self.cp(ppv[:, 8 * q:8 * q + 8, :, 1:1 + L], s4(p8v(ps[q])), eng='act' if q == 0 else 'dve')
            self.stream_up(self.c_in, [(RW + 2048, 1280, 0)], 1)
            self.cp(ppv[:, 16:24, :, 1:1 + L], s4(p8v(ps[2])), eng='act')
            self.cp(ppv[:, 24:26, :, 1:1 + L], s4(p8v(ps[3])[:, 0:2, :]), eng='dve')
        else:
            self.stream_up(self.c_in, [(1024, 2048, 0)], 0)
            for q in range(2):
                self.cp(xpv[:, 8 + 8 * q:16 + 8 * q, :, 3:3 + L], s4(p8v(ps[q])), eng='act' if q == 0 else 'dve')
            self.stream_up(self.c_in, [(RW + 1024, 2048, 0)], 1)
            for q in range(2):
                self.cp(ppv[:, 8 + 8 * q:16 + 8 * q, :, 1:1 + L], s4(p8v(ps[2 + q])), eng='act' if q == 0 else 'dve')
            self.stream_up(self.c_in, [(RW + 3072, 128, 0)], 0)
            self.cp(ppv[:, 24:25, :, 1:1 + L], s4(p8v(ps[0])[:, 0:1, :]), eng='act')
        self.dma(self.wba[:], self.c_in[:, 4096:4112].rearrange("(k p) c -> p k c", p=128), eng='sp')
        for k in range(16):
            self.mm(ps[0][:, 0, 0:16], self.xnT[:, k, :], self.wba[:, k, :], start=(k == 0), stop=(k == 15))
        ba = c[:, 8:24]
        self.cp(ba, ps[0][:, 0, 0:16])

        if self.STOP == 'B':
            return
        cw = self.P('convw').rearrange("p (c i) -> p c i", i=4)
        qv = self.qkv[:].rearrange("p c (s t) -> p c s t", s=nseq)
        tmp = self.scrA[:, 0:3072].rearrange("p (c s t) -> p c s t", c=24, s=nseq)
        wsh = (128, 24, nseq, L)
        sq = self.scrA[:, 0:2048].rearrange("p (c t) -> p c t", c=16)
        sqf = self.scrA[:, 0:2048]
        rn = g[0]
        qn, kn, va = self.qkv[:, 0:8, :], self.qkv[:, 8:16, :], self.qkv[:, 16:24, :]
        bet, gcol, Gc, gam, bgc, Gl, ekl, tmpc = (c[:, 32:40], c[:, 40:48], c[:, 48:56], c[:, 56:64], c[:, 64:72],
                                                  c[:, 72:80], c[:, 80:88], c[:, 88:96])
        scr8 = g[1][:].rearrange("p (h t) -> p h t", h=8)
        tdf = g[2][:].rearrange("p (h t) -> p h t", h=8)
        Dm = g[3][:].rearrange("p (h t) -> p h t", h=8)
        DTm = g[4][:].rearrange("p (h t) -> p h t", h=8)

        def chain_conv():
            if kind == 'p':
                self.cp(self.xhist[:], xpv[:, :, 0, L:L + 3])
                yield
            self.tt(qv, xpv[:, :, :, 3:3 + L], self.bc(cw[:, :, 3:4].unsqueeze(3), wsh), ALU.mult)
            yield
            for i in range(3):
                self.tt(tmp, xpv[:, :, :, i:i + L], self.bc(cw[:, :, i:i + 1].unsqueeze(3), wsh), ALU.mult,
                        eng=self.pool_eng)
                yield
                self.tt(qv, qv, tmp, ALU.add)
                yield
            self.act(self.qkv[:], self.qkv[:], AF.Silu)
            yield
            self.tt(sq, self.qkv[:, 0:16, :], self.qkv[:, 0:16, :], ALU.mult)
            yield
            for q in range(4):
                self.mm(ps[q // 2 + 2][:, q % 2, :], self.C('ones'), sqf[:, q * 512:(q + 1) * 512])
            yield
            for part in range(2):
                rv = rn[:].rearrange("p (c t) -> p c t", c=8)
                self.rsqrt(rv, ps[2 + part][:].rearrange("p a (b t) -> p (a b) t", b=4), 1.0, 1e-12)
                yield
                if part == 0:
                    self.stt(self.qkv[:, 0:8, :], self.qkv[:, 0:8, :], 128.0 ** -0.5, rv, ALU.mult, ALU.mult)
                else:
                    self.tt(self.qkv[:, 8:16, :], self.qkv[:, 8:16, :], rv, ALU.mult)
                yield

        def chain_cols():
            self.act(bet, ba[:, 0:8], AF.Exp, scale=-1.0)
            self.ts(bet, bet, 1.0, ALU.add)
            self.rcp(bet, bet)
            self.tt(tmpc, ba[:, 8:16], self.P('dtb'), ALU.add)
            yield
            self.act(tmpc, tmpc, AF.Exp)
            self.act(tmpc, tmpc, AF.Ln, bias=1.0)
            self.act(gcol, self.P('alog'), AF.Exp)
            yield
            self.stt(gcol, gcol, -1.0, tmpc, ALU.mult, ALU.mult)
            self.mm(ps[0][:, 0, 0:8], M('causT'), gcol)
            self.mm(ps[0][:, 0, 8:16], M('seq'), gcol)
            yield
            self.cp(Gc, ps[0][:, 0, 0:8])
            self.cp(Gl, ps[0][:, 0, 8:16])
            yield
            self.act(gam, Gc, AF.Exp)
            self.tt(bgc, bet, gam, ALU.mult)
            self.tt(ekl, Gl, Gc, ALU.subtract)
            self.act(ekl, ekl, AF.Exp)
            yield
            self.rowbc(ps[0], Gc, 8, scr8)
            yield
            self.tt(tdf, ps[0][:].rearrange("p a (b t) -> p (a b) t", b=4), self.bc(Gc.unsqueeze(2), (128, 8, 128)),
                    ALU.subtract)
            yield
            self.ts(Dm, tdf, 0.0, ALU.max, -1.0, ALU.mult)
            self.ts(DTm, tdf, 0.0, ALU.min)
            yield
            self.act(Dm, Dm, AF.Exp)
            self.act(DTm, DTm, AF.Exp)
            yield
            self.tt(Dm, Dm, self.bc(M('caus').unsqueeze(1), (128, 8, 128)), ALU.mult)
            self.tt(DTm, DTm, self.bc(M('causT').unsqueeze(1), (128, 8, 128)), ALU.mult)
            yield

        gens = [chain_cols(), chain_conv()]
        while gens:
            for gen_ in list(gens):
                try:
                    next(gen_)
                except StopIteration:
                    gens.remove(gen_)
        if self.STOP == 'C2':
            return
        for h in range(8):
            self.mm(ps[1][:, h // 4, (h % 4) * 128:(h % 4) * 128 + 128], kn[:, h, :], kn[:, h, :])
        A0 = g[5][:].rearrange("p (h t) -> p h t", h=8)
        B0 = g[6][:].rearrange("p (h t) -> p h t", h=8)
        A1 = g[7][:].rearrange("p (h t) -> p h t", h=8)
        B1 = g[8][:].rearrange("p (h t) -> p h t", h=8)
        self.tt(A0, ps[1][:].rearrange("p a (b t) -> p (a b) t", b=4), Dm, ALU.mult)
        self.tt(A0, A0, self.bc(bet.unsqueeze(2), (128, 8, 128)), ALU.mult)
        self.tt(A0, A0, self.bc(M('strict').unsqueeze(1), (128, 8, 128)), ALU.mult)
        for h in range(8):
            self.tr(ps[2][:, h // 4, (h % 4) * 128:(h % 4) * 128 + 128], A0[:, h, :])
        self.cp(B0, ps[2][:].rearrange("p a (b t) -> p (a b) t", b=4))
        X = g[9:11]
        Xv = [X[0][:].rearrange("p (h t) -> p h t", h=4), X[1][:].rearrange("p (h t) -> p h t", h=4)]
        ktm = g[11][:].rearrange("p (h t) -> p h t", h=8)
        for h in range(8):
            self.tr(ps[1][:, h // 4, (h % 4) * 128:(h % 4) * 128 + 128], kn[:, h, :])
        for h in range(8):
            self.tr(ps[3][:, h // 4, (h % 4) * 128:(h % 4) * 128 + 128], va[:, h, :])
        for q in range(2):
            self.tt(Xv[q][:, :, 0:128], ps[1][:, q, :].rearrange("p (h t) -> p h t", h=4),
                    self.bc(bgc[:, 4 * q:4 * q + 4].unsqueeze(2), (128, 4, 128)), ALU.mult)
            self.tt(Xv[q][:, :, 128:256], ps[3][:, q, :].rearrange("p (h t) -> p h t", h=4),
                    self.bc(bet[:, 4 * q:4 * q + 4].unsqueeze(2), (128, 4, 128)), ALU.mult)
        self.tt(ktm, ps[1][:].rearrange("p a (b t) -> p (a b) t", b=4), self.bc(ekl.unsqueeze(2), (128, 8, 128)),
                ALU.mult)

        class XW:
            def __getitem__(s2, idx):
                _, hs, cs = idx
                if isinstance(hs, int):
                    return Xv[hs // 4][:, hs % 4, cs]
                q = hs.start // 4
                return Xv[q][:, :, cs]
        XX = XW()
        self.solve(A0, B0, A1, B1, XX, 256, nlev)
        if self.STOP == 'D':
            return
        so_ = (full == 0)
        qkT = g[5][:].rearrange("p (h t) -> p h t", h=8)
        if not so_:
            for h in range(8):
                self.mm(ps[2][:, h // 4, (h % 4) * 128:(h % 4) * 128 + 128], kn[:, h, :], qn[:, h, :])
            self.tt(qkT, ps[2][:].rearrange("p a (b t) -> p (a b) t", b=4), DTm, ALU.mult)
        WT = g[6][:].rearrange("p (h t) -> p h t", h=8)
        for h in range(8):
            self.tr(ps[3][:, h // 4, (h % 4) * 128:(h % 4) * 128 + 128], XX[:, h, 0:128])
        self.cp(WT, ps[3][:].rearrange("p a (b t) -> p (a b) t", b=4), eng='act')
        qgT = g[7][:].rearrange("p (h t) -> p h t", h=8)
        if not so_:
            self.rowbc(ps[0], gam, 8, scr8)
            self.tt(qgT, ps[0][:].rearrange("p a (b t) -> p (a b) t", b=4), qn, ALU.mult)
        u = g[8][:].rearrange("p (h t) -> p h t", h=8)
        pws = ps[0][:].rearrange("p a (b t) -> p (a b) t", b=4)
        po = ps[1][:].rearrange("p a (b t) -> p (a b) t", b=4)
        zz = self.zeros
        if not so_:
            for a in range(2):
                self.mm(ps[1][:, a, :], zz[:, 0:128], zz[:, 0:512], start=True, stop=False, skip=True)
        if kind == 'p':
            for h in range(8):
                self.mm(pws[:, h, :], WT[:, h, :], self.Sg[:, h, :])
            if not so_:
                for h in range(8):
                    self.mm(po[:, h, :], qgT[:, h, :], self.Sg[:, h, :], start=False, stop=False, skip=True)
        else:
            WTm = g[1][:].rearrange("p (h t) -> p h t", h=8)
            qgTm = g[2][:].rearrange("p (h t) -> p h t", h=8)
            self.memset(WTm, 0.0)
            self.memset(qgTm, 0.0)
            for a in range(2):
                self.mm(ps[0][:, a, :], zz[:, 0:128], zz[:, 0:512], start=True, stop=False, skip=True)
            for s in range(16):
                Sb = (self.Sg, self.Hb)[s % 2]
                self.dma(Sb[:], self.sg[s].rearrange("h k v -> k h v"))
                if s > 0:
                    self.memset(WTm[:, :, 8 * s - 8:8 * s], 0.0)
                    self.memset(qgTm[:, :, 8 * s - 8:8 * s], 0.0)
                self.cp(WTm[:, :, 8 * s:8 * s + 8], WT[:, :, 8 * s:8 * s + 8])
                self.cp(qgTm[:, :, 8 * s:8 * s + 8], qgT[:, :, 8 * s:8 * s + 8])
                for h in range(8):
                    self.mm(pws[:, h, :], WTm[:, h, :], Sb[:, h, :], start=False, stop=False, skip=True)
                for h in range(8):
                    self.mm(po[:, h, :], qgTm[:, h, :], Sb[:, h, :], start=False, stop=False, skip=True)
        for q in range(2):
            self.tt(u[:, 4 * q:4 * q + 4, :], Xv[q][:, :, 128:256], pws[:, 4 * q:4 * q + 4, :], ALU.subtract)
        if not so_:
            for h in range(8):
                self.mm(po[:, h, :], qkT[:, h, :], u[:, h, :], start=False, stop=True, skip=True)
            o = g[5][:].rearrange("p (h t) -> p h t", h=8)
            self.cp(o, po, eng='act')
            osq = g[6][:].rearrange("p (h t) -> p h t", h=8)
            self.tt(osq, o, o, ALU.mult)
            self.reduce(tmpc, osq)
            self.rsqrt(tmpc, tmpc, 1.0 / 128, 1e-6)
            self.tt(o, o, self.bc(tmpc.unsqueeze(2), (128, 8, 128)), ALU.mult)
            for h in range(8):
                self.tr(ps[2][:, h // 4, (h % 4) * 128:(h % 4) * 128 + 128], o[:, h, :])
            self.act(self.zT[:], self.zT[:], AF.Silu)
            oT = g[6][:].rearrange("p (h t) -> p h t", h=8)
            self.ts(oT, ps[2][:].rearrange("p a (b t) -> p (a b) t", b=4), self.P('gng')[:, 0:1], ALU.mult)
            self.tt(self.mixedT[:, 0:8, :], oT, self.zT[:], ALU.mult)
        pS = ps[3][:].rearrange("p a (b t) -> p (a b) t", b=4)
        if kind == 'p':
            for h in range(8):
                self.mm(pS[:, h, :], ktm[:, h, :], u[:, h, :])
            self.act(tmpc, Gl, AF.Exp)
            self.tt(self.Sg[:], self.Sg[:], self.bc(tmpc.unsqueeze(2), (128, 8, 128)), ALU.mult)
            self.tt(self.Sg[:], self.Sg[:], pS, ALU.add)
            if last:
                self.dma(self.o_gdn_p.rearrange("h k v -> k h v"), self.Sg[:])
        else:
            egl = c[:, 96:104]
            self.act(egl, Gl, AF.Exp)
            glall = c[:, 112:240].rearrange("p (h s) -> p h s", h=8)
            gsc = g[3][:, 0:128].rearrange("p (h s) -> p h s", h=8)
            self.tt(gsc, self.bc(egl.unsqueeze(2), (128, 8, 16)), self.bc(M('rowm').unsqueeze(1), (128, 8, 16)), ALU.mult)
            self.mm(ps[0][:, 0, 0:128], self.C('ones'), g[3][:, 0:128])
            self.ts(glall, ps[0][:, 0, 0:128].rearrange("p (h s) -> p h s", h=8), 1.0 / 8, ALU.mult)
            v8_ = lambda t: t[:].rearrange("p (h t) -> p h t", h=8)
            for s in range(16):
                b = s % 2
                Sb = (self.Sg, self.Hb)[b]
                ktmm = v8_((g[1], g[4])[b])
                outb = v8_((g[0], g[2])[b])
                pSb = (ps[3], ps[2])[b][:].rearrange("p a (b t) -> p (a b) t", b=4)
                self.dma(Sb[:], self.sg[s].rearrange("h k v -> k h v"))
                self.act(ktmm, ktm, AF.Copy, scale=M('rowm')[:, s:s + 1])
                for h in range(8):
                    self.mm(pSb[:, h, :], ktmm[:, h, :], u[:, h, :])
                self.tt(outb, Sb[:], self.bc(glall[:, :, s:s + 1], (128, 8, 128)), ALU.mult)
                self.tt(outb, outb, pSb, ALU.add)
                self.dma(self.o_gdn_s[s].rearrange("h k v -> k h v"), outb)
        if kind == 's' or last:
            nr = 3 * nseq
            cst = self.qkv[:].rearrange("p c t -> p (c t)")[:, 0:24 * nr].rearrange("p (c s r) -> p c s r", c=24, s=nseq)
            self.cp(cst, xpv[:, :, :, L:L + 3])
            cst2 = self.qkv[:].rearrange("p c t -> p (c t)")[:, 0:24 * nr].rearrange("p (c r) -> p c r", c=24)
            for cg in range(3):
                for j in range(8):
                    cc = cg * 8 + j
                    self.tr(ps[j // 4][0:nr, (j % 4) // 2, (j % 2) * 128:(j % 2) * 128 + 128],
                            cst2[:, cc, :])
                for j in range(8):
                    cc = cg * 8 + j
                    self.cp(self.st48[0:nr, cc * 128:(cc + 1) * 128],
                            ps[j // 4][0:nr, (j % 4) // 2, (j % 2) * 128:(j % 2) * 128 + 128])
            self.dma(self.o_gconv_p if kind == 'p' else self.o_gconv_s, self.st48[0:nr, 0:3072])

        if self.STOP == 'E':
            return
        self.rwkv(kind, nseq, L, nlev, ppv, last, so_=(full == 0))

        if not full:
            return
        if self.DBG:
            self.cp(self.xn[:], self.mixedT[:].rearrange("p a b -> p (a b)"))
            self.dma(self.dbg[0 if kind == 'p' else 1], self.xn[:])
        if self.STOP == 'F':
            return
        self.stream_down(self.c_o, 16, self.mixedT, 0)
        for q in range(2):
            self.tt(self.xt[:, q * 1024:(q + 1) * 1024], self.xt[:, q * 1024:(q + 1) * 1024],
                    ps[q][:].rearrange("p a b -> p (a b)"), ALU.add)
        if self.STOP == 'G':
            return
        if pair_slot is not None:
            self.norm_T('ln2g', dst=self.xnT2[:, :, pair_slot * 128:(pair_slot + 1) * 128])
            return
        self.ffn(kind, nseq, L, last, hist_only=(full == 1))
        if full == 1:
            self.ts(self.fhist[:], self.fhist[:], self.P('hflag')[:, 0:1], ALU.mult)
            return
        self.final(y_dst)

    def final(self, y_dst):
        c = self.cols
        self.act(self.xn[:], self.xt[:], AF.Square, accum=c[:, 0:1])
        self.rsqrt(c[:, 1:2], c[:, 0:1], 1.0 / D, 1e-6)
        self.stt(self.xn[:], self.xt[:], c[:, 1:2], self.fg[:], ALU.mult, ALU.mult)
        self.dma(y_dst, self.xn[:])

    def rwkv(self, kind, nseq, L, nlev, ppv, last, so_=False):
        ps, g, c = self.ps, self.g, self.cols
        M = lambda n: self.C(n + '_' + kind)
        T = 128
        v8 = lambda t: t[:].rearrange("p (h t) -> p h t", h=8)
        if kind == 'p':
            self.cp(self.phist[:], ppv[:, :, 0, L:L + 1])
        if kind == 's' or last:
            for cg in range(4):
                for j in range(8):
                    cc = cg * 8 + j
                    if cc < 26:
                        self.tr(ps[j // 4][0:nseq, (j % 4) // 2, (j % 2) * 128:(j % 2) * 128 + 128],
                                ppv[:, cc, :, L])
                for j in range(8):
                    cc = cg * 8 + j
                    if cc < 26:
                        self.cp(self.st48[0:nseq, cc * 128:(cc + 1) * 128],
                                ps[j // 4][0:nseq, (j % 4) // 2, (j % 2) * 128:(j % 2) * 128 + 128])
            self.dma(self.o_shift_p if kind == 'p' else self.o_shift_s, self.st48[0:nseq, 0:3328])
        xs = self.scrA[:, 0:26 * 128].rearrange("p (c s t) -> p c s t", c=26, s=nseq)
        xsf = self.scrA[:, 0:26 * 128].rearrange("p (c t) -> p c t", c=26)
        self.tt(xs, ppv[:, :, :, 0:L], ppv[:, :, :, 1:1 + L], ALU.subtract)
        self.tt(xsf, xsf, self.bc(self.P('mu').unsqueeze(2), (128, 26, 128)), ALU.mult)
        self.tt(xs, xs, ppv[:, :, :, 1:1 + L], ALU.add)
        r, k, v = xsf[:, 0:8, :], xsf[:, 8:16, :], xsf[:, 16:24, :]
        twd = g[0]
        self.act(twd[0:64, 0:128], xsf[0:64, 24, :], AF.Tanh)
        for ch in range(8):
            self.mm(ps[0][:, ch // 4, (ch % 4) * 128:(ch % 4) * 128 + 128], self.wab[0:64, ch * 128:(ch + 1) * 128],
                    twd[0:64, 0:128])
        for ch in range(8):
            self.mm(ps[1][:, ch // 4, (ch % 4) * 128:(ch % 4) * 128 + 128], self.wab[64:128, ch * 128:(ch + 1) * 128],
                    xsf[64:128, 24, :])
        p8 = lambda t: t[:].rearrange("p a (b t) -> p (a b) t", b=4)
        ew = v8(g[1])
        self.tt(ew, p8(ps[0]), self.bc(self.P('w0').unsqueeze(2), (128, 8, 128)), ALU.add)
        self.act(ew, ew, AF.Exp, scale=-1.0)
        self.act(ew, ew, AF.Ln, bias=1.0)
        self.act(ew, ew, AF.Exp, scale=-1.0, bias=-0.5)
        av = v8(g[2])
        self.tt(av, p8(ps[1]), self.bc(self.P('a0').unsqueeze(2), (128, 8, 128)), ALU.add)
        self.act(av, av, AF.Sigmoid)
        sgd = g[0]
        gate = v8(g[3])
        if not so_:
            self.act(sgd[:, 128:256], xsf[:, 25, :], AF.Sigmoid)
            for ch in range(8):
                self.mm(ps[2][:, ch // 4, (ch % 4) * 128:(ch % 4) * 128 + 128], self.gb[:, ch * 128:(ch + 1) * 128],
                        sgd[:, 128:256])
            self.cp(gate, p8(ps[2]), eng='act')
        kkv = v8(g[4])
        self.tt(kkv, k, self.bc(self.P('kk').unsqueeze(2), (128, 8, 128)), ALU.mult)
        sq = v8(g[5])
        self.tt(sq, kkv, kkv, ALU.mult)
        for q in range(2):
            self.mm(ps[3][:, q, :], self.C('blk'), g[5][:, q * 512:(q + 1) * 512])
        rn = v8(g[5])
        self.rsqrt(rn, p8(ps[3]), 1.0, 1e-12)
        self.tt(kkv, kkv, rn, ALU.mult)
        k2 = v8(g[5])
        self.ts(k2, av, -1.0, ALU.add)
        self.tt(k2, k2, self.bc(self.P('ka').unsqueeze(2), (128, 8, 128)), ALU.mult)
        self.stt(k2, k2, 1.0, k, ALU.add, ALU.mult)
        bon = v8(g[6])
        if not so_:
            self.tt(bon, r, k2, ALU.mult)
            self.tt(bon, bon, self.bc(self.P('rk').unsqueeze(2), (128, 8, 128)), ALU.mult)
            for q in range(2):
                self.mm(ps[0][:, q, :], self.C('blk'), g[6][:, q * 512:(q + 1) * 512])
            self.tt(bon, p8(ps[0]), v, ALU.mult)
        if self.STOP == 'E1':
            return
        cs = v8(g[7])
        for ch in range(8):
            self.scan(cs[:, ch, :], M('seg'), ew[:, ch, :])
        at = v8(g[8])
        rt = v8(g[9])
        bt = v8(g[10])
        kt = v8(g[11])
        e1 = v8(g[12])
        self.tt(e1, cs, ew, ALU.subtract)
        self.act(e1, e1, AF.Exp, scale=-1.0)
        self.tt(at, kkv, e1, ALU.mult)
        self.act(e1, cs, AF.Exp, scale=-1.0)
        self.tt(rt, r, e1, ALU.mult)
        self.act(e1, cs, AF.Exp)
        self.tt(bt, kkv, av, ALU.mult)
        self.tt(bt, bt, e1, ALU.mult)
        self.tt(kt, k2, e1, ALU.mult)
        pc = c[:, 112:112 + 8 * nseq].rearrange("p (h s) -> p h s", h=8)
        csv = g[7][:].rearrange("p (h s t) -> p h s t", h=8, s=nseq)
        self.act(pc, csv[:, :, :, L - 1], AF.Exp, scale=-1.0)
        pcb = self.bc(pc.unsqueeze(3), (128, 8, nseq, L))
        bh = v8(g[12])
        kh = v8(g[4])
        v4 = lambda t: t[:].rearrange("p (h s t) -> p h s t", h=8, s=nseq)
        self.tt(v4(g[12]), v4(g[10]), pcb, ALU.mult)
        self.tt(v4(g[4]), v4(g[11]), pcb, ALU.mult)
        Vtm, Bh, Kh = g[0], g[1], g[2]
        for (dst, src) in ((Vtm, v), (Bh, bh), (Kh, kh)):
            for ch in range(8):
                self.tr(ps[1][:, ch // 4, (ch % 4) * 128:(ch % 4) * 128 + 128], src[:, ch, :])
            self.cp(dst[:], ps[1][:].rearrange("p a b -> p (a b)"), eng='act')
        if self.STOP == 'E2':
            return
        ytm = self.qkv[:, 0:8, :]
        Utm = self.qkv[:, 8:16, :]
        ytf = self.qkv[:, 0:8, :].rearrange("p a b -> p (a b)")
        Utf = self.qkv[:, 8:16, :].rearrange("p a b -> p (a b)")
        Sld = self.qkv[:, 16:24, :]
        AkT = self.xp[:, 0:2048].rearrange("p (h t) -> p h t", h=16)
        RkT = self.xp[:, 2048:4096].rearrange("p (h t) -> p h t", h=16)
        RbT = self.pp[:, 0:2048].rearrange("p (h t) -> p h t", h=16)
        hb = lambda t, i: t[:].bitcast(BF16)[:, i * 1024:(i + 1) * 1024].rearrange("p (h t) -> p h t", h=8)
        A0, B0 = hb(g[5], 0), hb(g[5], 1)
        bonz = self.zT[:]
        if not so_:
            self.cp(bonz, bon, eng='pool')
        A1, B1 = hb(g[7], 0), hb(g[7], 1)
        Xb8 = g[12][:].bitcast(BF16)[:, 0:512].rearrange("p (h t) -> p h t", h=8)
        if kind == 's':
            atm = self.pp[:, 2048:3072].rearrange("p (h t) -> p h t", h=8)
            rtm = self.scrA[:, 0:1024].rearrange("p (h t) -> p h t", h=8)
        for half in range(2):
            hs = range(half * 8, half * 8 + 8)
            def slot(t, i):
                return t[:, i % 2, (i // 2) * 128:(i // 2) * 128 + 128]
            pperm = lambda t: t[:].rearrange("p a (b t) -> p a b t", b=4)
            dperm = lambda d: d.rearrange("p (i two) t -> p two i t", two=2)
            m8 = lambda n: self.bc(M(n).unsqueeze(1).unsqueeze(1), (128, 2, 4, 128))
            for i, h in enumerate(hs):
                ch, o64 = h // 2, (h % 2) * 64
                sl = slice(o64, o64 + 64)
                self.mm(slot(ps[0], i), bt[sl, ch, :], at[sl, ch, :])
                if not so_:
                    self.mm(slot(ps[1], i), bt[sl, ch, :], rt[sl, ch, :])
                self.mm(slot(ps[2], i), at[sl, ch, :], bt[sl, ch, :])
            self.tt(dperm(B0), pperm(ps[0]), m8('strictT'), ALU.mult)
            if not so_:
                self.tt(dperm(RbT[:, half * 8:half * 8 + 8, :]), pperm(ps[1]), m8('causT'), ALU.mult)
            self.tt(dperm(A0), pperm(ps[2]), m8('strict'), ALU.mult)
            for i, h in enumerate(hs):
                ch, o64 = h // 2, (h % 2) * 64
                sl = slice(o64, o64 + 64)
                self.mm(slot(ps[0], i), kt[sl, ch, :], at[sl, ch, :])
                if not so_:
                    self.mm(slot(ps[1], i), kt[sl, ch, :], rt[sl, ch, :])
            self.tt(dperm(AkT[:, half * 8:half * 8 + 8, :]), pperm(ps[0]), m8('strictT'), ALU.mult)
            if not so_:
                self.tt(dperm(RkT[:, half * 8:half * 8 + 8, :]), pperm(ps[1]), m8('causT'), ALU.mult)
            prh = ps[1][:, 0, :]
            py = ps[1][:, 1, :]
            zz = self.zeros
            self.mm(prh, zz[:, 0:128], zz[:, 0:512], start=True, stop=False, skip=True)
            if not so_:
                self.mm(py, zz[:, 0:128], zz[:, 0:512], start=True, stop=False, skip=True)
            if kind == 'p':
                for cc in range(4):
                    ch = half * 4 + cc
                    self.mm(prh[:, cc * 128:(cc + 1) * 128], at[:, ch, :], self.Hb[:, ch, :], start=False, stop=False,
                            skip=True)
                    if not so_:
                        self.mm(py[:, cc * 128:(cc + 1) * 128], rt[:, ch, :], self.Hb[:, ch, :], start=False,
                                stop=False, skip=True)
            else:
                if half == 0:
                    self.memset(atm, 0.0)
                    self.memset(rtm, 0.0)
                for s in range(16):
                    Hs = (self.Hb, self.Sg)[s % 2]
                    self.dma(Hs[:], self.sr[s])
                    if s > 0:
                        self.memset(atm[:, :, 8 * s - 8:8 * s], 0.0)
                        self.memset(rtm[:, :, 8 * s - 8:8 * s], 0.0)
                    self.cp(atm[:, :, 8 * s:8 * s + 8], at[:, :, 8 * s:8 * s + 8])
                    self.cp(rtm[:, :, 8 * s:8 * s + 8], rt[:, :, 8 * s:8 * s + 8])
                    for cc in range(4):
                        ch = half * 4 + cc
                        self.mm(prh[:, cc * 128:(cc + 1) * 128], atm[:, ch, :], Hs[:, ch, :], start=False,
                                stop=False, skip=True)
                        self.mm(py[:, cc * 128:(cc + 1) * 128], rtm[:, ch, :], Hs[:, ch, :], start=False,
                                stop=False, skip=True)
                if half == 0:
                    self.memset(atm[:, :, 120:128], 0.0)
                    self.memset(rtm[:, :, 120:128], 0.0)
            for i, h in enumerate(hs):
                self.mm(prh[:, i * 64:(i + 1) * 64], AkT[:, h, :], Vtm[:, h * 64:(h + 1) * 64], start=False,
                        stop=True, skip=True)
            X = Utm[:, half * 4:half * 4 + 4, :].rearrange("p a (b c) -> p (a b) c", b=2)
            self.ts(X, prh.rearrange("p (h c) -> p h c", h=8), -1.0, ALU.mult)
            self.solve(A0, B0, A1, B1, X, 64, nlev, Xb=Xb8, xparts=[(X, Xb8)])
            for i, h in enumerate(hs if not so_ else []):
                self.mm(py[:, i * 64:(i + 1) * 64], RbT[:, h, :], Utf[:, h * 64:(h + 1) * 64], start=False, stop=False,
                        skip=True)
                self.mm(py[:, i * 64:(i + 1) * 64], RkT[:, h, :], Vtm[:, h * 64:(h + 1) * 64], start=False, stop=True,
                        skip=True)
            if not so_:
                self.cp(ytf[:, half * 512:(half + 1) * 512], py, eng='act')
        if self.STOP == 'E3':
            return
        if kind == 'p':
            for ch in range(8):
                po = ps[ch // 4][:, (ch % 4) // 2, (ch % 2) * 128:(ch % 2) * 128 + 128]
                self.mm(po, Bh[:, ch * 128:(ch + 1) * 128], Utf[:, ch * 128:(ch + 1) * 128], start=True, stop=False)
                self.mm(po, Kh[:, ch * 128:(ch + 1) * 128], Vtm[:, ch * 128:(ch + 1) * 128], start=False, stop=True)
            hn = v8(g[5])
            for q in range(2):
                self.tt(hn[:, 4 * q:4 * q + 4, :].rearrange("p (a b) t -> p a b t", a=2),
                        ps[q][:, :, 0:256].rearrange("p a (b t) -> p a b t", b=2),
                        self.bc(self.C('blk').unsqueeze(1).unsqueeze(1), (128, 2, 2, 128)), ALU.mult)
            self.tt(self.Hb[:], self.Hb[:], self.bc(pc[:, :, 0:1], (128, 8, 128)), ALU.mult)
            self.tt(self.Hb[:], self.Hb[:], hn, ALU.add)
            if last:
                self.store_H(self.o_rwkv_p)
        else:
            for s in range(16):
                b = s % 2
                Bm, Km = (g[5], g[4])[b], (g[6], g[12])[b]
                hn, t2 = v8((g[7], g[8])[b]), v8((g[9], g[10])[b])
                Hs = (self.Hb, self.Sg)[b]
                self.dma(Hs[:], self.sr[s])
                self.act(Bm[:], Bh[:], AF.Copy, scale=M('rowm')[:, s:s + 1])
                self.act(Km[:], Kh[:], AF.Copy, scale=M('rowm')[:, s:s + 1])
                for ch in range(8):
                    po = ps[2 * b + ch // 4][:, (ch % 4) // 2, (ch % 2) * 128:(ch % 2) * 128 + 128]
                    self.mm(po, Bm[:, ch * 128:(ch + 1) * 128], Utf[:, ch * 128:(ch + 1) * 128], start=True, stop=False)
                    self.mm(po, Km[:, ch * 128:(ch + 1) * 128], Vtm[:, ch * 128:(ch + 1) * 128], start=False, stop=True)
                self.tt(t2, Hs[:], self.bc(pc[:, :, s:s + 1], (128, 8, 128)), ALU.mult)
                for q in range(2):
                    self.tt(hn[:, 4 * q:4 * q + 4, :].rearrange("p (a b) t -> p a b t", a=2),
                            ps[2 * b + q][:, :, 0:256].rearrange("p a (b t) -> p a b t", b=2),
                            self.bc(self.C('blk').unsqueeze(1).unsqueeze(1), (128, 2, 2, 128)), ALU.mult)
                self.tt(hn, hn, t2, ALU.add)
                self.dma(self.o_rwkv_s[s], hn)
        if so_:
            return
        y3 = ytf.rearrange("p (h c) -> p h c", h=16)
        mean = c[:, 0:16]
        var = c[:, 16:32]
        self.reduce(mean, y3)
        self.ts(mean, mean, 1.0 / 64, ALU.mult)
        self.tt(y3, y3, self.bc(mean.unsqueeze(2), (128, 16, 64)), ALU.subtract)
        ysq = g[5][:].rearrange("p (h c) -> p h c", h=16)
        self.tt(ysq, y3, y3, ALU.mult)
        self.reduce(var, ysq)
        self.rsqrt(var, var, 1.0 / 64, 64e-5)
        self.tt(y3, y3, self.bc(var.unsqueeze(2), (128, 16, 64)), ALU.mult)
        for ch in range(8):
            self.tr(ps[2][:, ch // 4, (ch % 4) * 128:(ch % 4) * 128 + 128], ytf[:, ch * 128:(ch + 1) * 128])
        yT = v8(g[5])
        self.tt(yT, p8(ps[2]), self.bc(self.P('gnw').unsqueeze(2), (128, 8, 128)), ALU.mult)
        self.tt(yT, yT, self.bc(self.P('gnb').unsqueeze(2), (128, 8, 128)), ALU.add)
        self.tt(yT, yT, bonz, ALU.add)
        self.tt(self.mixedT[:, 8:16, :], yT, gate, ALU.mult)

    def load_H(self, s):
        self.dma(self.Hb[:], self.sr[s])

    def store_H(self, dst):
        self.dma(dst, self.Hb[:])

    def ffn(self, kind, nseq, L, last, hist_only=False):
        ps, g, c = self.ps, self.g, self.cols
        self.norm_T('ln2g')
        Lh = L + 2
        actT = self.xp[:, 0:2816].bitcast(BF16).rearrange("p (j t) -> p j t", j=44)
        fw = self.P('fconvw').rearrange("p (c i) -> p c i", i=3)
        qflat = self.qkv[:].rearrange("p c t -> p (c t)")
        for gi in range(6):
            j0 = gi * 8
            ng = min(8, 44 - j0)
            self.stream_up(self.c_up, [(j0 * 128, ng * 128, 0), (DFF + j0 * 128, ng * 128, 1024)], 0)
            for which in range(2):
                ch0 = which * 44 + j0
                c0 = ch0 * 128
                h = self.pp[:, which * 1920:which * 1920 + ng * nseq * Lh].rearrange("p (c s t) -> p c s t", c=ng, s=nseq)
                pt = ps[which][:].rearrange("p a (b t) -> p (a b) t", b=4)[:, 0:ng, :]
                if kind == 'p':
                    self.cp(h[:, :, 0, 0:2], self.fhist[:, ch0:ch0 + ng, :])
                else:
                    st = g[which][0:32, 0:ng * 128]
                    self.dma(st, self.sfc[:, c0:c0 + ng * 128])
                    for j in range(ng):
                        self.tr(ps[2][:, which, j * 32:(j + 1) * 32], st[:, j * 128:(j + 1) * 128])
                    self.cp(h[:, :, :, 0:2], ps[2][:, which, 0:ng * 32].rearrange("p (c s t) -> p c s t", c=ng, s=16))
                self.cp(h[:, :, :, 2:2 + L], pt.rearrange("p c (s t) -> p c s t", s=nseq), eng='act')
                if kind == 'p':
                    self.cp(self.fhist[:, ch0:ch0 + ng, :], h[:, :, 0, L:L + 2])
                    if last:
                        so = g[2 + which][0:2, 0:ng * 128]
                        for j in range(ng):
                            self.tr(ps[3][0:2, j // 4, (j % 4) * 128:(j % 4) * 128 + 128], h[:, j, 0, L:L + 2])
                        self.cp(so, ps[3][0:2, :, :].rearrange("p a b -> p (a b)")[:, 0:ng * 128])
                        self.dma(self.o_ffn_p[:, c0:c0 + ng * 128], so)
                else:
                    hs = self.scrA[:, 2048 + which * 256:2048 + which * 256 + ng * 32]
                    self.cp(hs.rearrange("p (c s r) -> p c s r", c=ng, s=16), h[:, :, :, L:L + 2])
                    hs3 = hs.rearrange("p (c r) -> p c r", c=ng)
                    so = g[2 + which][0:32, 0:ng * 128]
                    for j in range(ng):
                        self.tr(ps[3][0:32, j // 4, (j % 4) * 128:(j % 4) * 128 + 128], hs3[:, j, :])
                    self.cp(so, ps[3][0:32, :, :].rearrange("p a b -> p (a b)")[:, 0:ng * 128])
                    self.dma(self.o_ffn_s[:, c0:c0 + ng * 128], so)
                if hist_only:
                    continue
                wsh = (128, ng, nseq, L)
                o = self.scrA[:, which * 1024:which * 1024 + ng * 128].rearrange("p (c s t) -> p c s t", c=ng, s=nseq)
                tmp = qflat[:, which * 1024:which * 1024 + ng * 128].rearrange("p (c s t) -> p c s t", c=ng, s=nseq)
                self.tt(o, h[:, :, :, 2:2 + L], self.bc(fw[:, ch0:ch0 + ng, 2:3].unsqueeze(3), wsh), ALU.mult)
                for i in range(2):
                    self.tt(tmp, h[:, :, :, i:i + L], self.bc(fw[:, ch0:ch0 + ng, i:i + 1].unsqueeze(3), wsh), ALU.mult,
                            eng='pool')
                    self.tt(o, o, tmp, ALU.add)
            if hist_only:
                continue
            gt = self.scrA[:, 0:ng * 128]
            up = self.scrA[:, 1024:1024 + ng * 128]
            self.act(gt, gt, AF.Silu)
            self.tt(actT[:, j0:j0 + ng, :], gt.rearrange("p (c t) -> p c t", c=ng),
                    up.rearrange("p (c t) -> p c t", c=ng), ALU.mult)
        if hist_only:
            return
        self.stream_down(self.c_down, 44, actT, 1)
        for q in range(2):
            self.tt(self.xt[:, q * 1024:(q + 1) * 1024], self.xt[:, q * 1024:(q + 1) * 1024],
                    ps[2 + q][:].rearrange("p a b -> p (a b)"), ALU.add)

    def ffn_pair(self, last):
        ps, g = self.ps, self.g
        zz = self.zeros
        fw = self.P('fconvw').rearrange("p (c i) -> p c i", i=3)
        actA = self.xp[:, 0:2816].bitcast(BF16).rearrange("p (j t) -> p j t", j=22)
        actB = self.pp[:, 0:2816].bitcast(BF16).rearrange("p (j t) -> p j t", j=22)
        act_ap = lambda j: (actA if j < 22 else actB)[:, j % 22, :]
        hbuf = [self.qkv[:].rearrange("p c t -> p (c t)"), self.scrA]
        grp = 0
        for gi in range(6):
            j0 = gi * 8
            ng = min(8, 44 - j0)
            nq = (ng + 3) // 4
            for which in range(2):
                ch0 = which * 44 + j0
                pair = grp % 2
                grp += 1
                for bk in range(ng // 2):
                    self.mm(ps[pair * 2 + bk // 2][:, bk % 2, :], zz[:, 0:128], zz[:, 0:512], start=True, stop=False,
                            skip=True)
                for k in range(16):
                    t = self.wslot_load([(0, self.c_up[k * 128:(k + 1) * 128, ch0 * 128:(ch0 + ng) * 128])])
                    for j in range(ng):
                        self.mm(ps[pair * 2 + j // 4][:, (j % 4) // 2, (j % 2) * 256:(j % 2) * 256 + 256],
                                t[:, j * 128:(j + 1) * 128], self.xnT2[:, k, :], start=False, stop=(k == 15), skip=True)
                h = hbuf[which][:, 0:ng * 258].rearrange("p (c t) -> p c t", c=ng)
                self.cp(h[:, :, 0:2], self.fhist[:, ch0:ch0 + ng, :])
                for q in range(nq):
                    n4 = min(4, ng - 4 * q)
                    self.cp(h[:, 4 * q:4 * q + n4, 2:258],
                            ps[pair * 2 + q][:].rearrange("p a (b t) -> p (a b) t", b=2)[:, 0:n4, :],
                            eng='act' if q == 0 else 'dve')
                self.cp(self.fhist[:, ch0:ch0 + ng, :], h[:, :, 256:258])
                if last:
                    po_ = ps[(pair ^ 1) * 2]
                    so = g[12][0:2, 0:ng * 128]
                    for j in range(ng):
                        self.tr(po_[0:2, j // 4, (j % 4) * 128:(j % 4) * 128 + 128], h[:, j, 256:258])
                    self.cp(so, po_[0:2, :, :].rearrange("p a b -> p (a b)")[:, 0:ng * 128])
                    self.dma(self.o_ffn_p[:, ch0 * 128:(ch0 + ng) * 128], so)
                for q in range(nq):
                    n4 = min(4, ng - 4 * q)
                    wsh = (128, n4, 256)
                    o = g[which * 2 + q][:, 0:n4 * 256].rearrange("p (c t) -> p c t", c=n4)
                    tmp = g[4 + q][:, 0:n4 * 256].rearrange("p (c t) -> p c t", c=n4)
                    cs_ = slice(ch0 + 4 * q, ch0 + 4 * q + n4)
                    hq = h[:, 4 * q:4 * q + n4, :]
                    self.tt(o, hq[:, :, 2:258], self.bc(fw[:, cs_, 2:3], wsh), ALU.mult)
                    for i in range(2):
                        self.tt(tmp, hq[:, :, i:i + 256], self.bc(fw[:, cs_, i:i + 1], wsh), ALU.mult, eng='pool')
                        self.tt(o, o, tmp, ALU.add)
            for q in range(nq):
                n4 = min(4, ng - 4 * q)
                gt = g[q][:, 0:n4 * 256]
                up = g[2 + q][:, 0:n4 * 256]
                self.act(gt, gt, AF.Silu)
                for i in range(n4):
                    j = j0 + 4 * q + i
                    self.tt(act_ap(j), gt[:, i * 256:(i + 1) * 256], up[:, i * 256:(i + 1) * 256], ALU.mult,
                            eng='dve' if i % 2 == 0 else 'pool')
        for k in range(44):
            t = self.wslot_load([(0, self.c_down[k * 128:(k + 1) * 128, :])])
            for i in range(2):
                for n in range(4):
                    self.mm(ps[i * 2 + n // 2][:, n % 2, :], act_ap(k)[:, i * 128:(i + 1) * 128],
                            t[:, n * 512:(n + 1) * 512], start=(k == 0), stop=(k == 43))
        for i in range(2):
            xt = self.xts[i]
            for q in range(2):
                self.tt(xt[:, q * 1024:(q + 1) * 1024], xt[:, q * 1024:(q + 1) * 1024],
                        ps[i * 2 + q][:].rearrange("p a b -> p (a b)"), ALU.add)

    def build(self, do_sample=True):
        self.dma(self.params[:], self.params_d)
        self.dma(self.consts[:], self.consts_d)
        self.dma(self.fg[:], self.fg_d)
        self.dma(self.wab[:], self.wab_d)
        self.dma(self.gb[:], self.gb_d)
        self.memset(self.zeros[:], 0.0)
        self.convert_weights()
        ntile = self.NPRE + self.NMAIN
        paired = (self.NMAIN % 2 == 0) and PAIR_FFN
        for t in range(ntile):
            full = 2 if t >= self.NPRE else (1 if t == self.NPRE - 1 else 0)
            self.xt = self.xts[0]
            self.pool_eng = 'pool' if t >= 2 else 'dve'
            if full == 2:
                i = t - self.NPRE
                src, dst = self.xmain[i * 128:(i + 1) * 128, :], self.y_main[i * 128:(i + 1) * 128, :]
                if paired:
                    self.xt = self.xts[i % 2]
                    self.tile('p', src, dst, full, t == 0, t == ntile - 1, None, pair_slot=i % 2)
                    if i % 2 == 1:
                        self.ffn_pair(t == ntile - 1)
                        for m in range(2):
                            self.xt = self.xts[m]
                            self.final(self.y_main[(i - 1 + m) * 128:(i + m) * 128, :])
                    continue
            else:
                src, dst = self.xpre[t * 128:(t + 1) * 128, :], None
            self.tile('p', src, dst, full, t == 0, t == ntile - 1, None)
        self.xt = self.xts[0]
        if do_sample:
            self.tile('s', self.xs, self.y_s, 2, False, False, True)
        self.em.finish()
        self.em.replay()
        return self.nc


def rwkv_to_blockdiag(st):
    n = st.shape[0]
    out = np.zeros((n, 128, 8, 128), np.float32)
    t = np.asarray(st, np.float32).reshape(n, 8, 2, 64, 64)
    for two in range(2):
        out[:, two * 64:(two + 1) * 64, :, two * 64:(two + 1) * 64] = t[:, :, two].transpose(0, 3, 1, 2)
    return out


def rwkv_from_blockdiag(o):
    n = o.shape[0]
    res = np.zeros((n, 8, 2, 64, 64), np.float32)
    for two in range(2):
        res[:, :, two] = o[:, two * 64:(two + 1) * 64, :, two * 64:(two + 1) * 64].transpose(0, 2, 3, 1)
    return res.reshape(n, 16, 64, 64)


_CACHE = {}


def get_program(NPRE, NMAIN):
    key = (NPRE, NMAIN)
    if key not in _CACHE:
        _CACHE[key] = Builder(NPRE, NMAIN).build()
    return _CACHE[key]


def kernel(**inp):
    inp = {k: np.asarray(v) for k, v in inp.items()}
    NPRE, NMAIN = 8, 8
    nc = get_program(NPRE, NMAIN)
    consts = make_consts()
    params = make_params(inp)
    fg = np.ascontiguousarray(np.broadcast_to(inp['final_g'][None, :], (128, D))).astype(np.float32)
    wab = np.ascontiguousarray(np.concatenate([inp['rwkv_w_b'][0], inp['rwkv_a_b'][0]], axis=0)).astype(np.float32)
    gb = np.ascontiguousarray(inp['rwkv_g_b'][0]).astype(np.float32)
    shared = dict(w_in=np.ascontiguousarray(inp['w_in'][0]), w_o=np.ascontiguousarray(inp['w_o'][0]),
                  w_up=np.ascontiguousarray(inp['ffn_w_up'][0]), w_down=np.ascontiguousarray(inp['ffn_w_down'][0]),
                  params=params, consts=consts, fg=fg, wab=wab, gb=gb)
    xp_, xs_ = inp['x_prompt'], inp['x_sample']
    in_maps = []
    for cid in range(NCORES):
        b, half = cid // 2, cid % 2
        m = dict(shared)
        pc_ = params.copy()
        pc_[:, POFF['hflag'][0]] = float(half)
        m['params'] = pc_
        if half == 0:
            m['xpre'] = np.zeros((NPRE * 128, D), np.float32)
            m['xmain'] = np.ascontiguousarray(xp_[b, 0:1024])
        else:
            m['xpre'] = np.ascontiguousarray(xp_[b, 0:1024])
            m['xmain'] = np.ascontiguousarray(xp_[b, 1024:2048])
        sl = slice(cid * 16, cid * 16 + 16)
        m['xs'] = np.ascontiguousarray(xs_[sl].reshape(128, D))
        m['sg'] = np.ascontiguousarray(inp['state_gdn'][0, sl])
        m['sgc'] = np.ascontiguousarray(inp['state_gdn_conv'][0, sl].reshape(48, 3072))
        m['sr'] = rwkv_to_blockdiag(inp['state_rwkv'][0, sl])
        m['ssh'] = np.ascontiguousarray(inp['state_rwkv_shift'][0, sl])
        m['sfc'] = np.ascontiguousarray(inp['state_ffn_conv'][0, sl].reshape(32, 11264))
        in_maps.append(m)
    res = run_bass_kernel_spmd(nc, in_maps, core_ids=list(range(NCORES)))
    R = res.results
    y_prompt = np.zeros((4, 2048, D), np.float32)
    for cid in range(NCORES):
        b, half = cid // 2, cid % 2
        y_prompt[b, half * 1024:(half + 1) * 1024] = R[cid]['y_main']
    y_sample = np.concatenate([R[c]['y_s'].reshape(16, 8, D) for c in range(NCORES)], 0)
    odd = [1, 3, 5, 7]
    gdn_p = np.stack([R[c]['gdn_p'] for c in odd])[None]
    gconv_p = np.stack([R[c]['gconv_p'] for c in odd])[None]
    rwkv_p = np.stack([rwkv_from_blockdiag(R[c]['rwkv_p'][None])[0] for c in odd])[None]
    shift_p = np.stack([R[c]['shift_p'].reshape(3328) for c in odd])[None]
    ffn_p = np.stack([R[c]['ffn_p'] for c in odd])[None]
    gdn_s = np.concatenate([R[c]['gdn_s'] for c in range(NCORES)], 0)[None]
    gconv_s = np.concatenate([R[c]['gconv_s'].reshape(16, 3, 3072) for c in range(NCORES)], 0)[None]
    rwkv_s = np.concatenate([rwkv_from_blockdiag(R[c]['rwkv_s']) for c in range(NCORES)], 0)[None]
    shift_s = np.concatenate([R[c]['shift_s'] for c in range(NCORES)], 0)[None]
    ffn_s = np.concatenate([R[c]['ffn_s'].reshape(16, 2, 11264) for c in range(NCORES)], 0)[None]
    return (y_prompt, y_sample, gdn_p, gconv_p, rwkv_p, shift_p, ffn_p, gdn_s, gconv_s, rwkv_s, shift_s, ffn_s)
```
